# Optimizing a Trainium2 kernel written in Bass

```python
import jax
import jax.numpy as jnp
from jax import lax
import numpy as np

D_MODEL = 1024
BATCH = 2
SEQ = 8192
DEPTH = 2
DEC_BATCH = 32
DEC_SEQ = 16
PAST_LEN = 4096

CHUNK = 64
N_EVEN = (DEPTH + 1) // 2
N_ODD = DEPTH // 2
HEAD_DIM = 64
N_Q_HEADS = 8
N_KV_HEADS = 2
Q_GROUP = N_Q_HEADS // N_KV_HEADS
WINDOW = 128
WIN_CHUNKS = -(-WINDOW // CHUNK)
SWA_ROWS = WIN_CHUNKS * CHUNK
BAND = (WIN_CHUNKS + 1) * CHUNK
ROPE_THETA = 10000.0
NEG = -1e30
D_LRU = 512
LRU_BLOCKS = 8
LRU_BW = D_LRU // LRU_BLOCKS
CONV_W = 4
LRU_C = 8.0
Q_W = N_Q_HEADS * HEAD_DIM
KV_W = N_KV_HEADS * HEAD_DIM
EVEN_IN = Q_W + 2 * KV_W + 2 * D_LRU
EVEN_MIX = Q_W + D_LRU
CHUNK_MLP = 128
D_C = D_MODEL
C_GROUPS = 8
C_GW = D_C // C_GROUPS
D_FF = 4 * D_MODEL
EPS = 1e-6

kernel_name = 'hybrid_swa_rglru_gmlp_stream_step'


def rms_norm(x, g):
    xf = x.astype(jnp.float32)
    y = xf * lax.rsqrt(jnp.mean(xf * xf, axis=-1, keepdims=True) + EPS)
    return (y * g.astype(jnp.float32)).astype(x.dtype)


def layer_norm(x, g):
    xf = x.astype(jnp.float32)
    xc = xf - jnp.mean(xf, axis=-1, keepdims=True)
    y = xc * lax.rsqrt(jnp.mean(xc * xc, axis=-1, keepdims=True) + EPS)
    return (y * g.astype(jnp.float32)).astype(x.dtype)


def rope(x, pos):
    half = HEAD_DIM // 2
    inv = ROPE_THETA ** (-jnp.arange(half, dtype=jnp.float32) / half)
    ang = pos.astype(jnp.float32)[:, None] * inv[None, :]
    cos = jnp.cos(ang)[None, :, None, :]
    sin = jnp.sin(ang)[None, :, None, :]
    xf = x.astype(jnp.float32)
    x1, x2 = xf[..., :half], xf[..., half:]
    return jnp.concatenate([x1 * cos - x2 * sin, x2 * cos + x1 * sin], axis=-1).astype(x.dtype)


def softmax_with_sink(s, sink):
    sk = sink.astype(jnp.float32)[:, :, None, None]
    m = jnp.maximum(jnp.max(s, axis=-1, keepdims=True), sk)
    e = jnp.exp(s - m)
    return e / (jnp.sum(e, axis=-1, keepdims=True) + jnp.exp(sk - m))


def swa_banded(q, k, v, sinks):
    bsz, s_len = q.shape[:2]
    nc = s_len // CHUNK
    pad = WIN_CHUNKS * CHUNK
    kp = jnp.pad(k, ((0, 0), (pad, 0), (0, 0), (0, 0)))
    vp = jnp.pad(v, ((0, 0), (pad, 0), (0, 0), (0, 0)))

    def band(t):
        return jnp.concatenate(
            [t[:, j * CHUNK:j * CHUNK + s_len].reshape(bsz, nc, CHUNK, N_KV_HEADS, HEAD_DIM)
             for j in range(WIN_CHUNKS + 1)], axis=2)

    kb, vb = band(kp), band(vp)
    qb = q.reshape(bsz, nc, CHUNK, N_KV_HEADS, Q_GROUP, HEAD_DIM)
    s = jnp.einsum('bcqkgd,bcskd->bckgqs', qb, kb, preferred_element_type=jnp.float32) * (HEAD_DIM ** -0.5)
    key_chunk = jnp.arange(nc)[:, None] - WIN_CHUNKS + jnp.arange(BAND)[None, :] // CHUNK
    s = jnp.where((key_chunk >= 0)[None, :, None, None, None, :], s, NEG)
    p = softmax_with_sink(s, sinks.reshape(N_KV_HEADS, Q_GROUP))
    o = jnp.einsum('bckgqs,bcskd->bcqkgd', p.astype(v.dtype), vb)
    return o.reshape(bsz, s_len, Q_W)


def swa_step(q, kk, vv, sinks):
    bsz, t = q.shape[:2]
    qg = q.reshape(bsz, t, N_KV_HEADS, Q_GROUP, HEAD_DIM)
    s = jnp.einsum('btkgd,bskd->bkgts', qg, kk, preferred_element_type=jnp.float32) * (HEAD_DIM ** -0.5)
    p = softmax_with_sink(s, sinks.reshape(N_KV_HEADS, Q_GROUP))
    o = jnp.einsum('bkgts,bskd->btkgd', p.astype(vv.dtype), vv)
    return o.reshape(bsz, t, Q_W)


def causal_conv(xr, buf, w, b):
    t = xr.shape[1]
    xp = jnp.concatenate([buf.astype(xr.dtype), xr], axis=1)
    y = xp[:, 0:t] * w[0] + b
    for i in range(1, CONV_W):
        y = y + xp[:, i:i + t] * w[i]
    return y, xp[:, -(CONV_W - 1):]


def rg_lru(xc, h0, wa, ba, wx, bx, lam):
    bsz, t, _ = xc.shape
    xb = xc.reshape(bsz, t, LRU_BLOCKS, LRU_BW)
    r = jax.nn.sigmoid(jnp.einsum('btnc,ncd->btnd', xb, wa).reshape(bsz, t, D_LRU) + ba)
    gi = jax.nn.sigmoid(jnp.einsum('btnc,ncd->btnd', xb, wx).reshape(bsz, t, D_LRU) + bx)
    log_a = -LRU_C * r.astype(jnp.float32) * jax.nn.softplus(-lam.astype(jnp.float32))
    a = jnp.exp(log_a)
    bterm = jnp.sqrt(-jnp.expm1(2.0 * log_a)) * (gi * xc).astype(jnp.float32)
    bterm = bterm.at[:, 0].add(a[:, 0] * h0.astype(jnp.float32))

    def combine(lhs, rhs):
        a1, b1 = lhs
        a2, b2 = rhs
        return a1 * a2, a2 * b1 + b2

    _, h = lax.associative_scan(combine, (a, bterm), axis=1)
    return h.astype(xc.dtype), h[:, -1].astype(h0.dtype)


def even_mixer(x, pos, k_cache, v_cache, h0, conv_buf, w_in, q_g, k_g, sinks,
               conv_w, conv_b, wa, ba, wx, bx, lam, w_out):
    bsz, t, _ = x.shape
    h = x @ w_in
    o1 = Q_W
    o2 = o1 + KV_W
    o3 = o2 + KV_W
    o4 = o3 + D_LRU
    q = h[..., :o1].reshape(bsz, t, N_Q_HEADS, HEAD_DIM)
    k = h[..., o1:o2].reshape(bsz, t, N_KV_HEADS, HEAD_DIM)
    v = h[..., o2:o3].reshape(bsz, t, N_KV_HEADS, HEAD_DIM)
    xr = h[..., o3:o4]
    gr = h[..., o4:]
    q = rope(rms_norm(q, q_g), pos)
    k = rope(rms_norm(k, k_g), pos)
    if k_cache is None:
        att = swa_banded(q, k, v, sinks)
        new_k, new_v = k[:, -SWA_ROWS:], v[:, -SWA_ROWS:]
        h0 = jnp.zeros((bsz, D_LRU), x.dtype)
        conv_buf = jnp.zeros((bsz, CONV_W - 1, D_LRU), x.dtype)
    else:
        rows = k_cache.shape[1]
        kk = jnp.concatenate([k_cache.astype(k.dtype), k], axis=1)
        vv = jnp.concatenate([v_cache.astype(v.dtype), v], axis=1)
        att = swa_step(q, kk, vv, sinks)
        new_k, new_v = kk[:, -rows:], vv[:, -rows:]
    xc, new_buf = causal_conv(xr, conv_buf, conv_w, conv_b)
    hs, new_h = rg_lru(xc, h0, wa, ba, wx, bx, lam)
    rec = hs * jax.nn.gelu(gr)
    out = jnp.concatenate([att, rec], axis=-1) @ w_out
    return out, new_k, new_v, new_h, new_buf


def gmlp_mixer(x, w_in, v_g, ws, bs, w_out):
    bsz, t, _ = x.shape
    z = jax.nn.gelu(x @ w_in)
    u = z[..., :D_C]
    v = layer_norm(z[..., D_C:], v_g)
    rows = min(t, CHUNK_MLP)
    nc = t // rows
    mask = jnp.tril(jnp.ones((rows, rows), dtype=bool))
    w = jnp.where(mask[None], ws[:, :rows, :rows], 0.0)
    vb = v.reshape(bsz, nc, rows, C_GROUPS, C_GW)
    sv = jnp.einsum('gts,bcsgw->bctgw', w.astype(v.dtype), vb) + bs[:, :rows].T[None, None, :, :, None]
    out = (u * sv.reshape(bsz, t, D_C)) @ w_out
    return out, v


def channel_mlp(x, w1, w2):
    return jnp.square(jax.nn.relu(x @ w1)) @ w2


def setup_inputs(seed: int = 0) -> dict:
    key = jax.random.key(seed)
    ks = iter(jax.random.split(key, 40))

    def nrm(shape, scale):
        return scale * jax.random.normal(next(ks), shape, jnp.float32)

    e, o = N_EVEN, N_ODD
    u = jax.random.uniform(next(ks), (e, D_LRU), jnp.float32, 0.9, 0.999)
    sg = u ** (1.0 / LRU_C)
    lam = jnp.log(sg) - jnp.log1p(-sg)
    return {
        'x_prompt': nrm((BATCH, SEQ, D_MODEL), 1.0),
        'x_sample': nrm((DEC_BATCH, DEC_SEQ, D_MODEL), 1.0),
        'cache_swa_k': nrm((e, DEC_BATCH, SWA_ROWS, N_KV_HEADS, HEAD_DIM), 1.0),
        'cache_swa_v': nrm((e, DEC_BATCH, SWA_ROWS, N_KV_HEADS, HEAD_DIM), 1.0),
        'state_lru_h': nrm((e, DEC_BATCH, D_LRU), 0.5),
        'state_lru_conv': nrm((e, DEC_BATCH, CONV_W - 1, D_LRU), 1.0),
        'e_norm_g': 1.0 + nrm((e, D_MODEL), 0.02),
        'e_w_in': nrm((e, D_MODEL, EVEN_IN), D_MODEL ** -0.5),
        'e_q_norm_g': 1.0 + nrm((e, HEAD_DIM), 0.02),
        'e_k_norm_g': 1.0 + nrm((e, HEAD_DIM), 0.02),
        'e_sinks': nrm((e, N_Q_HEADS), 0.5),
        'e_conv_w': nrm((e, CONV_W, D_LRU), CONV_W ** -0.5),
        'e_conv_b': nrm((e, D_LRU), 0.02),
        'e_gate_a_w': nrm((e, LRU_BLOCKS, LRU_BW, LRU_BW), LRU_BW ** -0.5),
        'e_gate_a_b': nrm((e, D_LRU), 0.02),
        'e_gate_x_w': nrm((e, LRU_BLOCKS, LRU_BW, LRU_BW), LRU_BW ** -0.5),
        'e_gate_x_b': nrm((e, D_LRU), 0.02),
        'e_lru_lambda': lam,
        'e_w_out': nrm((e, EVEN_MIX, D_MODEL), EVEN_MIX ** -0.5),
        'o_norm_g': 1.0 + nrm((o, D_MODEL), 0.02),
        'o_w_in': nrm((o, D_MODEL, 2 * D_C), D_MODEL ** -0.5),
        'o_v_norm_g': 1.0 + nrm((o, D_C), 0.02),
        'o_spatial_w': nrm((o, C_GROUPS, CHUNK_MLP, CHUNK_MLP), CHUNK_MLP ** -0.5),
        'o_spatial_b': 1.0 + nrm((o, C_GROUPS, CHUNK_MLP), 0.02),
        'o_w_out': nrm((o, D_C, D_MODEL), D_C ** -0.5),
        'ffn_norm_g': 1.0 + nrm((DEPTH, D_MODEL), 0.02),
        'ffn_w1': nrm((DEPTH, D_MODEL, D_FF), D_MODEL ** -0.5),
        'ffn_w2': nrm((DEPTH, D_FF, D_MODEL), D_FF ** -0.5),
    }


def reference(x_prompt, x_sample, cache_swa_k, cache_swa_v, state_lru_h, state_lru_conv,
              e_norm_g, e_w_in, e_q_norm_g, e_k_norm_g, e_sinks, e_conv_w, e_conv_b,
              e_gate_a_w, e_gate_a_b, e_gate_x_w, e_gate_x_b, e_lru_lambda, e_w_out,
              o_norm_g, o_w_in, o_v_norm_g, o_spatial_w, o_spatial_b, o_w_out,
              ffn_norm_g, ffn_w1, ffn_w2):
    pos_p = jnp.arange(x_prompt.shape[1])
    pos_s = PAST_LEN + jnp.arange(x_sample.shape[1])
    yp, ys = x_prompt, x_sample
    kp_l, vp_l, hp_l, cp_l = [], [], [], []
    ks_l, vs_l, hs_l, cs_l = [], [], [], []
    gv_l = []
    for layer in range(DEPTH):
        if layer % 2 == 0:
            e = layer // 2
            ew = (e_w_in[e], e_q_norm_g[e], e_k_norm_g[e], e_sinks[e], e_conv_w[e], e_conv_b[e],
                  e_gate_a_w[e], e_gate_a_b[e], e_gate_x_w[e], e_gate_x_b[e], e_lru_lambda[e], e_w_out[e])
            mp, kpn, vpn, hpn, cpn = even_mixer(rms_norm(yp, e_norm_g[e]), pos_p, None, None, None, None, *ew)
            ms, ksn, vsn, hsn, csn = even_mixer(rms_norm(ys, e_norm_g[e]), pos_s, cache_swa_k[e], cache_swa_v[e],
                                                state_lru_h[e], state_lru_conv[e], *ew)
            kp_l.append(kpn)
            vp_l.append(vpn)
            hp_l.append(hpn)
            cp_l.append(cpn)
            ks_l.append(ksn)
            vs_l.append(vsn)
            hs_l.append(hsn)
            cs_l.append(csn)
        else:
            o = layer // 2
            ow = (o_w_in[o], o_v_norm_g[o], o_spatial_w[o], o_spatial_b[o], o_w_out[o])
            mp, _ = gmlp_mixer(rms_norm(yp, o_norm_g[o]), *ow)
            ms, vsn = gmlp_mixer(rms_norm(ys, o_norm_g[o]), *ow)
            gv_l.append(vsn)
        yp = yp + mp
        ys = ys + ms
        yp = yp + channel_mlp(rms_norm(yp, ffn_norm_g[layer]), ffn_w1[layer], ffn_w2[layer])
        ys = ys + channel_mlp(rms_norm(ys, ffn_norm_g[layer]), ffn_w1[layer], ffn_w2[layer])
    return (yp, ys,
            jnp.stack(kp_l), jnp.stack(vp_l), jnp.stack(hp_l), jnp.stack(cp_l),
            jnp.stack(ks_l), jnp.stack(vs_l), jnp.stack(hs_l), jnp.stack(cs_l),
            jnp.stack(gv_l))
```

```python
import numpy as np
from contextlib import ExitStack
import concourse.bass as bass
import concourse.mybir as mybir
from concourse.bass_utils import run_bass_kernel_spmd

F32 = mybir.dt.float32
BF16 = mybir.dt.bfloat16
AF = mybir.ActivationFunctionType
ALU = mybir.AluOpType

ENGS = ("pe", "act", "dve", "pool", "sp")
SCHED = True
MERGE_SAMPLE_FFN = True
SAMPLE_INORDER = True
TABLE_WAIT = 1.0
PRIO = True
INORDER = set()
NCORES = 8
NT = 2048
NS = 64
NTOK = NT + NS
NPRE = 6144
TS = 512
EPS = 1e-6


_ACT_SET = {"Exp": "e", "Ln": "e", "Sigmoid": "s", "Sqrt": "q", "Gelu_apprx_tanh": "g"}


def _free_elems(ap):
    try:
        sh = ap.shape
        n = 1
        for d in sh[1:]:
            n *= int(d)
        return n
    except Exception:
        return 512


def I(name, *args, **kw):
    import sys
    line = sys._getframe(1).f_lineno

    def f(e):
        r = getattr(e, name)(*args, **kw)
        if _DBG is not None:
            try:
                _DBG.append((r.ins.name, line, name))
            except Exception:
                pass
        return r
    out = kw.get("out", args[0] if args else None)
    n = _free_elems(out) if out is not None else 512
    f.aset = None
    f.lat = 0.0
    if name == "matmul":
        nn = _free_elems(kw.get("rhs", out))
        f.cost = max(nn, 64) / 2000.0 + 0.02
        f.lat = 0.15
    elif name == "transpose":
        f.cost = 0.12
        f.lat = 0.15
    elif name == "activation":
        f.cost = 0.22 + n / 1200.0
        fn = kw.get("func")
        f.aset = _ACT_SET.get(getattr(fn, "name", str(fn)).split(".")[-1])
    elif name == "dma_start":
        f.cost = 0.5
        f.lat = 2.5 + n * 128 * 4 / 150e3
    elif name == "tensor_tensor_scan":
        f.cost = 0.1 + 2.3 * n / 960.0
    elif name == "scalar_tensor_tensor":
        f.cost = 0.1 + 1.9 * n / 960.0
    elif name == "tensor_copy" and out is not None and args[1:2] == () and kw.get("in_") is not None and getattr(kw["in_"], "dtype", None) != getattr(out, "dtype", None):
        f.cost = 0.1 + 1.7 * n / 960.0
    elif name == "reciprocal":
        f.cost = 0.1 + 6.5 * n / 960.0
    else:
        f.cost = 0.1 + n / 960.0
    return f


_DBG = None


class Buf:
    __slots__ = ("name", "w", "r")

    def __init__(self, name=""):
        self.name = name
        self.w = []
        self.r = []


class Op:
    __slots__ = ("eng", "fn", "deps", "dma_sem", "dma_val", "needed", "sig", "is_dma", "phase", "idx", "succ", "nun", "fin", "chain", "why", "start", "tag")

    def __init__(self, eng, fn):
        self.eng = eng
        self.fn = fn
        self.deps = ()
        self.dma_sem = None
        self.dma_val = 0
        self.needed = False
        self.sig = 0
        self.is_dma = False


class Prog:
    def __init__(self, nc):
        self.nc = nc
        self.ops = {e: [] for e in ENGS}
        self.dma_counts = {}
        self.bar = {e: [] for e in ENGS}
        self.arena_dmas = []
        self.phase = 0
        self.nops = 0

    def op(self, eng, fn, reads=(), writes=(), dma_sem=None, arena=False, after=()):
        o = Op(eng, fn)
        o.chain = None
        o.tag = getattr(self, "cur_tag", None)
        o.phase = self.phase
        o.idx = self.nops
        self.nops += 1
        deps = []
        for b in reads:
            deps.extend(b.w)
        for b in writes:
            deps.extend(b.w)
            deps.extend(b.r)
        for b in after:
            deps.extend(b.w)
            deps.extend(b.r)
        if self.bar[eng]:
            deps.extend(self.bar[eng])
            self.bar[eng] = []
        for b in reads:
            b.r.append(o)
        for b in writes:
            b.w = [o]
            b.r = []
        o.deps = deps
        if dma_sem is not None:
            o.is_dma = True
            o.dma_sem = dma_sem
            c = self.dma_counts.get(dma_sem, 0) + 16
            self.dma_counts[dma_sem] = c
            o.dma_val = c
            if arena:
                self.arena_dmas.append(o)
        self.ops[eng].append(o)
        return o

    def barrier(self):
        last = []
        for e in ENGS:
            for o in reversed(self.ops[e]):
                if not o.is_dma:
                    last.append(o)
                    break
        last.extend(self.arena_dmas)
        self.phase_arena = getattr(self, "phase_arena", {})
        self.phase_arena[self.phase] = list(self.arena_dmas)
        self.arena_dmas = []
        for e in ENGS:
            self.bar[e] = list(last)
        self.phase += 1

    def schedule(self):
        import heapq
        allops = [o for e in ENGS for o in self.ops[e]]
        for e in ENGS:
            prev = None
            prev_tag = None
            for o in self.ops[e]:
                if e in INORDER:
                    if prev is not None and not any(d is prev for d in o.deps):
                        o.deps = list(o.deps) + [prev]
                        o.chain = prev
                    prev = o
                elif e == "dve" and SAMPLE_INORDER and o.tag == "sample":
                    if prev_tag is not None and prev_tag.phase == o.phase and not any(d is prev_tag for d in o.deps):
                        o.deps = list(o.deps) + [prev_tag]
                        o.chain = prev_tag
                    prev_tag = o
        for o in allops:
            o.succ = []
            o.fin = None
        for o in allops:
            ds = set(d for d in o.deps if d is not o)
            o.deps = list(ds)
            o.nun = len(ds)
            for d in ds:
                d.succ.append(o)
        bl = {}
        for o in sorted(allops, key=lambda o: -o.idx):
            m = 0.0
            for s_ in o.succ:
                if s_.phase == o.phase:
                    v = bl[id(s_)]
                    if v > m:
                        m = v
            bl[id(o)] = m + o.fn.cost * (2.0 if (o.eng == "pool" and not o.is_dma) else 1.0) + o.fn.lat + 0.2
        order = {e: [] for e in ENGS}
        tfree = {e: 0.0 for e in ENGS}
        pool_mult = 2.0
        aset = [None]
        byphase = {}
        for o in allops:
            byphase.setdefault(o.phase, []).append(o)
        tbase = 0.0
        for ph in sorted(byphase):
            ops = byphase[ph]
            future = {e: [] for e in ENGS}
            avail = {e: [] for e in ENGS}
            inphase = set(id(o) for o in ops)
            left = len(ops)
            for e in ENGS:
                tfree[e] = max(tfree[e], tbase)

            def push(o):
                rt = tbase
                for d in o.deps:
                    t = d.fin + (0.25 if d.eng != o.eng else 0.05)
                    if t > rt:
                        rt = t
                heapq.heappush(future[o.eng], (rt, o.idx, o))
            for o in ops:
                o.nun = sum(1 for d in o.deps if d.fin is None)
                if o.nun == 0:
                    push(o)
            while left:
                best = None
                for e in ENGS:
                    fu, av = future[e], avail[e]
                    while fu and fu[0][0] <= tfree[e]:
                        rt, idx, o = heapq.heappop(fu)
                        heapq.heappush(av, ((-bl[id(o)], idx) if PRIO else idx, o))
                    if av:
                        cand = av[0][1]
                        st = tfree[e]
                        src = "av"
                        if e == "act" and cand.fn.aset is not None and cand.fn.aset != aset[0]:
                            found = False
                            for idx2, o2 in sorted(av, key=lambda t: t[0])[:16]:
                                if o2.fn.aset is None or o2.fn.aset == aset[0]:
                                    cand = o2
                                    found = True
                                    break
                            if not found and TABLE_WAIT > 0 and fu:
                                for rt2, idx2, o2 in sorted(fu)[:8]:
                                    if rt2 > tfree[e] + TABLE_WAIT:
                                        break
                                    if o2.fn.aset == aset[0]:
                                        cand, st, src = o2, rt2, "fu2"
                                        break
                    elif fu:
                        rt, idx, cand = fu[0]
                        st = rt
                        src = "fu"
                    else:
                        continue
                    key = (st, cand.idx)
                    if best is None or key < best[0]:
                        best = (key, e, cand, src)
                (st, _), e, o, src = best
                if src == "fu2":
                    fu_ = future[e]
                    for i_, t_ in enumerate(fu_):
                        if t_[2] is o:
                            fu_.pop(i_)
                            break
                    heapq.heapify(fu_)
                elif src == "av":
                    av = avail[e]
                    for i_, t_ in enumerate(av):
                        if t_[1] is o:
                            av.pop(i_)
                            break
                    heapq.heapify(av)
                else:
                    heapq.heappop(future[e])
                rdy, rdep = tbase, None
                for d in o.deps:
                    if d.fin is not None:
                        t_ = d.fin + (0.25 if d.eng != o.eng else 0.05)
                        if t_ > rdy:
                            rdy, rdep = t_, d
                o.why = ("dep", rdep) if (rdep is not None and rdy >= st - 1e-9) else ("eng", order[e][-1] if order[e] else None)
                o.start = st
                cost = o.fn.cost
                if e == "pool" and not o.is_dma:
                    cost *= pool_mult
                if e == "act" and o.fn.aset is not None and o.fn.aset != aset[0]:
                    cost += 1.3
                    aset[0] = o.fn.aset
                    self.nswitch = getattr(self, "nswitch", 0) + 1
                if o.is_dma:
                    tfree[e] = st + (1.0 if e == "pool" else 0.35)
                    o.fin = st + cost + o.fn.lat
                else:
                    tfree[e] = st + cost
                    o.fin = st + cost + o.fn.lat
                order[e].append(o)
                left -= 1
                for s_ in o.succ:
                    if id(s_) in inphase:
                        s_.nun -= 1
                        if s_.nun == 0:
                            push(s_)
            tbase = max([tbase] + [o.fin for o in ops])
            self.phase_end = getattr(self, 'phase_end', []) + [round(tbase, 1)]
        lastc = {e: None for e in ENGS}
        curph = {e: -1 for e in ENGS}
        phase_last = {}
        for ph in sorted(byphase):
            snap = {}
            for e in ENGS:
                lst = [o for o in order[e] if o.phase == ph and not o.is_dma]
                if lst:
                    lastc[e] = lst[-1]
                snap[e] = lastc[e]
            phase_last[ph] = snap
        pa = getattr(self, "phase_arena", {})
        for e in ENGS:
            prev = None
            for o in order[e]:
                if prev is None or o.phase != prev:
                    if o.phase > 0:
                        phs = [p for p in phase_last if p < o.phase]
                        if phs:
                            pp = max(phs)
                            extra = [x for x in phase_last[pp].values() if x is not None] + list(pa.get(pp, []))
                            o.deps = list(set(o.deps) | set(x for x in extra if x is not o))
                    prev = o.phase
        self.ops = order
        cnt = {}
        for e in ENGS:
            for o in self.ops[e]:
                if o.is_dma:
                    c = cnt.get(o.dma_sem, 0) + 16
                    cnt[o.dma_sem] = c
                    o.dma_val = c
        self.est_us = tbase

    def emit(self, final_waits):
        nc = self.nc
        if SCHED:
            self.schedule()
        for e in ENGS:
            for o in self.ops[e]:
                for d in o.deps:
                    if d is o or d.is_dma:
                        continue
                    if d.eng == "pe" and o.eng == "pe":
                        continue
                    if getattr(o, "chain", None) is d:
                        continue
                    d.needed = True
        for e in ENGS:
            c = 0
            for o in self.ops[e]:
                if o.is_dma:
                    continue
                if o.needed:
                    c += 1
                    o.sig = c
        with ExitStack() as es:
            esem = {e: es.enter_context(nc.semaphore("s_" + e)) for e in ENGS}
            dsem = {k: es.enter_context(nc.semaphore("d_%s" % (k,))) for k in self.dma_counts}
            block = es.enter_context(nc.Block())

            def run(ename, eng):
                seen = {}
                for o in self.ops[ename]:
                    need = {}
                    for d in o.deps:
                        if d is o or getattr(o, "chain", None) is d:
                            continue
                        if d.is_dma:
                            key = ("d", d.dma_sem)
                            val = d.dma_val
                            sem = dsem[d.dma_sem]
                        else:
                            if d.eng == "pe" and ename == "pe":
                                continue
                            key = ("e", d.eng)
                            val = d.sig
                            sem = esem[d.eng]
                        if seen.get(key, 0) >= val:
                            continue
                        if need.get(key, (0, None))[0] < val:
                            need[key] = (val, sem)
                    for key, (val, sem) in need.items():
                        eng.wait_ge(sem, val)
                        seen[key] = val
                    ins = o.fn(eng)
                    if o.is_dma:
                        ins.then_inc(dsem[o.dma_sem], 16)
                    elif o.needed:
                        ins.then_inc(esem[ename], 1)
                if ename == "sp":
                    for key in final_waits:
                        eng.wait_ge(dsem[key], self.dma_counts[key])

            @block.tensor
            def _(eng):
                run("pe", eng)

            @block.scalar
            def _(eng):
                run("act", eng)

            @block.vector
            def _(eng):
                run("dve", eng)

            @block.gpsimd
            def _(eng):
                run("pool", eng)

            @block.sync
            def _(eng):
                run("sp", eng)


PV_ENORM, PV_F0NORM, PV_ONORM, PV_F1NORM = 0, 8, 16, 24
PV_QG, PV_KG, PV_CW, PV_CB, PV_BA, PV_BX, PV_LAM, PV_SINK = 32, 33, 34, 50, 54, 58, 62, 66
NPV = 80
C_ID, C_ONES, C_BONES, C_PERM, C_TRIU, C_BDM, C_HB, C_PF = 0, 128, 256, 384, 512, 640, 704, 705
NCST = 705 + 12


def build_program(stage=6):
    nc = bass.Bass("TRN2", target_bir_lowering=False)

    def din(name, shape):
        return nc.dram_tensor(name, list(shape), F32, kind="ExternalInput").ap()

    def dout(name, shape):
        return nc.dram_tensor(name, list(shape), F32, kind="ExternalOutput").ap()

    xmain_d = din("xmain", [NT, 1024])
    xpre_d = din("xpre", [NPRE, 1024])
    xhalo_d = din("xhalo", [128, 1024])
    xsm_d = din("xsm", [NS, 1024])
    ck_d = din("ck", [4, 128, 128])
    cv_d = din("cv", [4, 128, 128])
    sth_d = din("sth", [4, 512])
    stc_d = din("stc", [12, 512])
    pvec_d = din("pvec", [128, NPV])
    cst_d = din("cst", [128, NCST])
    cstab_d = din("cstab", [128, 2, 128 + NT + NS])
    ewin_d = din("ewin", [1024, 1792])
    ewout_d = din("ewout", [1024, 1024])
    ga_d = din("ga", [8, 64, 64])
    gx_d = din("gx", [8, 64, 64])
    owin_d = din("owin", [1024, 2048])
    owout_d = din("owout", [1024, 1024])
    wst_d = din("wst", [8, 128, 128])
    osb_d = din("osb", [8, 128])
    ovg_d = din("ovg", [1024])
    w1_d = din("w1", [2, 1024, 4096])
    w2_d = din("w2", [2, 4096, 1024])

    y_d = dout("y", [NT, 1024])
    ys_d = dout("ys", [NS, 1024])
    kp_d = dout("kp", [128, 128])
    vp_d = dout("vp", [128, 128])
    hp_d = dout("hp", [4, 128])
    cp_d = dout("cp", [3, 512])
    ksm_d = dout("ksm", [4, 128, 128])
    vsm_d = dout("vsm", [4, 128, 128])
    hsm_d = dout("hsm", [4, 512])
    csm_d = dout("csm", [12, 512])
    gv_d = dout("gv", [NS, 1024])

    P = Prog(nc)
    es = ExitStack()
    with es:
        def sb(name, shape, dt):
            return es.enter_context(nc.sbuf_tensor("sb_" + name, list(shape), dt))

        xres = sb("xres", [128, 8, NTOK], F32)
        ring = [sb("ring%d" % i, [128, 8192], BF16) for i in range(3)]
        pvec = sb("pvec", [128, NPV], F32)
        cst = sb("cst", [128, NCST], F32)
        cbf = sb("cbf", [128, 4 * 128], BF16)
        small = sb("small", [128, 64], F32)
        bda = sb("bda", [128, 4, 128], BF16)
        bdx = sb("bdx", [128, 4, 128], BF16)
        ARENA_F = 22390
        arena = sb("arena", [128, ARENA_F], F32)
        ps = [es.enter_context(nc.psum_tensor("ps%d" % i, [128, 512], F32)) for i in range(8)]
        psB = [Buf("ps%d" % i) for i in range(8)]
        bank_ctr = [0]

        def nb():
            i = bank_ctr[0] % 8
            bank_ctr[0] += 1
            return i

        class Arena:
            def __init__(self):
                self.off = 0

            def reset(self):
                self.off = 0

            def get(self, shape, dt):
                n = int(np.prod(shape[1:]))
                words = n if dt == F32 else (n + 1) // 2
                words = (words + 7) // 8 * 8
                assert self.off + words <= ARENA_F, ("arena overflow", self.off, words)
                v = arena[0:shape[0], self.off:self.off + words]
                self.off += words
                if dt != F32:
                    v = v.bitcast(dt)
                v = v[:, 0:n]
                if len(shape) == 3:
                    v = v.rearrange("p (a b) -> p a b", a=shape[1])
                elif len(shape) == 4:
                    v = v.rearrange("p (a b c) -> p a b c", a=shape[1], b=shape[2])
                return v

        AR = Arena()

        ident = cst[:, C_ID:C_ID + 128]
        ones_bf = cbf[:, 0:128]
        bones_bf = cbf[:, 128:256]
        perm_bf = cbf[:, 256:384]
        B_pvec, B_cst, B_cbf, B_small = Buf(), Buf(), Buf(), Buf()
        B_bd = Buf()
        xresB = [[Buf() for _ in range(8)] for _ in range(5)]
        ringB = [Buf() for _ in range(3)]
        ringB2 = [Buf() for _ in range(3)]

        def tile_cols(t):
            return (t * TS, TS) if t < 4 else (NT, NS)

        P.op("sp", I("dma_start", out=pvec[:], in_=pvec_d), writes=[B_pvec], dma_sem="setup_p")
        P.op("sp", I("dma_start", out=cst[:], in_=cst_d), writes=[B_cst], dma_sem="setup_c")
        P.op("dve", I("tensor_copy", out=cbf[:, 0:384], in_=cst[:, C_ONES:C_ONES + 384]), reads=[B_cst], writes=[B_cbf])
        P.op("pool", I("memset", bda[:], 0.0), writes=[B_bd])
        P.op("pool", I("memset", bdx[:], 0.0), writes=[B_bd])
        for (src, dst) in ((ga_d, bda), (gx_d, bdx)):
            v = src.rearrange("(cc two) c d -> two c cc d", two=2)
            P.op("pool", I("dma_start", out=dst[0:64, :, 0:64], in_=v[0]), reads=[B_bd], writes=[B_bd], dma_sem="setup2")
            P.op("pool", I("dma_start", out=dst[64:128, :, 64:128], in_=v[1]), reads=[B_bd], writes=[B_bd], dma_sem="setup2")
        P.op("act", I("activation", out=small[:, 16:20], in_=pvec[:, PV_LAM:PV_LAM + 4], func=AF.Exp, scale=-1.0), reads=[B_pvec], writes=[B_small])
        P.op("act", I("activation", out=small[:, 20:24], in_=small[:, 16:20], func=AF.Ln, bias=1.0, scale=1.0), reads=[B_small], writes=[B_small])
        P.op("dve", I("tensor_scalar", out=small[:, 0:4], in0=small[:, 20:24], scalar1=-8.0, scalar2=None, op0=ALU.mult), reads=[B_small], writes=[B_small])
        P.op("dve", I("tensor_scalar", out=small[:, 4:8], in0=small[:, 20:24], scalar1=-16.0, scalar2=None, op0=ALU.mult), reads=[B_small], writes=[B_small])
        P.op("act", I("activation", out=small[:, 8:16], in_=pvec[:, PV_SINK:PV_SINK + 8], func=AF.Exp), reads=[B_pvec, B_small], writes=[B_small])

        def wl_lru(slot):
            v = ewin_d[:, 768:1792].rearrange("(c p) n -> p c n", p=128)
            d = ring[slot][:, 0:8192].rearrange("p (c n) -> p c n", c=8)
            return [I("dma_start", out=d, in_=v)]

        def wl_qkv(slot):
            v = ewin_d[:, 0:768].rearrange("(c p) n -> p c n", p=128)
            d = ring[slot][:, 0:6144].rearrange("p (c n) -> p c n", c=8)
            return [I("dma_start", out=d, in_=v)]

        def wl_sq(src):
            def f(slot):
                v = src.rearrange("(c p) n -> p c n", p=128)
                d = ring[slot][:, 0:8192].rearrange("p (c n) -> p c n", c=8)
                return [I("dma_start", out=d, in_=v)]
            return f

        def wl_ffn(l, hb):
            def f(slot):
                v1 = w1_d[l, :, hb * 512:(hb + 1) * 512].rearrange("(c p) n -> p c n", p=128)
                d1 = ring[slot][:, 0:4096].rearrange("p (c n) -> p c n", c=8)
                v2 = w2_d[l, hb * 512:(hb + 1) * 512, :].rearrange("(c p) n -> p c n", p=128)
                d2 = ring[slot][:, 4096:8192].rearrange("p (c n) -> p c n", c=4)
                return [I("dma_start", out=d1, in_=v1), I("dma_start", out=d2, in_=v2)]
            return f

        wblocks = [wl_lru, wl_qkv, wl_sq(ewout_d)] + [wl_ffn(0, hb) for hb in range(8)] + \
                  [wl_sq(owin_d[:, 0:1024]), wl_sq(owin_d[:, 1024:2048]), wl_sq(owout_d)] + [wl_ffn(1, hb) for hb in range(8)]
        wloaded = [0]

        def prefetch(upto):
            while wloaded[0] <= upto and wloaded[0] < len(wblocks):
                i = wloaded[0]
                slot = i % 3
                fns = wblocks[i](slot)
                if len(fns) == 2:
                    P.op("pool", fns[1], writes=[ringB2[slot]], after=[ringB[slot]], dma_sem="ring%db" % slot)
                    P.op("pool", fns[0], writes=[ringB[slot]], after=[ringB2[slot]] if False else [], dma_sem="ring%d" % slot)
                else:
                    P.op("pool", fns[0], writes=[ringB[slot], ringB2[slot]], dma_sem="ring%d" % slot)
                wloaded[0] += 1

        def wslot(i):
            assert wloaded[0] > i
            return i % 3

        def load_x_group(src_rows, npart, dst, dstB, xin, xinB, slot):
            P.op("sp", I("dma_start", out=xin[0:npart, :], in_=src_rows), writes=[xinB], dma_sem="xin%d" % slot, arena=True)
            for half in range(2):
                b = nb()
                for cc in range(4):
                    c = half * 4 + cc
                    P.op("pe", I("transpose", ps[b][:, cc * 128:cc * 128 + npart], xin[0:npart, c * 128:(c + 1) * 128], ident[0:npart, 0:npart]),
                         reads=[xinB, B_cst], writes=[psB[b]])
                src = ps[b][:, :].rearrange("p (c n) -> p c n", c=4)[:, :, 0:npart]
                eng = "act" if half == 0 else "dve"
                if eng == "act":
                    P.op("act", I("activation", out=dst[:, half * 4:half * 4 + 4, :], in_=src, func=AF.Copy),
                         reads=[psB[b]], writes=dstB[half * 4:half * 4 + 4])
                else:
                    P.op("dve", I("tensor_copy", out=dst[:, half * 4:half * 4 + 4, :], in_=src),
                         reads=[psB[b]], writes=dstB[half * 4:half * 4 + 4])

        def rmsnorm(x, xB, N, gcol, out, outB, sq, sqB, rstd, rstdB):
            b = nb()
            for c in range(8):
                s = c % len(sq)
                if c % 2 == 0:
                    P.op("act", I("activation", out=sq[s][:, 0:N], in_=x[:, c, :], func=AF.Square), reads=[xB[c]], writes=[sqB[s]])
                else:
                    P.op("pool", I("tensor_tensor", out=sq[s][:, 0:N], in0=x[:, c, :], in1=x[:, c, :], op=ALU.mult), reads=[xB[c]], writes=[sqB[s]])
                P.op("pe", I("matmul", ps[b][:, 0:N], lhsT=ones_bf, rhs=sq[s][:, 0:N], start=(c == 0), stop=(c == 7)),
                     reads=[sqB[s], B_cbf], writes=[psB[b]])
            P.op("act", I("activation", out=rstd[:, 0:N], in_=ps[b][:, 0:N], func=AF.Ln, scale=1.0 / 1024.0, bias=EPS),
                 reads=[psB[b]], writes=[rstdB])
            P.op("act", I("activation", out=rstd[:, 0:N], in_=rstd[:, 0:N], func=AF.Exp, scale=-0.5), reads=[rstdB], writes=[rstdB])
            for c in range(8):
                P.op("dve", I("scalar_tensor_tensor", out=out[:, c, :], in0=x[:, c, :], scalar=pvec[:, gcol + c:gcol + c + 1], in1=rstd[:, 0:N],
                                                                 op0=ALU.mult, op1=ALU.mult),
                     reads=[xB[c], rstdB, B_pvec], writes=[outB[c]])

        def proj_fm(w3, wB, col0, nchunks, xn, xnB, N, evac):
            for m in range(nchunks):
                b = nb()
                for kc in range(8):
                    P.op("pe", I("matmul", ps[b][:, 0:N], lhsT=w3[:, kc, col0 + m * 128:col0 + (m + 1) * 128], rhs=xn[:, kc, :],
                                                                   start=(kc == 0), stop=(kc == 7)),
                         reads=[wB, xnB[kc]], writes=[psB[b]])
                evac(m, b)

        def lru_chunk(c, N, segs, xrb_c, xrbB, T, TB, hinit_fn, hs_out, flag_ap, want_rec, g_c, rec_c, recB):
            nseg, L = segs
            xc, xcb, r, gi, m2, hs = T["xc"], T["xcb"], T["r"], T["gi"], T["m2"], hs_out

            def v3(t):
                return t[:, 0:N].rearrange("p (s l) -> p s l", s=nseg)
            cw = lambda i: pvec[:, PV_CW + c * 4 + i:PV_CW + c * 4 + i + 1]
            P.op("act", I("activation", out=v3(xc), in_=xrb_c[:, :, 3:3 + L], func=AF.Identity, bias=pvec[:, PV_CB + c:PV_CB + c + 1], scale=cw(3)),
                 reads=[xrbB, B_pvec], writes=[TB["xc"]])
            for k in range(1, 4):
                P.op("dve", I("scalar_tensor_tensor", out=v3(xc), in0=xrb_c[:, :, 3 - k:3 - k + L], scalar=cw(3 - k), in1=v3(xc), op0=ALU.mult, op1=ALU.add),
                     reads=[xrbB, B_pvec, TB["xc"]], writes=[TB["xc"]])
            if flag_ap is not None:
                P.op("act", I("activation", out=xcb[:, 0:N], in_=xc[:, 0:N], func=AF.Copy), reads=[TB["xc"]], writes=[TB["xcb"]])
            else:
                P.op("dve", I("tensor_copy", out=xcb[:, 0:N], in_=xc[:, 0:N]), reads=[TB["xc"]], writes=[TB["xcb"]])
            ba_, bx_ = nb(), nb()
            P.op("pe", I("matmul", ps[ba_][:, 0:N], lhsT=bda[:, c, :], rhs=xcb[:, 0:N], start=True, stop=True), reads=[B_bd, TB["xcb"]], writes=[psB[ba_]])
            P.op("pe", I("matmul", ps[bx_][:, 0:N], lhsT=bdx[:, c, :], rhs=xcb[:, 0:N], start=True, stop=True), reads=[B_bd, TB["xcb"]], writes=[psB[bx_]])
            P.op("act", I("activation", out=r[:, 0:N], in_=ps[ba_][:, 0:N], func=AF.Sigmoid, bias=pvec[:, PV_BA + c:PV_BA + c + 1]), reads=[psB[ba_], B_pvec], writes=[TB["r"]])
            P.op("act", I("activation", out=gi[:, 0:N], in_=ps[bx_][:, 0:N], func=AF.Sigmoid, bias=pvec[:, PV_BX + c:PV_BX + c + 1]), reads=[psB[bx_], B_pvec], writes=[TB["gi"]])
            P.op("act", I("activation", out=r[:, 0:N], in_=r[:, 0:N], func=AF.Exp, scale=small[:, c:c + 1]), reads=[TB["r"], B_small], writes=[TB["r"]])
            P.op("pool", I("tensor_tensor", out=m2[:, 0:N], in0=r[:, 0:N], in1=r[:, 0:N], op=ALU.mult), reads=[TB["r"]], writes=[TB["m2"]])
            P.op("act", I("activation", out=m2[:, 0:N], in_=m2[:, 0:N], func=AF.Ln, bias=1.0, scale=-1.0), reads=[TB["m2"]], writes=[TB["m2"]])
            P.op("act", I("activation", out=m2[:, 0:N], in_=m2[:, 0:N], func=AF.Exp, scale=0.5), reads=[TB["m2"]], writes=[TB["m2"]])
            P.op("pool", I("tensor_tensor", out=gi[:, 0:N], in0=gi[:, 0:N], in1=xc[:, 0:N], op=ALU.mult), reads=[TB["gi"], TB["xc"]], writes=[TB["gi"]])
            P.op("dve", I("tensor_tensor", out=gi[:, 0:N], in0=gi[:, 0:N], in1=m2[:, 0:N], op=ALU.mult), reads=[TB["gi"], TB["m2"]], writes=[TB["gi"]])
            for s in range(nseg):
                init_ap, initB = hinit_fn(s)
                P.op("dve", I("tensor_tensor_scan", out=hs[:, s * L:(s + 1) * L], data0=r[:, s * L:(s + 1) * L], data1=gi[:, s * L:(s + 1) * L],
                                                                                initial=init_ap, op0=ALU.mult, op1=ALU.add),
                     reads=[TB["r"], TB["gi"], initB], writes=[TB["hs"]])
            if want_rec:
                P.op("pool", I("tensor_tensor", out=rec_c, in0=hs[:, 0:N], in1=g_c, op=ALU.mult), reads=[TB["hs"], TB["g"]], writes=[recB])

        main_loads = []
        for t in range(5):
            c0, N = tile_cols(t)
            ngrp = N // 128 if N >= 128 else 1
            for g in range(ngrp):
                npart = min(128, N)
                src = xmain_d[c0 + g * 128:c0 + g * 128 + npart, :] if t < 4 else xsm_d[0:NS, :]
                dst = xres[:, :, c0 + g * 128:c0 + g * 128 + npart]
                main_loads.append((src, npart, dst, xresB[t]))
        prefetch(0)

        G = {}

        def alloc_persist(with_kv=True):
            AR.reset()
            G["xrb"] = AR.get([128, 4, 515], F32)
            G["hstate"] = AR.get([128, 4], F32)
            if with_kv:
                G["KT"] = AR.get([128, 640], BF16)
                G["VA"] = AR.get([64, 10, 256], BF16)

        def alloc_common(NB, nq, nsets=2, nsq=4, want_g=True, want_q=True, nxn=1):
            G["xn_l"] = [AR.get([128, 8, NB], BF16) for _ in range(nxn)]
            G["xnB_l"] = [[Buf() for _ in range(8)] for _ in range(nxn)]
            G["xn"], G["xnB"] = G["xn_l"][0], G["xnB_l"][0]
            G["sq"] = [AR.get([128, NB], BF16) for _ in range(nsq)]
            G["sqB"] = [Buf() for _ in range(nsq)]
            G["rstd"] = AR.get([128, NB], F32)
            G["rstdB"] = Buf()
            TT, TTB = [], []
            hs_ = AR.get([128, NB], F32)
            g_ = AR.get([128, NB], F32) if want_g else None
            hsB_, gB_ = Buf(), Buf()
            for i in range(nsets):
                d = {k: AR.get([128, NB], F32) for k in ("xc", "r", "gi", "m2")}
                d["xcb"] = AR.get([128, NB], BF16)
                d["hs"], d["g"] = hs_, g_
                TT.append(d)
                dB = {k: Buf() for k in ("xc", "r", "gi", "m2", "xcb")}
                dB["hs"], dB["g"] = hsB_, gB_
                TTB.append(dB)
            G["TT"], G["TTB"] = TT, TTB
            if want_q:
                G["qf"] = AR.get([128, nq, NB], F32)
                G["qfB"] = [Buf() for _ in range(nq)]
                G["qnb"] = AR.get([128, NB], BF16)
                G["qnbB"] = Buf()
                G["cs"] = AR.get([128, 2, NB], F32)
                G["csB"] = Buf()
                G["tmp1"] = AR.get([128, max(NB, 64)], F32)
                G["tmp1B"] = Buf()

        xrbB = [Buf() for _ in range(4)]
        hstB = [Buf() for _ in range(4)]
        KTB, VAB = Buf(), Buf()

        alloc_persist(with_kv=False)
        xin = [AR.get([128, 1024], F32) for _ in range(2)]
        xinB = [Buf(), Buf()]
        alloc_common(512, 2, nsets=4, nsq=4, want_g=False, want_q=False, nxn=2)
        wstg = ring[2][:, 0:8192].bitcast(F32).rearrange("p (c n) -> p c n", c=8)
        wpx = AR.get([128, 8, 512], BF16)
        wpxB = Buf()
        P.op("sp", I("dma_start", out=wstg, in_=ewin_d[:, 768:1280].rearrange("(c p) n -> p c n", p=128)), writes=[ringB[2], ringB2[2]], dma_sem="wstg")
        for kc in range(8):
            P.op("dve" if kc % 2 == 0 else "pool", I("tensor_scalar", out=wpx[:, kc, :], in0=wstg[:, kc, :], scalar1=pvec[:, PV_ENORM + kc:PV_ENORM + kc + 1], scalar2=None, op0=ALU.mult),
                 reads=[ringB[2], B_pvec], writes=[wpxB])

        wl = ring[0][:, 0:8192].rearrange("p (c n) -> p c n", c=8)
        wlB = ringB[0]

        P.op("pool", I("memset", G["hstate"][:], 0.0), writes=hstB)
        P.op("pool", I("memset", G["xrb"][:], 0.0), writes=xrbB)
        def lru_tile(xnv, xnvB, N, segs, flag_ap, want_rec, hinit_override=None, hs_keep=None, wxr=None, wxrB=None, evac_rstd=None):
            nseg, L = segs
            if wxr is None:
                wxr, wxrB = wl, wlB
            xrb, hstate, TT, TTB = G["xrb"], G["hstate"], G["TT"], G["TTB"]
            for c in range(4):
                T, TB = TT[c % len(TT)], TTB[c % len(TT)]
                b = nb()
                for kc in range(8):
                    P.op("pe", I("matmul", ps[b][:, 0:N], lhsT=wxr[:, kc, c * 128:(c + 1) * 128], rhs=xnv[:, kc, :], start=(kc == 0), stop=(kc == 7)),
                         reads=[wxrB, xnvB[kc]], writes=[psB[b]])
                if nseg == 1:
                    xrb_c = xrb[:, c, 0:3 + L].unsqueeze(1)
                else:
                    xrb_c = G["xrb_s"][:, c, :, :]
                if evac_rstd is None:
                    P.op("dve", I("tensor_copy", out=xrb_c[:, :, 3:3 + L], in_=ps[b][:, 0:N].rearrange("p (s l) -> p s l", s=nseg)),
                         reads=[psB[b]], writes=[xrbB[c]])
                else:
                    P.op("dve", I("tensor_tensor", out=xrb_c[:, 0, 3:3 + L], in0=ps[b][:, 0:N], in1=evac_rstd[0][:, 0:N], op=ALU.mult),
                         reads=[psB[b], evac_rstd[1]], writes=[xrbB[c]])
                if want_rec:
                    b2 = nb()
                    for kc in range(8):
                        P.op("pe", I("matmul", ps[b2][:, 0:N], lhsT=wl[:, kc, 512 + c * 128:512 + (c + 1) * 128], rhs=xnv[:, kc, :], start=(kc == 0), stop=(kc == 7)),
                             reads=[wlB, xnvB[kc]], writes=[psB[b2]])
                    P.op("act", I("activation", out=T["g"][:, 0:N], in_=ps[b2][:, 0:N], func=AF.Gelu_apprx_tanh), reads=[psB[b2]], writes=[TB["g"]])
                if hinit_override is None:
                    hinit = lambda s, c=c: (hstate[:, c:c + 1], hstB[c])
                else:
                    hinit = lambda s, c=c: hinit_override(c, s)
                rec_c = G["rec"][:, c, 0:N] if want_rec else None
                recB_c = G["recB"][c] if want_rec else None
                lru_chunk(c, N, segs, xrb_c, xrbB[c], T, TB, hinit, T["hs"], flag_ap, want_rec, (T["g"][:, 0:N] if want_rec else None), rec_c, recB_c)
                if nseg == 1:
                    if flag_ap is None:
                        P.op("pool", I("tensor_copy", out=hstate[:, c:c + 1], in_=T["hs"][:, N - 1:N]), reads=[TB["hs"]], writes=[hstB[c]])
                    else:
                        P.op("pool", I("tensor_scalar", out=hstate[:, c:c + 1], in0=T["hs"][:, N - 1:N], scalar1=flag_ap, scalar2=None, op0=ALU.mult), reads=[TB["hs"], B_cst], writes=[hstB[c]])
                    P.op("pool", I("tensor_copy", out=xrb[:, c, 0:3], in_=xrb[:, c, N:N + 3]), reads=[xrbB[c]], writes=[xrbB[c]])
                elif hs_keep is not None:
                    hs_keep(c, T, TB)

        def norm_tile(x, xB, N, gcol, k=0):
            k = k % len(G["xn_l"])
            xn = G["xn_l"][k][:, :, 0:N]
            G["xnB"] = G["xnB_l"][k]
            rmsnorm(x, xB, N, gcol, xn, G["xnB"], G["sq"], G["sqB"], G["rstd"], G["rstdB"])
            return xn

        def prefix_tile(pt, gi0):
            k = pt % 2
            xb, xbB = G["xn_l"][k], G["xnB_l"][k]
            sq, sqB, rstd, rstdB = G["sq"], G["sqB"], G["rstd"], G["rstdB"]
            for g in range(4):
                slot = (gi0 + g) % 2
                src = xpre_d[pt * TS + g * 128:pt * TS + (g + 1) * 128, :]
                P.op("sp", I("dma_start", out=xin[slot][:, :], in_=src), writes=[xinB[slot]], dma_sem="xin%d" % slot, arena=True)
                for half in range(2):
                    b = nb()
                    for cc in range(4):
                        c = half * 4 + cc
                        P.op("pe", I("transpose", ps[b][:, cc * 128:(cc + 1) * 128], xin[slot][:, c * 128:(c + 1) * 128], ident), reads=[xinB[slot], B_cst], writes=[psB[b]])
                    srcp = ps[b][:, :].rearrange("p (c n) -> p c n", c=4)
                    dst = xb[:, half * 4:half * 4 + 4, g * 128:(g + 1) * 128]
                    if half == 0:
                        P.op("act", I("activation", out=dst, in_=srcp, func=AF.Copy), reads=[psB[b]], writes=xbB[0:4])
                    else:
                        P.op("dve", I("tensor_copy", out=dst, in_=srcp), reads=[psB[b]], writes=xbB[4:8])
            bs_ = nb()
            for c in range(8):
                s_ = c % len(sq)
                if c % 2 == 0:
                    P.op("act", I("activation", out=sq[s_][:, :], in_=xb[:, c, :], func=AF.Square), reads=[xbB[c]], writes=[sqB[s_]])
                else:
                    P.op("pool", I("tensor_tensor", out=sq[s_][:, :], in0=xb[:, c, :], in1=xb[:, c, :], op=ALU.mult), reads=[xbB[c]], writes=[sqB[s_]])
                P.op("pe", I("matmul", ps[bs_][:, :], lhsT=ones_bf, rhs=sq[s_][:, :], start=(c == 0), stop=(c == 7)), reads=[sqB[s_], B_cbf], writes=[psB[bs_]])
            P.op("act", I("activation", out=rstd[:, :], in_=ps[bs_][:, :], func=AF.Ln, scale=1.0 / 1024.0, bias=EPS), reads=[psB[bs_]], writes=[rstdB])
            P.op("act", I("activation", out=rstd[:, :], in_=rstd[:, :], func=AF.Exp, scale=-0.5), reads=[rstdB], writes=[rstdB])
            lru_tile(xb, xbB, TS, (1, TS), cst[:, C_PF + pt:C_PF + pt + 1], False, wxr=wpx, wxrB=wpxB, evac_rstd=(rstd, rstdB))

        prefetch(2)
        gctr = [0]

        def load_main(n):
            for _ in range(n):
                if main_loads:
                    src, npart, dst, dB = main_loads.pop(0)
                    load_x_group(src, npart, dst, dB, xin[gctr[0] % 2], xinB[gctr[0] % 2], gctr[0] % 2)
                    gctr[0] += 1
        for pt in range((NPRE // TS) if stage >= 1 else 0):
            prefix_tile(pt, gctr[0])
            gctr[0] += 4
            load_main(2 if pt < 5 else 1)
        load_main(len(main_loads))
        prefetch(2)

        wq = ring[1][:, 0:6144].rearrange("p (c n) -> p c n", c=8)
        wqB = ringB[1]
        wo = ring[2][:, 0:8192].rearrange("p (c n) -> p c n", c=8)
        woB = ringB[2]

        def qk_one(slot, N, gcol, dst, dstB, keep_f32=False):
            qf, qfB, sq, sqB, tmp1, tmp1B, qnb, qnbB, cs, csB = (G[k] for k in ("qf", "qfB", "sq", "sqB", "tmp1", "tmp1B", "qnb", "qnbB", "cs", "csB"))
            s = slot
            P.op("act", I("activation", out=sq[s][:, 0:N], in_=qf[:, slot, 0:N], func=AF.Square), reads=[qfB[slot]], writes=[sqB[s]])
            b = nb()
            P.op("pe", I("matmul", ps[b][:, 0:N], lhsT=bones_bf, rhs=sq[s][:, 0:N], start=True, stop=True), reads=[sqB[s], B_cbf], writes=[psB[b]])
            P.op("act", I("activation", out=tmp1[:, 0:N], in_=ps[b][:, 0:N], func=AF.Ln, scale=1.0 / 64.0, bias=EPS),
                 reads=[psB[b]], writes=[tmp1B])
            P.op("act", I("activation", out=tmp1[:, 0:N], in_=tmp1[:, 0:N], func=AF.Exp, scale=-0.5), reads=[tmp1B], writes=[tmp1B])
            P.op("dve", I("scalar_tensor_tensor", out=qf[:, slot, 0:N], in0=qf[:, slot, 0:N], scalar=pvec[:, gcol:gcol + 1], in1=tmp1[:, 0:N], op0=ALU.mult, op1=ALU.mult),
                 reads=[qfB[slot], tmp1B, B_pvec], writes=[qfB[slot]])
            P.op("dve", I("tensor_copy", out=qnb[:, 0:N], in_=qf[:, slot, 0:N]), reads=[qfB[slot]], writes=[qnbB])
            b2 = nb()
            P.op("pe", I("matmul", ps[b2][:, 0:N], lhsT=perm_bf, rhs=qnb[:, 0:N], start=True, stop=True), reads=[qnbB, B_cbf], writes=[psB[b2]])
            P.op("dve", I("tensor_tensor", out=tmp1[:, 0:N], in0=ps[b2][:, 0:N], in1=cs[:, 1, 0:N], op=ALU.mult), reads=[psB[b2], csB, tmp1B], writes=[tmp1B])
            P.op("pool", I("tensor_tensor", out=qf[:, slot, 0:N], in0=qf[:, slot, 0:N], in1=cs[:, 0, 0:N], op=ALU.mult), reads=[qfB[slot], csB], writes=[qfB[slot]])
            if keep_f32:
                P.op("pool", I("tensor_tensor", out=qf[:, slot, 0:N], in0=qf[:, slot, 0:N], in1=tmp1[:, 0:N], op=ALU.add), reads=[qfB[slot], tmp1B], writes=[qfB[slot]])
                P.op("act", I("activation", out=dst, in_=qf[:, slot, 0:N], func=AF.Copy), reads=[qfB[slot]], writes=[dstB])
            else:
                P.op("pool", I("tensor_tensor", out=dst, in0=qf[:, slot, 0:N], in1=tmp1[:, 0:N], op=ALU.add), reads=[qfB[slot], tmp1B], writes=[dstB])

        def load_cs(col0, N):
            P.op("sp", I("dma_start", out=G["cs"][:, :, 0:N], in_=cstab_d[:, :, col0:col0 + N]), writes=[G["csB"]], dma_sem="cs", arena=True)

        def proj_one(col0, xnv, xnvB, N, slot):
            qf, qfB = G["qf"], G["qfB"]
            b = nb()
            for kc in range(8):
                P.op("pe", I("matmul", ps[b][:, 0:N], lhsT=wq[:, kc, col0:col0 + 128], rhs=xnv[:, kc, :], start=(kc == 0), stop=(kc == 7)),
                     reads=[wqB, xnvB[kc]], writes=[psB[b]])
            P.op("dve", I("tensor_copy", out=qf[:, slot, 0:N], in_=ps[b][:, 0:N]), reads=[psB[b]], writes=[qfB[slot]])

        def k_proj(xnv, xnvB, N, ktdst, keep=False):
            proj_one(512, xnv, xnvB, N, 0)
            qk_one(0, N, PV_KG, ktdst, KTB, keep_f32=keep)

        def q_proj(xnv, xnvB, N):
            for i in range(4):
                proj_one(i * 128, xnv, xnvB, N, i % 2)
                qk_one(i % 2, N, PV_QG, G["QT"][:, i, 0:N], G["QTB"][i])

        def v_proj_chunk(xnv, xnvB, tok0, ntok, b, bcol):
            for kc in range(8):
                P.op("pe", I("matmul", ps[b][0:ntok, bcol:bcol + 128], lhsT=xnv[:, kc, tok0:tok0 + ntok], rhs=wq[:, kc, 640:768], start=(kc == 0), stop=(kc == 7)),
                     reads=[wqB, xnvB[kc]], writes=[psB[b]])

        def va_store(b, nchunks, slot0):
            VA = G["VA"]
            src = ps[b][0:64, :].rearrange("p (j n) -> p j n", j=4)[:, 0:nchunks, :]
            P.op("dve", I("tensor_copy", out=VA[:, slot0:slot0 + nchunks, 0:64], in_=src[:, :, 0:64]), reads=[psB[b]], writes=[VAB])
            P.op("dve", I("tensor_copy", out=VA[:, slot0:slot0 + nchunks, 192:256], in_=src[:, :, 64:128]), reads=[psB[b]], writes=[VAB])

        def wout_residual(t, N):
            c0, _ = tile_cols(t)
            att, rec, attB, recB = G["att"], G["rec"], G["attB"], G["recB"]
            for oc in range(8):
                b = nb()
                for kc in range(8):
                    src = att[:, kc, 0:N] if kc < 4 else rec[:, kc - 4, 0:N]
                    srcB = attB[kc] if kc < 4 else recB[kc - 4]
                    P.op("pe", I("matmul", ps[b][:, 0:N], lhsT=wo[:, kc, oc * 128:(oc + 1) * 128], rhs=src, start=(kc == 0), stop=(kc == 7)),
                         reads=[woB, srcB], writes=[psB[b]])
                P.op("dve", I("tensor_tensor", out=xres[:, oc, c0:c0 + N], in0=ps[b][:, 0:N], in1=xres[:, oc, c0:c0 + N], op=ALU.add),
                     reads=[psB[b], xresB[t][oc]], writes=[xresB[t][oc]])

        P.barrier()
        alloc_persist()
        xin = [AR.get([128, 1024], F32)]
        xinB = [Buf()]
        xT = AR.get([128, 8, 128], F32)
        xTB = [Buf() for _ in range(8)]
        alloc_common(128, 1, nsets=0, nsq=2, want_g=False)
        P.op("pool", I("memset", G["VA"][:], 1.0), writes=[VAB])
        load_x_group(xhalo_d[0:128, :], 128, xT[:, :, 0:128], xTB, xin[0], xinB[0], 0)
        xnh = norm_tile(xT[:, :, 0:128], xTB, 128, PV_ENORM)
        load_cs(0, 128)
        k_proj(xnh, G["xnB"], 128, G["KT"][:, 0:128])
        b = nb()
        v_proj_chunk(xnh, G["xnB"], 0, 64, b, 0)
        v_proj_chunk(xnh, G["xnB"], 64, 64, b, 128)
        va_store(b, 2, 0)
        P.barrier()

        alloc_persist()
        alloc_common(512, 2, nsets=2, nsq=2, nxn=2)
        G["rec"] = AR.get([128, 4, 512], BF16)
        G["recB"] = [Buf() for _ in range(4)]
        G["att"] = AR.get([128, 4, 512], BF16)
        G["attB"] = [Buf() for _ in range(4)]
        G["QT"] = AR.get([128, 4, 512], BF16)
        G["QTB"] = [Buf() for _ in range(4)]
        PT_l = [AR.get([64, 1536], BF16) for _ in range(2)]
        PTB_l = [Buf(), Buf()]
        PTB_g = [[Buf(), Buf()], [Buf(), Buf()]]
        rc = AR.get([128, 256], F32)
        rcB = Buf()
        rc2 = AR.get([128, 256], F32)
        rc2B = Buf()
        eskB = B_small
        csflat = G["cs"].rearrange("p a n -> p (a n)")
        kf32 = csflat[:, 0:128]
        kf32B = G["csB"]
        vlast = csflat[0:64, 128:384].rearrange("p (j n) -> p j n", j=2)
        vlastB = G["csB"]
        outst = csflat[:, 512:1024]
        outstB = G["csB"]
        _ARENA_USE["main"] = AR.off
        def attn_norm_g(g, bank, col0, ncol, eskg, dst, dstB, v4):
            if g == 0:
                o_, d_, r_, rB_ = slice(0, 64), slice(64, 128), rc, rcB
            else:
                o_, d_, r_, rB_ = slice(64, 128), slice(0, 64), rc2, rc2B
            P.op("dve", I("tensor_tensor", out=v4(r_[o_, 0:ncol]), in0=v4(ps[bank][d_, col0:col0 + ncol]), in1=eskg, op=ALU.add), reads=[psB[bank], eskB, rB_], writes=[rB_])
            P.op("act", I("activation", out=r_[o_, 0:ncol], in_=r_[o_, 0:ncol], func=AF.Ln), reads=[rB_], writes=[rB_])
            P.op("act", I("activation", out=r_[o_, 0:ncol], in_=r_[o_, 0:ncol], func=AF.Exp, scale=-1.0), reads=[rB_], writes=[rB_])
            P.op("dve", I("tensor_tensor", out=dst, in0=v4(ps[bank][o_, col0:col0 + ncol]), in1=v4(r_[o_, 0:ncol]), op=ALU.mult), reads=[psB[bank], rB_], writes=dstB)

        def attn_block(t, blk):
            KT, VA, QT, QTB, att, attB = G["KT"], G["VA"], G["QT"], G["QTB"], G["att"], G["attB"]
            PT = PT_l[blk % 2]
            v4 = lambda a: a.rearrange("p (h t) -> p h t", h=4)
            for g in range(2):
                PTB = PTB_g[blk % 2][g]
                sA, sB = nb(), nb()
                pr = slice(0, 64) if g == 0 else slice(64, 128)
                for j in range(3):
                    kcol = (blk + j) * 64
                    for h4 in range(4):
                        bank = sA if j < 2 else sB
                        col = (j % 2) * 256 + h4 * 64
                        P.op("pe", I("matmul", ps[bank][0:64, col:col + 64], lhsT=KT[pr, kcol:kcol + 64], rhs=QT[pr, h4, blk * 64:(blk + 1) * 64], start=True, stop=True),
                             reads=[KTB, QTB[h4]], writes=[psB[bank]])
                if t == 0 and blk < 2:
                    for j in range(3):
                        bank = sA if j < 2 else sB
                        col = (j % 2) * 256
                        kw = dict(bias=cst[0:64, C_HB:C_HB + 1]) if blk + j < 2 else {}
                        P.op("act", I("activation", out=PT[:, j * 512 + g * 256:j * 512 + (g + 1) * 256], in_=ps[bank][0:64, col:col + 256], func=AF.Exp, scale=0.125, **kw),
                             reads=[psB[bank], B_cst, PTB], writes=[PTB])
                else:
                    P.op("act", I("activation", out=PT[:, 0:1024].rearrange("p (j x) -> p j x", j=2)[:, :, g * 256:(g + 1) * 256],
                                  in_=ps[sA][0:64, :].rearrange("p (j x) -> p j x", j=2), func=AF.Exp, scale=0.125),
                         reads=[psB[sA], PTB], writes=[PTB])
                    P.op("act", I("activation", out=PT[:, 1024 + g * 256:1024 + (g + 1) * 256], in_=ps[sB][0:64, 0:256], func=AF.Exp, scale=0.125),
                         reads=[psB[sB], PTB], writes=[PTB])
                for j in range(3):
                    P.op("pe", I("matmul", ps[sB][:, 256:512], lhsT=VA[:, blk + j, g * 128:(g + 1) * 128], rhs=PT[:, j * 512 + g * 256:j * 512 + (g + 1) * 256], start=(j == 0), stop=(j == 2)),
                         reads=[VAB, PTB], writes=[psB[sB]])
                if SUB < 3:
                    continue
                if g == 0:
                    attn_norm_g(0, sB, 256, 256, small[64:128, 8:12].unsqueeze(2).to_broadcast([64, 4, 64]), att[0:64, :, blk * 64:(blk + 1) * 64], attB, v4)
                else:
                    attn_norm_g(1, sB, 256, 256, small[0:64, 12:16].unsqueeze(2).to_broadcast([64, 4, 64]), att[64:128, :, blk * 64:(blk + 1) * 64], attB, v4)

        def l0_main_tile(t):
            c0, N = tile_cols(t)
            KT, VA = G["KT"], G["VA"]
            xv = xres[:, :, c0:c0 + N]
            xnv = norm_tile(xv, xresB[t], N, PV_ENORM, k=t)
            xnB = G["xnB"]
            load_cs(128 + c0, N)
            q_proj(xnv, xnB, N)
            k_proj(xnv, xnB, N, KT[:, 128:640], keep=(t == 3))
            if t == 3:
                b = nb()
                P.op("pe", I("transpose", ps[b][:, 0:128], G["qf"][:, 0, 384:512], ident), reads=[G["qfB"][0], B_cst], writes=[psB[b]])
                P.op("dve", I("tensor_copy", out=kf32[:, :], in_=ps[b][:, 0:128]), reads=[psB[b]], writes=[kf32B])
                P.op("sp", I("dma_start", out=kp_d, in_=kf32[:, :]), reads=[kf32B], dma_sem="o_kf", arena=True)
            for half in range(2):
                b = nb()
                for j in range(4):
                    v_proj_chunk(xnv, xnB, (half * 4 + j) * 64, 64, b, j * 128)
                va_store(b, 4, 2 + half * 4)
                if t == 3 and half == 1:
                    P.op("dve", I("tensor_copy", out=vlast[:, :, :], in_=ps[b][0:64, 256:512].rearrange("p (j n) -> p j n", j=2)), reads=[psB[b]], writes=[vlastB])
                    P.op("sp", I("dma_start", out=vp_d.rearrange("(j p) n -> p j n", p=64), in_=vlast[:, :, :]), reads=[vlastB], dma_sem="o_vl", arena=True)
            for blk in range(8 if SUB >= 2 else 0):
                attn_block(t, blk)
            if t < 3:
                P.op("pool", I("tensor_copy", out=KT[:, 0:128], in_=KT[:, 512:640]), reads=[KTB], writes=[KTB])
                P.op("pool", I("tensor_copy", out=VA[:, 0:2, :], in_=VA[:, 8:10, :]), reads=[VAB], writes=[VAB])
            if SUB >= 4:
                lru_tile(xnv, xnB, N, (1, N), None, True)
            if SUB >= 5:
                wout_residual(t, N)

        for t in range(4 if stage >= 2 else 0):
            l0_main_tile(t)

        b = nb()
        P.op("pe", I("transpose", ps[b][0:4, 0:128], G["hstate"][:, 0:4], ident), reads=hstB + [B_cst], writes=[psB[b]])
        P.op("dve", I("tensor_copy", out=outst[0:4, 0:128], in_=ps[b][0:4, 0:128]), reads=[psB[b]], writes=[outstB])
        P.op("sp", I("dma_start", out=hp_d, in_=outst[0:4, 0:128]), reads=[outstB], dma_sem="o_st", arena=True)
        b = nb()
        for c in range(4):
            P.op("pe", I("transpose", ps[b][0:3, c * 128:(c + 1) * 128], G["xrb"][:, c, 0:3], ident), reads=[xrbB[c], B_cst], writes=[psB[b]])
        P.op("dve", I("tensor_copy", out=outst[0:3, 0:512], in_=ps[b][0:3, 0:512]), reads=[psB[b], outstB], writes=[outstB])
        P.op("sp", I("dma_start", out=cp_d, in_=outst[0:3, 0:512]), reads=[outstB], dma_sem="o_st", arena=True)
        P.barrier()

        def l0_sample_tile():
            AR.reset()
            alloc_common(64, 2)
            G["KT"] = AR.get([128, 64], BF16)
            G["rec"] = AR.get([128, 4, 64], BF16)
            G["recB"] = [Buf() for _ in range(4)]
            G["att"] = AR.get([128, 4, 64], BF16)
            G["attB"] = [Buf() for _ in range(4)]
            G["QT"] = AR.get([128, 4, 64], BF16)
            G["QTB"] = [Buf() for _ in range(4)]
            G["xrb_s"] = AR.get([128, 4, 4, 19], F32)
            xrb_s = G["xrb_s"]
            rc = AR.get([128, 256], F32)
            rc2 = AR.get([128, 256], F32)
            rcB, rc2B = Buf(), Buf()
            outst = AR.get([128, 512], F32)
            outstB = Buf()
            h0s = AR.get([128, 4, 4], F32)
            h0sB = Buf()
            hns = AR.get([128, 4, 4], F32)
            hnsB = Buf()
            ckT = AR.get([128, 4, 128], BF16)
            ckTB = Buf()
            cvA = AR.get([128, 4, 256], BF16)
            cvAB = Buf()
            cld = AR.get([128, 4, 128], F32)
            cldB = Buf()
            vas = AR.get([16, 4, 256], BF16)
            vasB = Buf()
            vsf = AR.get([16, 4, 128], F32)
            vsfB = Buf()
            pts_c = AR.get([128, 512], BF16)
            pts_n = AR.get([16, 512], BF16)
            ptsB = Buf()
            esk_s = AR.get([128, 2, 4, 16], F32)
            eskB = Buf()
            st12 = AR.get([16, 512], F32)
            st12B = Buf()
            cvt = AR.get([128, 48], F32)
            cvtB = Buf()
            KT, QT, QTB, att, attB = G["KT"], G["QT"], G["QTB"], G["att"], G["attB"]
            c0, N = tile_cols(4)
            xv = xres[:, :, c0:c0 + N]
            for g in range(2):
                P.op("dve", I("tensor_copy", out=esk_s[:, g, :, :], in_=small[:, 8 + g * 4:12 + g * 4].unsqueeze(2).to_broadcast([128, 4, 16])),
                     reads=[B_small], writes=[eskB])
            P.op("sp", I("dma_start", out=st12[0:12, :], in_=stc_d), writes=[st12B], dma_sem="st12", arena=True)
            b = nb()
            for c in range(4):
                P.op("pe", I("transpose", ps[b][:, c * 12:(c + 1) * 12], st12[0:12, c * 128:(c + 1) * 128], ident[0:12, 0:12]), reads=[st12B, B_cst], writes=[psB[b]])
            P.op("dve", I("tensor_copy", out=xrb_s[:, :, :, 0:3], in_=ps[b][:, 0:48].rearrange("p (c s k) -> p c s k", c=4, s=4)), reads=[psB[b]], writes=xrbB)
            P.op("sp", I("dma_start", out=st12[0:4, :], in_=sth_d), reads=[st12B], writes=[st12B], dma_sem="st12", arena=True)
            b = nb()
            for c in range(4):
                P.op("pe", I("transpose", ps[b][:, c * 4:(c + 1) * 4], st12[0:4, c * 128:(c + 1) * 128], ident[0:4, 0:4]), reads=[st12B, B_cst], writes=[psB[b]])
            P.op("dve", I("tensor_copy", out=h0s[:, :, :], in_=ps[b][:, 0:16].rearrange("p (c s) -> p c s", c=4)), reads=[psB[b]], writes=[h0sB])
            P.op("sp", I("dma_start", out=cld[:, :, :], in_=ck_d.rearrange("s r n -> r s n")), writes=[cldB], dma_sem="cld", arena=True)
            b = nb()
            for s in range(4):
                P.op("pe", I("transpose", ps[b][:, s * 128:(s + 1) * 128], cld[:, s, :], ident), reads=[cldB, B_cst], writes=[psB[b]])
            P.op("dve", I("tensor_copy", out=ckT[:, :, :], in_=ps[b][:, :].rearrange("p (s n) -> p s n", s=4)), reads=[psB[b]], writes=[ckTB])
            P.op("sp", I("dma_start", out=cld[:, :, :], in_=cv_d.rearrange("s r n -> r s n")), reads=[cldB], writes=[cldB], dma_sem="cld", arena=True)
            P.op("pool", I("memset", cvA[:], 1.0), writes=[cvAB])
            P.op("pool", I("memset", vas[:], 1.0), writes=[vasB])
            P.op("dve", I("tensor_copy", out=cvA[:, :, 0:64], in_=cld[:, :, 0:64]), reads=[cldB, cvAB], writes=[cvAB])
            P.op("dve", I("tensor_copy", out=cvA[:, :, 192:256], in_=cld[:, :, 64:128]), reads=[cldB, cvAB], writes=[cvAB])
            P.op("sp", I("dma_start", out=ksm_d[:, 0:112, :], in_=ck_d[:, 16:128, :]), dma_sem="o_dram")
            P.op("sp", I("dma_start", out=vsm_d[:, 0:112, :], in_=cv_d[:, 16:128, :]), dma_sem="o_dram")

            xns = norm_tile(xv, xresB[4], N, PV_ENORM)
            xnB = G["xnB"]
            load_cs(128 + NT, N)
            q_proj(xns, xnB, N)
            k_proj(xns, xnB, N, KT[:, 0:N], keep=True)
            b = nb()
            P.op("pe", I("transpose", ps[b][0:64, 0:128], G["qf"][:, 0, 0:64], ident), reads=[G["qfB"][0], B_cst], writes=[psB[b]])
            P.op("dve", I("tensor_copy", out=outst[0:64, 0:128], in_=ps[b][0:64, 0:128]), reads=[psB[b], outstB], writes=[outstB])
            for s in range(4):
                P.op("sp", I("dma_start", out=ksm_d[s, 112:128, :], in_=outst[s * 16:(s + 1) * 16, 0:128]), reads=[outstB], dma_sem="o_st", arena=True)
            b = nb()
            for s in range(4):
                v_proj_chunk(xns, xnB, s * 16, 16, b, s * 128)
            srcv = ps[b][0:16, :].rearrange("p (s n) -> p s n", s=4)
            P.op("act", I("activation", out=vas[:, :, 0:64], in_=srcv[:, :, 0:64], func=AF.Copy), reads=[psB[b], vasB], writes=[vasB])
            P.op("dve", I("tensor_copy", out=vas[:, :, 192:256], in_=srcv[:, :, 64:128]), reads=[psB[b], vasB], writes=[vasB])
            P.op("dve", I("tensor_copy", out=vsf[:, :, :], in_=srcv), reads=[psB[b]], writes=[vsfB])
            P.op("sp", I("dma_start", out=vsm_d[:, 112:128, :].rearrange("s r n -> r s n"), in_=vsf[:, :, :]), reads=[vsfB], dma_sem="o_vsf", arena=True)
            scb, snb, ob = [nb(), nb()], [nb(), nb()], nb()
            for s in range(4):
                for h in range(8):
                    g, h4 = h // 4, h % 4
                    pr = slice(0, 64) if g == 0 else slice(64, 128)
                    col = s * 64 + h4 * 16
                    P.op("pe", I("matmul", ps[scb[g]][:, col:col + 16], lhsT=ckT[pr, s, :], rhs=QT[pr, h4, s * 16:(s + 1) * 16], start=True, stop=True),
                         reads=[ckTB, QTB[h4]], writes=[psB[scb[g]]])
                    P.op("pe", I("matmul", ps[snb[g]][0:16, col:col + 16], lhsT=KT[pr, s * 16:(s + 1) * 16], rhs=QT[pr, h4, s * 16:(s + 1) * 16], start=True, stop=True),
                         reads=[KTB, QTB[h4]], writes=[psB[snb[g]]])
            for g in range(2):
                P.op("act", I("activation", out=pts_c[:, g * 256:(g + 1) * 256], in_=ps[scb[g]][:, 0:256], func=AF.Exp, scale=0.125), reads=[psB[scb[g]], ptsB], writes=[ptsB])
                P.op("act", I("activation", out=pts_n[:, g * 256:(g + 1) * 256], in_=ps[snb[g]][0:16, 0:256], func=AF.Exp, scale=0.125), reads=[psB[snb[g]], ptsB], writes=[ptsB])
            for g in range(2):
                for s in range(4):
                    oc0 = g * 256 + s * 64
                    P.op("pe", I("matmul", ps[ob][:, oc0:oc0 + 64], lhsT=cvA[:, s, g * 128:(g + 1) * 128], rhs=pts_c[:, g * 256 + s * 64:g * 256 + (s + 1) * 64], start=True, stop=False),
                         reads=[cvAB, ptsB], writes=[psB[ob]])
                    P.op("pe", I("matmul", ps[ob][:, oc0:oc0 + 64], lhsT=vas[:, s, g * 128:(g + 1) * 128], rhs=pts_n[:, g * 256 + s * 64:g * 256 + (s + 1) * 64], start=False, stop=True),
                         reads=[vasB, ptsB], writes=[psB[ob]])
            v4 = lambda a: a.rearrange("p (s h t) -> p s h t", s=4, h=4)
            P.op("dve", I("tensor_tensor", out=v4(rc[0:64, 0:256]), in0=v4(ps[ob][64:128, 0:256]), in1=esk_s[64:128, 0, :, :].unsqueeze(1).to_broadcast([64, 4, 4, 16]), op=ALU.add),
                 reads=[psB[ob], eskB, rcB], writes=[rcB])
            P.op("act", I("activation", out=rc[0:64, 0:256], in_=rc[0:64, 0:256], func=AF.Ln), reads=[rcB], writes=[rcB])
            P.op("act", I("activation", out=rc[0:64, 0:256], in_=rc[0:64, 0:256], func=AF.Exp, scale=-1.0), reads=[rcB], writes=[rcB])
            P.op("dve", I("tensor_tensor", out=att[0:64, :, 0:64].rearrange("p h (s t) -> p s h t", s=4), in0=v4(ps[ob][0:64, 0:256]), in1=v4(rc[0:64, 0:256]), op=ALU.mult),
                 reads=[psB[ob], rcB], writes=attB)
            P.op("dve", I("tensor_tensor", out=v4(rc2[64:128, 0:256]), in0=v4(ps[ob][0:64, 256:512]), in1=esk_s[0:64, 1, :, :].unsqueeze(1).to_broadcast([64, 4, 4, 16]), op=ALU.add),
                 reads=[psB[ob], eskB, rc2B], writes=[rc2B])
            P.op("act", I("activation", out=rc2[64:128, 0:256], in_=rc2[64:128, 0:256], func=AF.Ln), reads=[rc2B], writes=[rc2B])
            P.op("act", I("activation", out=rc2[64:128, 0:256], in_=rc2[64:128, 0:256], func=AF.Exp, scale=-1.0), reads=[rc2B], writes=[rc2B])
            P.op("dve", I("tensor_tensor", out=att[64:128, :, 0:64].rearrange("p h (s t) -> p s h t", s=4), in0=v4(ps[ob][64:128, 256:512]), in1=v4(rc2[64:128, 0:256]), op=ALU.mult),
                 reads=[psB[ob], rc2B], writes=attB)

            def keep_h(c, T, TB):
                P.op("act", I("activation", out=hns[:, c, :], in_=T["hs"][:, 0:64].rearrange("p (s l) -> p s l", s=4)[:, :, 15], func=AF.Copy), reads=[TB["hs"], hnsB], writes=[hnsB])
            lru_tile(xns, xnB, N, (4, 16), None, True, hinit_override=lambda c, s: (h0s[:, c, s:s + 1], h0sB), hs_keep=keep_h)
            wout_residual(4, N)
            b = nb()
            for c in range(4):
                P.op("pe", I("transpose", ps[b][0:4, c * 128:(c + 1) * 128], hns[:, c, :], ident), reads=[hnsB, B_cst], writes=[psB[b]])
            P.op("dve", I("tensor_copy", out=outst[0:4, 0:512], in_=ps[b][0:4, 0:512]), reads=[psB[b], outstB], writes=[outstB])
            P.op("sp", I("dma_start", out=hsm_d, in_=outst[0:4, 0:512]), reads=[outstB], dma_sem="o_st", arena=True)
            P.op("pool", I("tensor_copy", out=cvt[:, 0:48].rearrange("p (c s k) -> p c s k", c=4, s=4), in_=xrb_s[:, :, :, 16:19]), reads=xrbB, writes=[cvtB])
            b = nb()
            for c in range(4):
                P.op("pe", I("transpose", ps[b][0:12, c * 128:(c + 1) * 128], cvt[:, c * 12:(c + 1) * 12], ident), reads=[cvtB, B_cst], writes=[psB[b]])
            P.op("dve", I("tensor_copy", out=outst[0:12, 0:512], in_=ps[b][0:12, 0:512]), reads=[psB[b], outstB], writes=[outstB])
            P.op("sp", I("dma_start", out=csm_d, in_=outst[0:12, 0:512]), reads=[outstB], dma_sem="o_st", arena=True)

        P.cur_tag = "sample"
        if stage >= 3:
            l0_sample_tile()
        P.cur_tag = None
        prefetch(5)
        if not MERGE_SAMPLE_FFN:
            P.barrier()

        ystB = [Buf(), Buf()]
        yctr = [0]

        def out_tile(t, yst):
            c0, N = tile_cols(t)
            ng = max(1, N // 128)
            npart = min(128, N)
            for g in range(ng):
                k = yctr[0] % 2
                ybuf, yB = yst[k], ystB[k]
                for half in range(2):
                    b = nb()
                    for cc in range(4):
                        c = half * 4 + cc
                        P.op("pe", I("transpose", ps[b][0:npart, cc * 128:(cc + 1) * 128], xres[:, c, c0 + g * 128:c0 + g * 128 + npart], ident),
                             reads=[xresB[t][c], B_cst], writes=[psB[b]])
                    if half == 0:
                        P.op("act", I("activation", out=ybuf[0:npart, 0:512], in_=ps[b][0:npart, :], func=AF.Copy), reads=[psB[b]], writes=[yB])
                    else:
                        P.op("dve", I("tensor_copy", out=ybuf[0:npart, 512:1024], in_=ps[b][0:npart, :]), reads=[psB[b]], writes=[yB])
                dst = y_d[c0 + g * 128:c0 + g * 128 + npart, :] if t < 4 else ys_d[0:NS, :]
                P.op("sp", I("dma_start", out=dst, in_=ybuf[0:npart, :]), reads=[yB], dma_sem="o_y%d" % k, arena=True)
                yctr[0] += 1

        def ffn_phase(layer, gcol, wbase, emit_out=False, reset=True):
            if reset:
                AR.reset()
            xna = AR.get([128, 8, NTOK], BF16)
            xnaB = [[Buf() for _ in range(8)] for _ in range(5)]
            sq_ = [AR.get([128, 512], BF16) for _ in range(4)]
            sqB_ = [Buf() for _ in range(4)]
            rstd_l = [AR.get([128, 512], F32) for _ in range(2)]
            rstdB_l = [Buf(), Buf()]
            hb_ = [AR.get([128, 4, 512], BF16) for _ in range(2)]
            hbB = [[Buf() for _ in range(4)] for _ in range(2)]
            rl = [AR.get([128, 512], F32) for _ in range(2)]
            rlB = [Buf(), Buf()]
            yst_f = [AR.get([128, 1024], F32) for _ in range(2)] if emit_out else None
            for t in range(5):
                c0, N = tile_cols(t)
                rmsnorm(xres[:, :, c0:c0 + N], xresB[t], N, gcol, xna[:, :, c0:c0 + N], xnaB[t], sq_, sqB_, rstd_l[t % 2], rstdB_l[t % 2])
            it = [0]
            for hb in range(8):
                wi = wbase + hb
                prefetch(wi + 2)
                slot = wslot(wi)
                W1 = ring[slot][:, 0:4096].rearrange("p (c n) -> p c n", c=8)
                W2 = ring[slot][:, 4096:8192].rearrange("p (c n) -> p c n", c=4)
                WB = ringB[slot]
                WB2 = ringB2[slot]

                def mm1(t, hsel):
                    c0, N = tile_cols(t)
                    for m in range(4):
                        b = nb()
                        for kc in range(8):
                            P.op("pe", I("matmul", ps[b][:, 0:N], lhsT=W1[:, kc, m * 128:(m + 1) * 128], rhs=xna[:, kc, c0:c0 + N], start=(kc == 0), stop=(kc == 7)),
                                 reads=[WB, xnaB[t][kc]], writes=[psB[b]])
                        r_ = m % 2
                        P.op("act", I("activation", out=rl[r_][:, 0:N], in_=ps[b][:, 0:N], func=AF.Relu), reads=[psB[b]], writes=[rlB[r_]])
                        P.op("act", I("activation", out=hb_[hsel][:, m, 0:N], in_=rl[r_][:, 0:N], func=AF.Square), reads=[rlB[r_]], writes=[hbB[hsel][m]])

                def mm2(t, hsel):
                    c0, N = tile_cols(t)
                    for oc in range(8):
                        b = nb()
                        for m in range(4):
                            P.op("pe", I("matmul", ps[b][:, 0:N], lhsT=W2[:, m, oc * 128:(oc + 1) * 128], rhs=hb_[hsel][:, m, 0:N], start=(m == 0), stop=(m == 3)),
                                 reads=[WB2, hbB[hsel][m]], writes=[psB[b]])
                        P.op("dve", I("tensor_tensor", out=xres[:, oc, c0:c0 + N], in0=ps[b][:, 0:N], in1=xres[:, oc, c0:c0 + N], op=ALU.add),
                             reads=[psB[b], xresB[t][oc]], writes=[xresB[t][oc]])

                mm1(0, it[0] % 2)
                for t in range(5):
                    if t + 1 < 5:
                        mm1(t + 1, (it[0] + 1) % 2)
                    mm2(t, it[0] % 2)
                    it[0] += 1
                    if emit_out and hb == 7:
                        out_tile(t, yst_f)

        if stage >= 4:
            ffn_phase(0, PV_F0NORM, 3, reset=not MERGE_SAMPLE_FFN)
        prefetch(13)
        P.barrier()

        AR.reset()
        xn1 = AR.get([128, 8, 512], BF16)
        xn1B = [Buf() for _ in range(8)]
        sq1 = [AR.get([128, 512], BF16) for _ in range(4)]
        sq1B = [Buf() for _ in range(4)]
        rstd1 = AR.get([128, 512], F32)
        rstd1B = Buf()
        ub = AR.get([128, 8, 512], BF16)
        ubB = [Buf() for _ in range(8)]
        vz = [AR.get([128, 1024], F32) for _ in range(3)]
        vzB = [Buf(), Buf(), Buf()]
        vn = AR.get([128, 4, 1024], BF16)
        vnB = [Buf() for _ in range(4)]
        gt = AR.get([128, 8, 512], BF16)
        gtB = [Buf() for _ in range(8)]
        vgb = AR.get([128, 1024], F32)
        bsb = AR.get([128, 8, 128], F32)
        bss = AR.get([128, 8, 4, 16], F32)
        wsf = AR.get([128, 8, 128], F32)
        wsb = AR.get([128, 8, 128], BF16)
        wbdf = AR.get([64, 8, 64], F32)
        wbd = AR.get([64, 8, 64], BF16)
        stt_l = [AR.get([128, 8], F32) for _ in range(2)]
        sttB_l = [Buf(), Buf()]
        sqd_l = [AR.get([128, 512], F32) for _ in range(2)]
        sqdB_l = [Buf(), Buf()]
        tmpg_l = [AR.get([128, 512], F32) for _ in range(3)]
        tmpgB_l = [Buf() for _ in range(3)]
        B_l1c = Buf()
        P.op("sp", I("dma_start", out=vgb[:, :], in_=ovg_d.partition_broadcast(128)), writes=[B_l1c], dma_sem="l1c", arena=True)
        P.op("sp", I("dma_start", out=bsb[:, :, :], in_=osb_d.partition_broadcast(128)), writes=[B_l1c], dma_sem="l1c", arena=True)
        P.op("sp", I("dma_start", out=wsf[:, :, :], in_=wst_d.rearrange("g s t -> s g t")), writes=[B_l1c], dma_sem="l1c", arena=True)
        P.op("pool", I("memset", wbdf[:], 0.0), writes=[B_l1c])
        for i in range(4):
            P.op("sp", I("dma_start", out=wbdf[16 * i:16 * i + 16, :, 16 * i:16 * i + 16], in_=wst_d[:, 0:16, 0:16].rearrange("g s t -> s g t")), reads=[B_l1c], writes=[B_l1c], dma_sem="l1c", arena=True)
        P.op("dve", I("tensor_tensor", out=wsb[:, :, :], in0=wsf[:, :, :], in1=cst[:, C_TRIU:C_TRIU + 128].unsqueeze(1).to_broadcast([128, 8, 128]), op=ALU.mult), reads=[B_l1c, B_cst], writes=[B_l1c])
        P.op("dve", I("tensor_tensor", out=wbd[:, :, :], in0=wbdf[:, :, :], in1=cst[0:64, C_BDM:C_BDM + 64].unsqueeze(1).to_broadcast([64, 8, 64]), op=ALU.mult), reads=[B_l1c, B_cst], writes=[B_l1c])
        P.op("dve", I("tensor_copy", out=bss[:, :, :, :], in_=bsb[:, :, 0:16].unsqueeze(2).to_broadcast([128, 8, 4, 16])), reads=[B_l1c], writes=[B_l1c])

        wu = ring[wslot(11)][:, 0:8192].rearrange("p (c n) -> p c n", c=8)
        wuB = ringB[wslot(11)]
        wv = ring[wslot(12)][:, 0:8192].rearrange("p (c n) -> p c n", c=8)
        wvB = ringB[wslot(12)]
        wo1 = ring[wslot(13)][:, 0:8192].rearrange("p (c n) -> p c n", c=8)
        wo1B = ringB[wslot(13)]
        for t in range(5 if stage >= 5 else 0):
            c0, N = tile_cols(t)
            xnv = xn1[:, :, 0:N]
            rmsnorm(xres[:, :, c0:c0 + N], xresB[t], N, PV_ONORM, xnv, xn1B, sq1, sq1B, rstd1, rstd1B)
            proj_fm(wu, wuB, 0, 8, xnv, xn1B, N,
                    lambda m, b: P.op("act", I("activation", out=ub[:, m, 0:N], in_=ps[b][:, 0:N], func=AF.Gelu_apprx_tanh), reads=[psB[b]], writes=[ubB[m]]))
            ng = max(1, N // 128)
            npart = min(128, N)
            for tg in range(ng):
                z = vz[tg % 3]
                zB = vzB[tg % 3]
                stt, sttB = stt_l[tg % 2], sttB_l[tg % 2]
                sqd, sqdB = sqd_l[tg % 2], sqdB_l[tg % 2]
                for half in range(2):
                    b = nb()
                    for kc in range(8):
                        P.op("pe", I("matmul", ps[b][0:npart, :], lhsT=xnv[:, kc, tg * 128:tg * 128 + npart], rhs=wv[:, kc, half * 512:(half + 1) * 512], start=(kc == 0), stop=(kc == 7)),
                             reads=[wvB, xn1B[kc]], writes=[psB[b]])
                    P.op("act", I("activation", out=z[0:npart, half * 512:(half + 1) * 512], in_=ps[b][0:npart, :], func=AF.Gelu_apprx_tanh, accum_out=stt[0:npart, half:half + 1]),
                         reads=[psB[b], sttB], writes=[zB, sttB])
                P.op("dve", I("tensor_tensor", out=stt[0:npart, 2:3], in0=stt[0:npart, 0:1], in1=stt[0:npart, 1:2], op=ALU.add), reads=[sttB], writes=[sttB])
                P.op("dve", I("tensor_scalar", out=stt[0:npart, 2:3], in0=stt[0:npart, 2:3], scalar1=-1.0 / 1024.0, scalar2=None, op0=ALU.mult), reads=[sttB], writes=[sttB])
                P.op("dve", I("tensor_scalar", out=z[0:npart, :], in0=z[0:npart, :], scalar1=stt[0:npart, 2:3], scalar2=None, op0=ALU.add), reads=[zB, sttB], writes=[zB])
                P.op("act", I("activation", out=sqd[0:npart, 0:512], in_=z[0:npart, 0:512], func=AF.Square, accum_out=stt[0:npart, 3:4]), reads=[zB, sttB, sqdB], writes=[sqdB, sttB])
                P.op("act", I("activation", out=sqd[0:npart, 0:512], in_=z[0:npart, 512:1024], func=AF.Square, accum_out=stt[0:npart, 4:5]), reads=[zB, sttB, sqdB], writes=[sqdB, sttB])
                P.op("dve", I("tensor_tensor", out=stt[0:npart, 5:6], in0=stt[0:npart, 3:4], in1=stt[0:npart, 4:5], op=ALU.add), reads=[sttB], writes=[sttB])
                P.op("act", I("activation", out=stt[0:npart, 5:6], in_=stt[0:npart, 5:6], func=AF.Ln, scale=1.0 / 1024.0, bias=EPS), reads=[sttB], writes=[sttB])
                P.op("act", I("activation", out=stt[0:npart, 5:6], in_=stt[0:npart, 5:6], func=AF.Exp, scale=-0.5), reads=[sttB], writes=[sttB])
                P.op("dve", I("scalar_tensor_tensor", out=z[0:npart, :], in0=z[0:npart, :], scalar=stt[0:npart, 5:6], in1=vgb[0:npart, :], op0=ALU.mult, op1=ALU.mult),
                     reads=[zB, sttB, B_l1c], writes=[zB])
                P.op("act", I("activation", out=vn[0:npart, tg, :], in_=z[0:npart, :], func=AF.Copy), reads=[zB], writes=[vnB[tg]])
                if t == 4:
                    P.op("sp", I("dma_start", out=gv_d, in_=z[0:NS, :]), reads=[zB], dma_sem="o_gv", arena=True)
            for g in range(8):
                b = nb()
                tmpg, tmpgB = tmpg_l[g % 3], tmpgB_l[g % 3]
                for tg in range(ng):
                    if t < 4:
                        P.op("pe", I("matmul", ps[b][:, tg * 128:(tg + 1) * 128], lhsT=vn[:, tg, g * 128:(g + 1) * 128], rhs=wsb[:, g, :], start=True, stop=True),
                             reads=[vnB[tg], B_l1c], writes=[psB[b]])
                    else:
                        P.op("pe", I("matmul", ps[b][:, 0:64], lhsT=vn[0:64, 0, g * 128:(g + 1) * 128], rhs=wbd[:, g, :], start=True, stop=True),
                             reads=[vnB[0], B_l1c], writes=[psB[b]])
                if t < 4:
                    P.op("dve", I("tensor_tensor", out=tmpg[:, 0:512].rearrange("p (a n) -> p a n", a=4), in0=ps[b][:, :].rearrange("p (a n) -> p a n", a=4),
                                                                   in1=bsb[:, g, :].unsqueeze(1).to_broadcast([128, 4, 128]), op=ALU.add), reads=[psB[b], B_l1c, tmpgB], writes=[tmpgB])
                else:
                    P.op("dve", I("tensor_tensor", out=tmpg[:, 0:64], in0=ps[b][:, 0:64], in1=bss[:, g, :, :].rearrange("p a n -> p (a n)"), op=ALU.add),
                         reads=[psB[b], B_l1c, tmpgB], writes=[tmpgB])
                P.op("pool", I("tensor_tensor", out=gt[:, g, 0:N], in0=tmpg[:, 0:N], in1=ub[:, g, 0:N], op=ALU.mult), reads=[tmpgB, ubB[g]], writes=[gtB[g]])
            for oc in range(8):
                b = nb()
                for kc in range(8):
                    P.op("pe", I("matmul", ps[b][:, 0:N], lhsT=wo1[:, kc, oc * 128:(oc + 1) * 128], rhs=gt[:, kc, 0:N], start=(kc == 0), stop=(kc == 7)),
                         reads=[wo1B, gtB[kc]], writes=[psB[b]])
                P.op("dve", I("tensor_tensor", out=xres[:, oc, c0:c0 + N], in0=ps[b][:, 0:N], in1=xres[:, oc, c0:c0 + N], op=ALU.add),
                     reads=[psB[b], xresB[t][oc]], writes=[xresB[t][oc]])
        prefetch(16)
        P.barrier()

        if stage >= 6:
            ffn_phase(1, PV_F1NORM, 14, emit_out=True)
        P.barrier()

        if stage < 6:
            AR.reset()
            yst_l = [AR.get([128, 1024], F32) for _ in range(2)]
            for t in range(5):
                out_tile(t, yst_l)

        _EST[0] = None
        P.emit(final_waits=[k for k in P.dma_counts if str(k).startswith("o_")])
        _EST[0] = (getattr(P, "est_us", None), getattr(P, "phase_end", None))
    return nc


_EST = [None]
_ARENA_USE = {}
_PROG_CACHE = {}
STAGE = 6
SUB = 9


def _rope_tables(pos):
    half = 32
    inv = (np.float32(10000.0) ** (-np.arange(half, dtype=np.float32) / np.float32(half))).astype(np.float32)
    ang = pos.astype(np.float32)[None, :] * inv[:, None]
    cos = np.cos(ang).astype(np.float32)
    sin = np.sin(ang).astype(np.float32)
    cosr = np.concatenate([cos, cos, cos, cos], 0)
    sinr = np.concatenate([-sin, sin, -sin, sin], 0)
    return np.stack([cosr, sinr], 1)


def kernel(x_prompt, x_sample, cache_swa_k, cache_swa_v, state_lru_h, state_lru_conv,
           e_norm_g, e_w_in, e_q_norm_g, e_k_norm_g, e_sinks, e_conv_w, e_conv_b,
           e_gate_a_w, e_gate_a_b, e_gate_x_w, e_gate_x_b, e_lru_lambda, e_w_out,
           o_norm_g, o_w_in, o_v_norm_g, o_spatial_w, o_spatial_b, o_w_out,
           ffn_norm_g, ffn_w1, ffn_w2):
    f = lambda a: np.ascontiguousarray(np.asarray(a, dtype=np.float32))
    x_prompt, x_sample = f(x_prompt), f(x_sample)
    cache_swa_k, cache_swa_v = f(cache_swa_k), f(cache_swa_v)
    state_lru_h, state_lru_conv = f(state_lru_h), f(state_lru_conv)

    def chunks(v):
        v = f(v).reshape(-1, 128)
        return v.T
    pvec = np.zeros((128, NPV), np.float32)
    pvec[:, PV_ENORM:PV_ENORM + 8] = chunks(e_norm_g[0])
    pvec[:, PV_F0NORM:PV_F0NORM + 8] = chunks(ffn_norm_g[0])
    pvec[:, PV_ONORM:PV_ONORM + 8] = chunks(o_norm_g[0])
    pvec[:, PV_F1NORM:PV_F1NORM + 8] = chunks(ffn_norm_g[1])
    pvec[:, PV_QG] = np.tile(f(e_q_norm_g[0]), 2)
    pvec[:, PV_KG] = np.tile(f(e_k_norm_g[0]), 2)
    cw = f(e_conv_w[0])
    for c in range(4):
        for i in range(4):
            pvec[:, PV_CW + c * 4 + i] = cw[i, c * 128:(c + 1) * 128]
    pvec[:, PV_CB:PV_CB + 4] = chunks(e_conv_b[0])
    pvec[:, PV_BA:PV_BA + 4] = chunks(e_gate_a_b[0])
    pvec[:, PV_BX:PV_BX + 4] = chunks(e_gate_x_b[0])
    pvec[:, PV_LAM:PV_LAM + 4] = chunks(e_lru_lambda[0])
    pvec[:, PV_SINK:PV_SINK + 8] = np.broadcast_to(f(e_sinks[0])[None, :], (128, 8))

    qperm = np.concatenate([np.arange(h * 64, (h + 1) * 64) for h in (0, 4, 1, 5, 2, 6, 3, 7)])
    ewin = f(e_w_in[0])
    ewin = np.ascontiguousarray(np.concatenate([ewin[:, qperm], ewin[:, 512:]], 1))
    ewout = f(e_w_out[0])
    ewout = np.ascontiguousarray(np.concatenate([ewout[qperm, :], ewout[512:, :]], 0))
    wst = np.ascontiguousarray(np.transpose(f(o_spatial_w[0]), (0, 2, 1)))

    base = np.zeros((128, NCST), np.float32)
    base[:, C_ID:C_ID + 128] = np.eye(128, dtype=np.float32)
    base[:, C_ONES:C_ONES + 128] = 1.0
    blk = np.arange(128) // 64
    base[:, C_BONES:C_BONES + 128] = (blk[:, None] == blk[None, :]).astype(np.float32)
    pm = np.zeros((128, 128), np.float32)
    for m in range(128):
        k = m + 32 if (m % 64) < 32 else m - 32
        pm[k, m] = 1.0
    base[:, C_PERM:C_PERM + 128] = pm
    base[:, C_TRIU:C_TRIU + 128] = np.triu(np.ones((128, 128), np.float32))
    s_ = np.arange(64)
    base[0:64, C_BDM:C_BDM + 64] = ((s_[:, None] // 16 == s_[None, :] // 16) & (s_[:, None] <= s_[None, :])).astype(np.float32)

    common = {
        "pvec": pvec, "ewin": ewin, "ewout": ewout, "ga": f(e_gate_a_w[0]), "gx": f(e_gate_x_w[0]),
        "owin": f(o_w_in[0]), "owout": f(o_w_out[0]), "wst": wst, "osb": f(o_spatial_b[0]), "ovg": f(o_v_norm_g[0]),
        "w1": f(ffn_w1), "w2": f(ffn_w2),
    }
    in_maps = []
    for c in range(NCORES):
        seq, j = c // 4, c % 4
        start = j * NT
        xmain = x_prompt[seq, start:start + NT]
        xpre = np.zeros((NPRE, 1024), np.float32)
        if start > 0:
            xpre[NPRE - start:] = x_prompt[seq, 0:start]
        xhalo = np.zeros((128, 1024), np.float32)
        if start > 0:
            xhalo[:] = x_prompt[seq, start - 128:start]
        cst = base.copy()
        cst[:, C_HB] = 0.0 if start > 0 else -30000.0
        for ptile in range(NPRE // TS):
            cst[:, C_PF + ptile] = 1.0 if (ptile * TS >= NPRE - start) else 0.0
        pos = np.concatenate([np.arange(start - 128, start + NT), 4096 + (np.arange(NS) % 16)]).astype(np.float32)
        m = dict(common)
        m.update({
            "xmain": np.ascontiguousarray(xmain), "xpre": xpre, "xhalo": xhalo,
            "xsm": np.ascontiguousarray(x_sample[4 * c:4 * c + 4].reshape(NS, 1024)),
            "ck": np.ascontiguousarray(cache_swa_k[0, 4 * c:4 * c + 4].reshape(4, 128, 128)),
            "cv": np.ascontiguousarray(cache_swa_v[0, 4 * c:4 * c + 4].reshape(4, 128, 128)),
            "sth": np.ascontiguousarray(state_lru_h[0, 4 * c:4 * c + 4]),
            "stc": np.ascontiguousarray(state_lru_conv[0, 4 * c:4 * c + 4].reshape(12, 512)),
            "cst": cst, "cstab": np.ascontiguousarray(_rope_tables(pos)),
        })
        in_maps.append(m)

    if "nc" not in _PROG_CACHE:
        _PROG_CACHE["nc"] = build_program(STAGE)
    nc = _PROG_CACHE["nc"]
    res = run_bass_kernel_spmd(nc, in_maps, core_ids=list(range(NCORES)))
    r = res.results

    y_prompt = np.stack([np.concatenate([r[s * 4 + j]["y"] for j in range(4)], 0) for s in range(2)], 0)
    y_sample = np.concatenate([r[c]["ys"].reshape(4, 16, 1024) for c in range(NCORES)], 0)
    kp = np.stack([r[s * 4 + 3]["kp"].reshape(128, 2, 64) for s in range(2)], 0)[None]
    vp = np.stack([r[s * 4 + 3]["vp"].reshape(128, 2, 64) for s in range(2)], 0)[None]
    hp = np.stack([r[s * 4 + 3]["hp"].reshape(512) for s in range(2)], 0)[None]
    cp = np.stack([r[s * 4 + 3]["cp"] for s in range(2)], 0)[None]
    ksm = np.concatenate([r[c]["ksm"].reshape(4, 128, 2, 64) for c in range(NCORES)], 0)[None]
    vsm = np.concatenate([r[c]["vsm"].reshape(4, 128, 2, 64) for c in range(NCORES)], 0)[None]
    hsm = np.concatenate([r[c]["hsm"] for c in range(NCORES)], 0)[None]
    csm = np.concatenate([r[c]["csm"].reshape(4, 3, 512) for c in range(NCORES)], 0)[None]
    gv = np.concatenate([r[c]["gv"].reshape(4, 16, 1024) for c in range(NCORES)], 0)[None]
    outs = (y_prompt, y_sample, kp, vp, hp, cp, ksm, vsm, hsm, csm, gv)
    return tuple(np.ascontiguousarray(o, dtype=np.float32) for o in outs)
```

```python
import numpy as np
from contextlib import ExitStack
import concourse.bass as bass
import concourse.mybir as mybir
from concourse.bass_utils import run_bass_kernel_spmd

F32 = mybir.dt.float32
BF16 = mybir.dt.bfloat16
AF = mybir.ActivationFunctionType
ALU = mybir.AluOpType

ENGS = ("pe", "act", "dve", "pool", "sp")
SCHED = True
MERGE_SAMPLE_FFN = True
SAMPLE_INORDER = True
TABLE_WAIT = 1.0
PRIO = True
INORDER = set()
NCORES = 8
NT = 2048
NS = 64
NTOK = NT + NS
NPRE = 6144
TS = 512
EPS = 1e-6


_ACT_SET = {"Exp": "e", "Ln": "e", "Sigmoid": "s", "Sqrt": "q", "Gelu_apprx_tanh": "g"}


def _free_elems(ap):
    try:
        sh = ap.shape
        n = 1
        for d in sh[1:]:
            n *= int(d)
        return n
    except Exception:
        return 512


def I(name, *args, **kw):
    import sys
    line = sys._getframe(1).f_lineno

    def f(e):
        r = getattr(e, name)(*args, **kw)
        if _DBG is not None:
            try:
                _DBG.append((r.ins.name, line, name))
            except Exception:
                pass
        return r
    out = kw.get("out", args[0] if args else None)
    n = _free_elems(out) if out is not None else 512
    f.aset = None
    f.lat = 0.0
    if name == "matmul":
        nn = _free_elems(kw.get("rhs", out))
        f.cost = max(nn, 64) / 2000.0 + 0.02
        f.lat = 0.15
    elif name == "transpose":
        f.cost = 0.12
        f.lat = 0.15
    elif name == "activation":
        f.cost = 0.22 + n / 1200.0
        fn = kw.get("func")
        f.aset = _ACT_SET.get(getattr(fn, "name", str(fn)).split(".")[-1])
    elif name == "dma_start":
        f.cost = 0.5
        f.lat = 2.5 + n * 128 * 4 / 150e3
    elif name == "tensor_tensor_scan":
        f.cost = 0.1 + 2.3 * n / 960.0
    elif name == "scalar_tensor_tensor":
        f.cost = 0.1 + 1.9 * n / 960.0
    elif name == "tensor_copy" and out is not None and args[1:2] == () and kw.get("in_") is not None and getattr(kw["in_"], "dtype", None) != getattr(out, "dtype", None):
        f.cost = 0.1 + 1.7 * n / 960.0
    elif name == "reciprocal":
        f.cost = 0.1 + 6.5 * n / 960.0
    else:
        f.cost = 0.1 + n / 960.0
    return f


_DBG = None


class Buf:
    __slots__ = ("name", "w", "r")

    def __init__(self, name=""):
        self.name = name
        self.w = []
        self.r = []


class Op:
    __slots__ = ("eng", "fn", "deps", "dma_sem", "dma_val", "needed", "sig", "is_dma", "phase", "idx", "succ", "nun", "fin", "chain", "why", "start", "tag")

    def __init__(self, eng, fn):
        self.eng = eng
        self.fn = fn
        self.deps = ()
        self.dma_sem = None
        self.dma_val = 0
        self.needed = False
        self.sig = 0
        self.is_dma = False


class Prog:
    def __init__(self, nc):
        self.nc = nc
        self.ops = {e: [] for e in ENGS}
        self.dma_counts = {}
        self.bar = {e: [] for e in ENGS}
        self.arena_dmas = []
        self.phase = 0
        self.nops = 0

    def op(self, eng, fn, reads=(), writes=(), dma_sem=None, arena=False, after=()):
        o = Op(eng, fn)
        o.chain = None
        o.tag = getattr(self, "cur_tag", None)
        o.phase = self.phase
        o.idx = self.nops
        self.nops += 1
        deps = []
        for b in reads:
            deps.extend(b.w)
        for b in writes:
            deps.extend(b.w)
            deps.extend(b.r)
        for b in after:
            deps.extend(b.w)
            deps.extend(b.r)
        if self.bar[eng]:
            deps.extend(self.bar[eng])
            self.bar[eng] = []
        for b in reads:
            b.r.append(o)
        for b in writes:
            b.w = [o]
            b.r = []
        o.deps = deps
        if dma_sem is not None:
            o.is_dma = True
            o.dma_sem = dma_sem
            c = self.dma_counts.get(dma_sem, 0) + 16
            self.dma_counts[dma_sem] = c
            o.dma_val = c
            if arena:
                self.arena_dmas.append(o)
        self.ops[eng].append(o)
        return o

    def barrier(self):
        last = []
        for e in ENGS:
            for o in reversed(self.ops[e]):
                if not o.is_dma:
                    last.append(o)
                    break
        last.extend(self.arena_dmas)
        self.phase_arena = getattr(self, "phase_arena", {})
        self.phase_arena[self.phase] = list(self.arena_dmas)
        self.arena_dmas = []
        for e in ENGS:
            self.bar[e] = list(last)
        self.phase += 1

    def schedule(self):
        import heapq
        allops = [o for e in ENGS for o in self.ops[e]]
        for e in ENGS:
            prev = None
            prev_tag = None
            for o in self.ops[e]:
                if e in INORDER:
                    if prev is not None and not any(d is prev for d in o.deps):
                        o.deps = list(o.deps) + [prev]
                        o.chain = prev
                    prev = o
                elif e == "dve" and SAMPLE_INORDER and o.tag == "sample":
                    if prev_tag is not None and prev_tag.phase == o.phase and not any(d is prev_tag for d in o.deps):
                        o.deps = list(o.deps) + [prev_tag]
                        o.chain = prev_tag
                    prev_tag = o
        for o in allops:
            o.succ = []
            o.fin = None
        for o in allops:
            ds = set(d for d in o.deps if d is not o)
            o.deps = list(ds)
            o.nun = len(ds)
            for d in ds:
                d.succ.append(o)
        bl = {}
        for o in sorted(allops, key=lambda o: -o.idx):
            m = 0.0
            for s_ in o.succ:
                if s_.phase == o.phase:
                    v = bl[id(s_)]
                    if v > m:
                        m = v
            bl[id(o)] = m + o.fn.cost * (2.0 if (o.eng == "pool" and not o.is_dma) else 1.0) + o.fn.lat + 0.2
        order = {e: [] for e in ENGS}
        tfree = {e: 0.0 for e in ENGS}
        pool_mult = 2.0
        aset = [None]
        byphase = {}
        for o in allops:
            byphase.setdefault(o.phase, []).append(o)
        tbase = 0.0
        for ph in sorted(byphase):
            ops = byphase[ph]
            future = {e: [] for e in ENGS}
            avail = {e: [] for e in ENGS}
            inphase = set(id(o) for o in ops)
            left = len(ops)
            for e in ENGS:
                tfree[e] = max(tfree[e], tbase)

            def push(o):
                rt = tbase
                for d in o.deps:
                    t = d.fin + (0.25 if d.eng != o.eng else 0.05)
                    if t > rt:
                        rt = t
                heapq.heappush(future[o.eng], (rt, o.idx, o))
            for o in ops:
                o.nun = sum(1 for d in o.deps if d.fin is None)
                if o.nun == 0:
                    push(o)
            while left:
                best = None
                for e in ENGS:
                    fu, av = future[e], avail[e]
                    while fu and fu[0][0] <= tfree[e]:
                        rt, idx, o = heapq.heappop(fu)
                        heapq.heappush(av, ((-bl[id(o)], idx) if PRIO else idx, o))
                    if av:
                        cand = av[0][1]
                        st = tfree[e]
                        src = "av"
                        if e == "act" and cand.fn.aset is not None and cand.fn.aset != aset[0]:
                            found = False
                            for idx2, o2 in sorted(av, key=lambda t: t[0])[:16]:
                                if o2.fn.aset is None or o2.fn.aset == aset[0]:
                                    cand = o2
                                    found = True
                                    break
                            if not found and TABLE_WAIT > 0 and fu:
                                for rt2, idx2, o2 in sorted(fu)[:8]:
                                    if rt2 > tfree[e] + TABLE_WAIT:
                                        break
                                    if o2.fn.aset == aset[0]:
                                        cand, st, src = o2, rt2, "fu2"
                                        break
                    elif fu:
                        rt, idx, cand = fu[0]
                        st = rt
                        src = "fu"
                    else:
                        continue
                    key = (st, cand.idx)
                    if best is None or key < best[0]:
                        best = (key, e, cand, src)
                (st, _), e, o, src = best
                if src == "fu2":
                    fu_ = future[e]
                    for i_, t_ in enumerate(fu_):
                        if t_[2] is o:
                            fu_.pop(i_)
                            break
                    heapq.heapify(fu_)
                elif src == "av":
                    av = avail[e]
                    for i_, t_ in enumerate(av):
                        if t_[1] is o:
                            av.pop(i_)
                            break
                    heapq.heapify(av)
                else:
                    heapq.heappop(future[e])
                rdy, rdep = tbase, None
                for d in o.deps:
                    if d.fin is not None:
                        t_ = d.fin + (0.25 if d.eng != o.eng else 0.05)
                        if t_ > rdy:
                            rdy, rdep = t_, d
                o.why = ("dep", rdep) if (rdep is not None and rdy >= st - 1e-9) else ("eng", order[e][-1] if order[e] else None)
                o.start = st
                cost = o.fn.cost
                if e == "pool" and not o.is_dma:
                    cost *= pool_mult
                if e == "act" and o.fn.aset is not None and o.fn.aset != aset[0]:
                    cost += 1.3
                    aset[0] = o.fn.aset
                    self.nswitch = getattr(self, "nswitch", 0) + 1
                if o.is_dma:
                    tfree[e] = st + (1.0 if e == "pool" else 0.35)
                    o.fin = st + cost + o.fn.lat
                else:
                    tfree[e] = st + cost
                    o.fin = st + cost + o.fn.lat
                order[e].append(o)
                left -= 1
                for s_ in o.succ:
                    if id(s_) in inphase:
                        s_.nun -= 1
                        if s_.nun == 0:
                            push(s_)
            tbase = max([tbase] + [o.fin for o in ops])
            self.phase_end = getattr(self, 'phase_end', []) + [round(tbase, 1)]
        lastc = {e: None for e in ENGS}
        curph = {e: -1 for e in ENGS}
        phase_last = {}
        for ph in sorted(byphase):
            snap = {}
            for e in ENGS:
                lst = [o for o in order[e] if o.phase == ph and not o.is_dma]
                if lst:
                    lastc[e] = lst[-1]
                snap[e] = lastc[e]
            phase_last[ph] = snap
        pa = getattr(self, "phase_arena", {})
        for e in ENGS:
            prev = None
            for o in order[e]:
                if prev is None or o.phase != prev:
                    if o.phase > 0:
                        phs = [p for p in phase_last if p < o.phase]
                        if phs:
                            pp = max(phs)
                            extra = [x for x in phase_last[pp].values() if x is not None] + list(pa.get(pp, []))
                            o.deps = list(set(o.deps) | set(x for x in extra if x is not o))
                    prev = o.phase
        self.ops = order
        cnt = {}
        for e in ENGS:
            for o in self.ops[e]:
                if o.is_dma:
                    c = cnt.get(o.dma_sem, 0) + 16
                    cnt[o.dma_sem] = c
                    o.dma_val = c
        self.est_us = tbase

    def emit(self, final_waits):
        nc = self.nc
        if SCHED:
            self.schedule()
        for e in ENGS:
            for o in self.ops[e]:
                for d in o.deps:
                    if d is o or d.is_dma:
                        continue
                    if d.eng == "pe" and o.eng == "pe":
                        continue
                    if getattr(o, "chain", None) is d:
                        continue
                    d.needed = True
        for e in ENGS:
            c = 0
            for o in self.ops[e]:
                if o.is_dma:
                    continue
                if o.needed:
                    c += 1
                    o.sig = c
        with ExitStack() as es:
            esem = {e: es.enter_context(nc.semaphore("s_" + e)) for e in ENGS}
            dsem = {k: es.enter_context(nc.semaphore("d_%s" % (k,))) for k in self.dma_counts}
            block = es.enter_context(nc.Block())

            def run(ename, eng):
                seen = {}
                for o in self.ops[ename]:
                    need = {}
                    for d in o.deps:
                        if d is o or getattr(o, "chain", None) is d:
                            continue
                        if d.is_dma:
                            key = ("d", d.dma_sem)
                            val = d.dma_val
                            sem = dsem[d.dma_sem]
                        else:
                            if d.eng == "pe" and ename == "pe":
                                continue
                            key = ("e", d.eng)
                            val = d.sig
                            sem = esem[d.eng]
                        if seen.get(key, 0) >= val:
                            continue
                        if need.get(key, (0, None))[0] < val:
                            need[key] = (val, sem)
                    for key, (val, sem) in need.items():
                        eng.wait_ge(sem, val)
                        seen[key] = val
                    ins = o.fn(eng)
                    if o.is_dma:
                        ins.then_inc(dsem[o.dma_sem], 16)
                    elif o.needed:
                        ins.then_inc(esem[ename], 1)
                if ename == "sp":
                    for key in final_waits:
                        eng.wait_ge(dsem[key], self.dma_counts[key])

            @block.tensor
            def _(eng):
                run("pe", eng)

            @block.scalar
            def _(eng):
                run("act", eng)

            @block.vector
            def _(eng):
                run("dve", eng)

            @block.gpsimd
            def _(eng):
                run("pool", eng)

            @block.sync
            def _(eng):
                run("sp", eng)


PV_ENORM, PV_F0NORM, PV_ONORM, PV_F1NORM = 0, 8, 16, 24
PV_QG, PV_KG, PV_CW, PV_CB, PV_BA, PV_BX, PV_LAM, PV_SINK = 32, 33, 34, 50, 54, 58, 62, 66
NPV = 80
C_ID, C_ONES, C_BONES, C_PERM, C_TRIU, C_BDM, C_HB, C_PF = 0, 128, 256, 384, 512, 640, 704, 705
NCST = 705 + 12


def build_program(stage=6):
    nc = bass.Bass("TRN2", target_bir_lowering=False)

    def din(name, shape):
        return nc.dram_tensor(name, list(shape), F32, kind="ExternalInput").ap()

    def dout(name, shape):
        return nc.dram_tensor(name, list(shape), F32, kind="ExternalOutput").ap()

    xmain_d = din("xmain", [NT, 1024])
    xpre_d = din("xpre", [NPRE, 1024])
    xhalo_d = din("xhalo", [128, 1024])
    xsm_d = din("xsm", [NS, 1024])
    ck_d = din("ck", [4, 128, 128])
    cv_d = din("cv", [4, 128, 128])
    sth_d = din("sth", [4, 512])
    stc_d = din("stc", [12, 512])
    pvec_d = din("pvec", [128, NPV])
    cst_d = din("cst", [128, NCST])
    cstab_d = din("cstab", [128, 2, 128 + NT + NS])
    ewin_d = din("ewin", [1024, 1792])
    ewout_d = din("ewout", [1024, 1024])
    ga_d = din("ga", [8, 64, 64])
    gx_d = din("gx", [8, 64, 64])
    owin_d = din("owin", [1024, 2048])
    owout_d = din("owout", [1024, 1024])
    wst_d = din("wst", [8, 128, 128])
    osb_d = din("osb", [8, 128])
    ovg_d = din("ovg", [1024])
    w1_d = din("w1", [2, 1024, 4096])
    w2_d = din("w2", [2, 4096, 1024])

    y_d = dout("y", [NT, 1024])
    ys_d = dout("ys", [NS, 1024])
    kp_d = dout("kp", [128, 128])
    vp_d = dout("vp", [128, 128])
    hp_d = dout("hp", [4, 128])
    cp_d = dout("cp", [3, 512])
    ksm_d = dout("ksm", [4, 128, 128])
    vsm_d = dout("vsm", [4, 128, 128])
    hsm_d = dout("hsm", [4, 512])
    csm_d = dout("csm", [12, 512])
    gv_d = dout("gv", [NS, 1024])

    P = Prog(nc)
    es = ExitStack()
    with es:
        def sb(name, shape, dt):
            return es.enter_context(nc.sbuf_tensor("sb_" + name, list(shape), dt))

        xres = sb("xres", [128, 8, NTOK], F32)
        ring = [sb("ring%d" % i, [128, 8192], BF16) for i in range(3)]
        pvec = sb("pvec", [128, NPV], F32)
        cst = sb("cst", [128, NCST], F32)
        cbf = sb("cbf", [128, 4 * 128], BF16)
        small = sb("small", [128, 64], F32)
        bda = sb("bda", [128, 4, 128], BF16)
        bdx = sb("bdx", [128, 4, 128], BF16)
        ARENA_F = 22390
        arena = sb("arena", [128, ARENA_F], F32)
        ps = [es.enter_context(nc.psum_tensor("ps%d" % i, [128, 512], F32)) for i in range(8)]
        psB = [Buf("ps%d" % i) for i in range(8)]
        bank_ctr = [0]

        def nb():
            i = bank_ctr[0] % 8
            bank_ctr[0] += 1
            return i

        class Arena:
            def __init__(self):
                self.off = 0

            def reset(self):
                self.off = 0

            def get(self, shape, dt):
                n = int(np.prod(shape[1:]))
                words = n if dt == F32 else (n + 1) // 2
                words = (words + 7) // 8 * 8
                assert self.off + words <= ARENA_F, ("arena overflow", self.off, words)
                v = arena[0:shape[0], self.off:self.off + words]
                self.off += words
                if dt != F32:
                    v = v.bitcast(dt)
                v = v[:, 0:n]
                if len(shape) == 3:
                    v = v.rearrange("p (a b) -> p a b", a=shape[1])
                elif len(shape) == 4:
                    v = v.rearrange("p (a b c) -> p a b c", a=shape[1], b=shape[2])
                return v

        AR = Arena()

        ident = cst[:, C_ID:C_ID + 128]
        ones_bf = cbf[:, 0:128]
        bones_bf = cbf[:, 128:256]
        perm_bf = cbf[:, 256:384]
        B_pvec, B_cst, B_cbf, B_small = Buf(), Buf(), Buf(), Buf()
        B_bd = Buf()
        xresB = [[Buf() for _ in range(8)] for _ in range(5)]
        ringB = [Buf() for _ in range(3)]
        ringB2 = [Buf() for _ in range(3)]

        def tile_cols(t):
            return (t * TS, TS) if t < 4 else (NT, NS)

        P.op("sp", I("dma_start", out=pvec[:], in_=pvec_d), writes=[B_pvec], dma_sem="setup_p")
        P.op("sp", I("dma_start", out=cst[:], in_=cst_d), writes=[B_cst], dma_sem="setup_c")
        P.op("dve", I("tensor_copy", out=cbf[:, 0:384], in_=cst[:, C_ONES:C_ONES + 384]), reads=[B_cst], writes=[B_cbf])
        P.op("pool", I("memset", bda[:], 0.0), writes=[B_bd])
        P.op("pool", I("memset", bdx[:], 0.0), writes=[B_bd])
        for (src, dst) in ((ga_d, bda), (gx_d, bdx)):
            v = src.rearrange("(cc two) c d -> two c cc d", two=2)
            P.op("pool", I("dma_start", out=dst[0:64, :, 0:64], in_=v[0]), reads=[B_bd], writes=[B_bd], dma_sem="setup2")
            P.op("pool", I("dma_start", out=dst[64:128, :, 64:128], in_=v[1]), reads=[B_bd], writes=[B_bd], dma_sem="setup2")
        P.op("act", I("activation", out=small[:, 16:20], in_=pvec[:, PV_LAM:PV_LAM + 4], func=AF.Exp, scale=-1.0), reads=[B_pvec], writes=[B_small])
        P.op("act", I("activation", out=small[:, 20:24], in_=small[:, 16:20], func=AF.Ln, bias=1.0, scale=1.0), reads=[B_small], writes=[B_small])
        P.op("dve", I("tensor_scalar", out=small[:, 0:4], in0=small[:, 20:24], scalar1=-8.0, scalar2=None, op0=ALU.mult), reads=[B_small], writes=[B_small])
        P.op("dve", I("tensor_scalar", out=small[:, 4:8], in0=small[:, 20:24], scalar1=-16.0, scalar2=None, op0=ALU.mult), reads=[B_small], writes=[B_small])
        P.op("act", I("activation", out=small[:, 8:16], in_=pvec[:, PV_SINK:PV_SINK + 8], func=AF.Exp), reads=[B_pvec, B_small], writes=[B_small])

        def wl_lru(slot):
            v = ewin_d[:, 768:1792].rearrange("(c p) n -> p c n", p=128)
            d = ring[slot][:, 0:8192].rearrange("p (c n) -> p c n", c=8)
            return [I("dma_start", out=d, in_=v)]

        def wl_qkv(slot):
            v = ewin_d[:, 0:768].rearrange("(c p) n -> p c n", p=128)
            d = ring[slot][:, 0:6144].rearrange("p (c n) -> p c n", c=8)
            return [I("dma_start", out=d, in_=v)]

        def wl_sq(src):
            def f(slot):
                v = src.rearrange("(c p) n -> p c n", p=128)
                d = ring[slot][:, 0:8192].rearrange("p (c n) -> p c n", c=8)
                return [I("dma_start", out=d, in_=v)]
            return f

        def wl_ffn(l, hb):
            def f(slot):
                v1 = w1_d[l, :, hb * 512:(hb + 1) * 512].rearrange("(c p) n -> p c n", p=128)
                d1 = ring[slot][:, 0:4096].rearrange("p (c n) -> p c n", c=8)
                v2 = w2_d[l, hb * 512:(hb + 1) * 512, :].rearrange("(c p) n -> p c n", p=128)
                d2 = ring[slot][:, 4096:8192].rearrange("p (c n) -> p c n", c=4)
                return [I("dma_start", out=d1, in_=v1), I("dma_start", out=d2, in_=v2)]
            return f

        wblocks = [wl_lru, wl_qkv, wl_sq(ewout_d)] + [wl_ffn(0, hb) for hb in range(8)] + \
                  [wl_sq(owin_d[:, 0:1024]), wl_sq(owin_d[:, 1024:2048]), wl_sq(owout_d)] + [wl_ffn(1, hb) for hb in range(8)]
        wloaded = [0]

        def prefetch(upto):
            while wloaded[0] <= upto and wloaded[0] < len(wblocks):
                i = wloaded[0]
                slot = i % 3
                fns = wblocks[i](slot)
                if len(fns) == 2:
                    P.op("pool", fns[1], writes=[ringB2[slot]], after=[ringB[slot]], dma_sem="ring%db" % slot)
                    P.op("pool", fns[0], writes=[ringB[slot]], after=[ringB2[slot]] if False else [], dma_sem="ring%d" % slot)
                else:
                    P.op("pool", fns[0], writes=[ringB[slot], ringB2[slot]], dma_sem="ring%d" % slot)
                wloaded[0] += 1

        def wslot(i):
            assert wloaded[0] > i
            return i % 3

        def load_x_group(src_rows, npart, dst, dstB, xin, xinB, slot):
            P.op("sp", I("dma_start", out=xin[0:npart, :], in_=src_rows), writes=[xinB], dma_sem="xin%d" % slot, arena=True)
            for half in range(2):
                b = nb()
                for cc in range(4):
                    c = half * 4 + cc
                    P.op("pe", I("transpose", ps[b][:, cc * 128:cc * 128 + npart], xin[0:npart, c * 128:(c + 1) * 128], ident[0:npart, 0:npart]),
                         reads=[xinB, B_cst], writes=[psB[b]])
                src = ps[b][:, :].rearrange("p (c n) -> p c n", c=4)[:, :, 0:npart]
                eng = "act" if half == 0 else "dve"
                if eng == "act":
                    P.op("act", I("activation", out=dst[:, half * 4:half * 4 + 4, :], in_=src, func=AF.Copy),
                         reads=[psB[b]], writes=dstB[half * 4:half * 4 + 4])
                else:
                    P.op("dve", I("tensor_copy", out=dst[:, half * 4:half * 4 + 4, :], in_=src),
                         reads=[psB[b]], writes=dstB[half * 4:half * 4 + 4])

        def rmsnorm(x, xB, N, gcol, out, outB, sq, sqB, rstd, rstdB):
            b = nb()
            for c in range(8):
                s = c % len(sq)
                if c % 2 == 0:
                    P.op("act", I("activation", out=sq[s][:, 0:N], in_=x[:, c, :], func=AF.Square), reads=[xB[c]], writes=[sqB[s]])
                else:
                    P.op("pool", I("tensor_tensor", out=sq[s][:, 0:N], in0=x[:, c, :], in1=x[:, c, :], op=ALU.mult), reads=[xB[c]], writes=[sqB[s]])
                P.op("pe", I("matmul", ps[b][:, 0:N], lhsT=ones_bf, rhs=sq[s][:, 0:N], start=(c == 0), stop=(c == 7)),
                     reads=[sqB[s], B_cbf], writes=[psB[b]])
            P.op("act", I("activation", out=rstd[:, 0:N], in_=ps[b][:, 0:N], func=AF.Ln, scale=1.0 / 1024.0, bias=EPS),
                 reads=[psB[b]], writes=[rstdB])
            P.op("act", I("activation", out=rstd[:, 0:N], in_=rstd[:, 0:N], func=AF.Exp, scale=-0.5), reads=[rstdB], writes=[rstdB])
            for c in range(8):
                P.op("dve", I("scalar_tensor_tensor", out=out[:, c, :], in0=x[:, c, :], scalar=pvec[:, gcol + c:gcol + c + 1], in1=rstd[:, 0:N],
                                                                 op0=ALU.mult, op1=ALU.mult),
                     reads=[xB[c], rstdB, B_pvec], writes=[outB[c]])

        def proj_fm(w3, wB, col0, nchunks, xn, xnB, N, evac):
            for m in range(nchunks):
                b = nb()
                for kc in range(8):
                    P.op("pe", I("matmul", ps[b][:, 0:N], lhsT=w3[:, kc, col0 + m * 128:col0 + (m + 1) * 128], rhs=xn[:, kc, :],
                                                                   start=(kc == 0), stop=(kc == 7)),
                         reads=[wB, xnB[kc]], writes=[psB[b]])
                evac(m, b)

        def lru_chunk(c, N, segs, xrb_c, xrbB, T, TB, hinit_fn, hs_out, flag_ap, want_rec, g_c, rec_c, recB):
            nseg, L = segs
            xc, xcb, r, gi, m2, hs = T["xc"], T["xcb"], T["r"], T["gi"], T["m2"], hs_out

            def v3(t):
                return t[:, 0:N].rearrange("p (s l) -> p s l", s=nseg)
            cw = lambda i: pvec[:, PV_CW + c * 4 + i:PV_CW + c * 4 + i + 1]
            P.op("act", I("activation", out=v3(xc), in_=xrb_c[:, :, 3:3 + L], func=AF.Identity, bias=pvec[:, PV_CB + c:PV_CB + c + 1], scale=cw(3)),
                 reads=[xrbB, B_pvec], writes=[TB["xc"]])
            for k in range(1, 4):
                P.op("dve", I("scalar_tensor_tensor", out=v3(xc), in0=xrb_c[:, :, 3 - k:3 - k + L], scalar=cw(3 - k), in1=v3(xc), op0=ALU.mult, op1=ALU.add),
                     reads=[xrbB, B_pvec, TB["xc"]], writes=[TB["xc"]])
            if flag_ap is not None:
                P.op("act", I("activation", out=xcb[:, 0:N], in_=xc[:, 0:N], func=AF.Copy), reads=[TB["xc"]], writes=[TB["xcb"]])
            else:
                P.op("dve", I("tensor_copy", out=xcb[:, 0:N], in_=xc[:, 0:N]), reads=[TB["xc"]], writes=[TB["xcb"]])
            ba_, bx_ = nb(), nb()
            P.op("pe", I("matmul", ps[ba_][:, 0:N], lhsT=bda[:, c, :], rhs=xcb[:, 0:N], start=True, stop=True), reads=[B_bd, TB["xcb"]], writes=[psB[ba_]])
            P.op("pe", I("matmul", ps[bx_][:, 0:N], lhsT=bdx[:, c, :], rhs=xcb[:, 0:N], start=True, stop=True), reads=[B_bd, TB["xcb"]], writes=[psB[bx_]])
            P.op("act", I("activation", out=r[:, 0:N], in_=ps[ba_][:, 0:N], func=AF.Sigmoid, bias=pvec[:, PV_BA + c:PV_BA + c + 1]), reads=[psB[ba_], B_pvec], writes=[TB["r"]])
            P.op("act", I("activation", out=gi[:, 0:N], in_=ps[bx_][:, 0:N], func=AF.Sigmoid, bias=pvec[:, PV_BX + c:PV_BX + c + 1]), reads=[psB[bx_], B_pvec], writes=[TB["gi"]])
            P.op("act", I("activation", out=r[:, 0:N], in_=r[:, 0:N], func=AF.Exp, scale=small[:, c:c + 1]), reads=[TB["r"], B_small], writes=[TB["r"]])
            P.op("pool", I("tensor_tensor", out=m2[:, 0:N], in0=r[:, 0:N], in1=r[:, 0:N], op=ALU.mult), reads=[TB["r"]], writes=[TB["m2"]])
            P.op("act", I("activation", out=m2[:, 0:N], in_=m2[:, 0:N], func=AF.Ln, bias=1.0, scale=-1.0), reads=[TB["m2"]], writes=[TB["m2"]])
            P.op("act", I("activation", out=m2[:, 0:N], in_=m2[:, 0:N], func=AF.Exp, scale=0.5), reads=[TB["m2"]], writes=[TB["m2"]])
            P.op("pool", I("tensor_tensor", out=gi[:, 0:N], in0=gi[:, 0:N], in1=xc[:, 0:N], op=ALU.mult), reads=[TB["gi"], TB["xc"]], writes=[TB["gi"]])
            P.op("dve", I("tensor_tensor", out=gi[:, 0:N], in0=gi[:, 0:N], in1=m2[:, 0:N], op=ALU.mult), reads=[TB["gi"], TB["m2"]], writes=[TB["gi"]])
            for s in range(nseg):
                init_ap, initB = hinit_fn(s)
                P.op("dve", I("tensor_tensor_scan", out=hs[:, s * L:(s + 1) * L], data0=r[:, s * L:(s + 1) * L], data1=gi[:, s * L:(s + 1) * L],
                                                                                initial=init_ap, op0=ALU.mult, op1=ALU.add),
                     reads=[TB["r"], TB["gi"], initB], writes=[TB["hs"]])
            if want_rec:
                P.op("pool", I("tensor_tensor", out=rec_c, in0=hs[:, 0:N], in1=g_c, op=ALU.mult), reads=[TB["hs"], TB["g"]], writes=[recB])

        main_loads = []
        for t in range(5):
            c0, N = tile_cols(t)
            ngrp = N // 128 if N >= 128 else 1
            for g in range(ngrp):
                npart = min(128, N)
                src = xmain_d[c0 + g * 128:c0 + g * 128 + npart, :] if t < 4 else xsm_d[0:NS, :]
                dst = xres[:, :, c0 + g * 128:c0 + g * 128 + npart]
                main_loads.append((src, npart, dst, xresB[t]))
        prefetch(0)

        G = {}

        def alloc_persist(with_kv=True):
            AR.reset()
            G["xrb"] = AR.get([128, 4, 515], F32)
            G["hstate"] = AR.get([128, 4], F32)
            if with_kv:
                G["KT"] = AR.get([128, 640], BF16)
                G["VA"] = AR.get([64, 10, 256], BF16)

        def alloc_common(NB, nq, nsets=2, nsq=4, want_g=True, want_q=True, nxn=1):
            G["xn_l"] = [AR.get([128, 8, NB], BF16) for _ in range(nxn)]
            G["xnB_l"] = [[Buf() for _ in range(8)] for _ in range(nxn)]
            G["xn"], G["xnB"] = G["xn_l"][0], G["xnB_l"][0]
            G["sq"] = [AR.get([128, NB], BF16) for _ in range(nsq)]
            G["sqB"] = [Buf() for _ in range(nsq)]
            G["rstd"] = AR.get([128, NB], F32)
            G["rstdB"] = Buf()
            TT, TTB = [], []
            hs_ = AR.get([128, NB], F32)
            hs2_ = AR.get([128, NB], F32) if (nsets == 4 and not want_g) else hs_
            g_ = AR.get([128, NB], F32) if want_g else None
            hsB_, gB_ = Buf(), Buf()
            hsB2_ = Buf() if hs2_ is not hs_ else hsB_
            for i in range(nsets):
                d = {k: AR.get([128, NB], F32) for k in ("xc", "r", "gi", "m2")}
                d["xcb"] = AR.get([128, NB], BF16)
                d["hs"], d["g"] = (hs_ if i % 2 == 0 else hs2_), g_
                TT.append(d)
                dB = {k: Buf() for k in ("xc", "r", "gi", "m2", "xcb")}
                dB["hs"], dB["g"] = (hsB_ if i % 2 == 0 else hsB2_), gB_
                TTB.append(dB)
            G["TT"], G["TTB"] = TT, TTB
            if want_q:
                G["qf"] = AR.get([128, nq, NB], F32)
                G["qfB"] = [Buf() for _ in range(nq)]
                G["qnb"] = AR.get([128, NB], BF16)
                G["qnbB"] = Buf()
                G["cs"] = AR.get([128, 2, NB], F32)
                G["csB"] = Buf()
                G["tmp1"] = AR.get([128, max(NB, 64)], F32)
                G["tmp1B"] = Buf()

        xrbB = [Buf() for _ in range(4)]
        hstB = [Buf() for _ in range(4)]
        KTB, VAB = Buf(), Buf()

        alloc_persist(with_kv=False)
        xin = [AR.get([128, 1024], F32) for _ in range(2)]
        xinB = [Buf(), Buf()]
        alloc_common(512, 2, nsets=4, nsq=4, want_g=False, want_q=False, nxn=2)
        wstg = ring[2][:, 0:8192].bitcast(F32).rearrange("p (c n) -> p c n", c=8)
        wpx = AR.get([128, 8, 512], BF16)
        wpxB = Buf()
        P.op("sp", I("dma_start", out=wstg, in_=ewin_d[:, 768:1280].rearrange("(c p) n -> p c n", p=128)), writes=[ringB[2], ringB2[2]], dma_sem="wstg")
        for kc in range(8):
            P.op("dve" if kc % 2 == 0 else "pool", I("tensor_scalar", out=wpx[:, kc, :], in0=wstg[:, kc, :], scalar1=pvec[:, PV_ENORM + kc:PV_ENORM + kc + 1], scalar2=None, op0=ALU.mult),
                 reads=[ringB[2], B_pvec], writes=[wpxB])

        wl = ring[0][:, 0:8192].rearrange("p (c n) -> p c n", c=8)
        wlB = ringB[0]

        P.op("pool", I("memset", G["hstate"][:], 0.0), writes=hstB)
        P.op("pool", I("memset", G["xrb"][:], 0.0), writes=xrbB)
        def lru_tile(xnv, xnvB, N, segs, flag_ap, want_rec, hinit_override=None, hs_keep=None, wxr=None, wxrB=None, evac_rstd=None):
            nseg, L = segs
            if wxr is None:
                wxr, wxrB = wl, wlB
            xrb, hstate, TT, TTB = G["xrb"], G["hstate"], G["TT"], G["TTB"]
            for c in range(4):
                T, TB = TT[c % len(TT)], TTB[c % len(TT)]
                b = nb()
                for kc in range(8):
                    P.op("pe", I("matmul", ps[b][:, 0:N], lhsT=wxr[:, kc, c * 128:(c + 1) * 128], rhs=xnv[:, kc, :], start=(kc == 0), stop=(kc == 7)),
                         reads=[wxrB, xnvB[kc]], writes=[psB[b]])
                if nseg == 1:
                    xrb_c = xrb[:, c, 0:3 + L].unsqueeze(1)
                else:
                    xrb_c = G["xrb_s"][:, c, :, :]
                if evac_rstd is None:
                    P.op("dve", I("tensor_copy", out=xrb_c[:, :, 3:3 + L], in_=ps[b][:, 0:N].rearrange("p (s l) -> p s l", s=nseg)),
                         reads=[psB[b]], writes=[xrbB[c]])
                else:
                    P.op("dve", I("tensor_tensor", out=xrb_c[:, 0, 3:3 + L], in0=ps[b][:, 0:N], in1=evac_rstd[0][:, 0:N], op=ALU.mult),
                         reads=[psB[b], evac_rstd[1]], writes=[xrbB[c]])
                if want_rec:
                    b2 = nb()
                    for kc in range(8):
                        P.op("pe", I("matmul", ps[b2][:, 0:N], lhsT=wl[:, kc, 512 + c * 128:512 + (c + 1) * 128], rhs=xnv[:, kc, :], start=(kc == 0), stop=(kc == 7)),
                             reads=[wlB, xnvB[kc]], writes=[psB[b2]])
                    P.op("act", I("activation", out=T["g"][:, 0:N], in_=ps[b2][:, 0:N], func=AF.Gelu_apprx_tanh), reads=[psB[b2]], writes=[TB["g"]])
                if hinit_override is None:
                    hinit = lambda s, c=c: (hstate[:, c:c + 1], hstB[c])
                else:
                    hinit = lambda s, c=c: hinit_override(c, s)
                rec_c = G["rec"][:, c, 0:N] if want_rec else None
                recB_c = G["recB"][c] if want_rec else None
                lru_chunk(c, N, segs, xrb_c, xrbB[c], T, TB, hinit, T["hs"], flag_ap, want_rec, (T["g"][:, 0:N] if want_rec else None), rec_c, recB_c)
                if nseg == 1:
                    if flag_ap is None:
                        P.op("pool", I("tensor_copy", out=hstate[:, c:c + 1], in_=T["hs"][:, N - 1:N]), reads=[TB["hs"]], writes=[hstB[c]])
                    else:
                        P.op("pool", I("tensor_scalar", out=hstate[:, c:c + 1], in0=T["hs"][:, N - 1:N], scalar1=flag_ap, scalar2=None, op0=ALU.mult), reads=[TB["hs"], B_cst], writes=[hstB[c]])
                    P.op("pool", I("tensor_copy", out=xrb[:, c, 0:3], in_=xrb[:, c, N:N + 3]), reads=[xrbB[c]], writes=[xrbB[c]])
                elif hs_keep is not None:
                    hs_keep(c, T, TB)

        def norm_tile(x, xB, N, gcol, k=0):
            k = k % len(G["xn_l"])
            xn = G["xn_l"][k][:, :, 0:N]
            G["xnB"] = G["xnB_l"][k]
            rmsnorm(x, xB, N, gcol, xn, G["xnB"], G["sq"], G["sqB"], G["rstd"], G["rstdB"])
            return xn

        def prefix_tile(pt, gi0):
            k = pt % 2
            xb, xbB = G["xn_l"][k], G["xnB_l"][k]
            sq, sqB, rstd, rstdB = G["sq"], G["sqB"], G["rstd"], G["rstdB"]
            for g in range(4):
                slot = (gi0 + g) % 2
                src = xpre_d[pt * TS + g * 128:pt * TS + (g + 1) * 128, :]
                P.op("sp", I("dma_start", out=xin[slot][:, :], in_=src), writes=[xinB[slot]], dma_sem="xin%d" % slot, arena=True)
                for half in range(2):
                    b = nb()
                    for cc in range(4):
                        c = half * 4 + cc
                        P.op("pe", I("transpose", ps[b][:, cc * 128:(cc + 1) * 128], xin[slot][:, c * 128:(c + 1) * 128], ident), reads=[xinB[slot], B_cst], writes=[psB[b]])
                    srcp = ps[b][:, :].rearrange("p (c n) -> p c n", c=4)
                    dst = xb[:, half * 4:half * 4 + 4, g * 128:(g + 1) * 128]
                    if half == 0:
                        P.op("act", I("activation", out=dst, in_=srcp, func=AF.Copy), reads=[psB[b]], writes=xbB[0:4])
                    else:
                        P.op("dve", I("tensor_copy", out=dst, in_=srcp), reads=[psB[b]], writes=xbB[4:8])
            bs_ = nb()
            for c in range(8):
                s_ = c % len(sq)
                if c % 2 == 0:
                    P.op("act", I("activation", out=sq[s_][:, :], in_=xb[:, c, :], func=AF.Square), reads=[xbB[c]], writes=[sqB[s_]])
                else:
                    P.op("pool", I("tensor_tensor", out=sq[s_][:, :], in0=xb[:, c, :], in1=xb[:, c, :], op=ALU.mult), reads=[xbB[c]], writes=[sqB[s_]])
                P.op("pe", I("matmul", ps[bs_][:, :], lhsT=ones_bf, rhs=sq[s_][:, :], start=(c == 0), stop=(c == 7)), reads=[sqB[s_], B_cbf], writes=[psB[bs_]])
            P.op("act", I("activation", out=rstd[:, :], in_=ps[bs_][:, :], func=AF.Ln, scale=1.0 / 1024.0, bias=EPS), reads=[psB[bs_]], writes=[rstdB])
            P.op("act", I("activation", out=rstd[:, :], in_=rstd[:, :], func=AF.Exp, scale=-0.5), reads=[rstdB], writes=[rstdB])
            lru_tile(xb, xbB, TS, (1, TS), cst[:, C_PF + pt:C_PF + pt + 1], False, wxr=wpx, wxrB=wpxB, evac_rstd=(rstd, rstdB))

        prefetch(2)
        gctr = [0]

        def load_main(n):
            for _ in range(n):
                if main_loads:
                    src, npart, dst, dB = main_loads.pop(0)
                    load_x_group(src, npart, dst, dB, xin[gctr[0] % 2], xinB[gctr[0] % 2], gctr[0] % 2)
                    gctr[0] += 1
        for pt in range((NPRE // TS) if stage >= 1 else 0):
            prefix_tile(pt, gctr[0])
            gctr[0] += 4
            load_main(2 if pt < 5 else 1)
        load_main(len(main_loads))
        prefetch(2)

        wq = ring[1][:, 0:6144].rearrange("p (c n) -> p c n", c=8)
        wqB = ringB[1]
        wo = ring[2][:, 0:8192].rearrange("p (c n) -> p c n", c=8)
        woB = ringB[2]

        def qk_one(slot, N, gcol, dst, dstB, keep_f32=False):
            qf, qfB, sq, sqB, tmp1, tmp1B, qnb, qnbB, cs, csB = (G[k] for k in ("qf", "qfB", "sq", "sqB", "tmp1", "tmp1B", "qnb", "qnbB", "cs", "csB"))
            s = slot
            P.op("act", I("activation", out=sq[s][:, 0:N], in_=qf[:, slot, 0:N], func=AF.Square), reads=[qfB[slot]], writes=[sqB[s]])
            b = nb()
            P.op("pe", I("matmul", ps[b][:, 0:N], lhsT=bones_bf, rhs=sq[s][:, 0:N], start=True, stop=True), reads=[sqB[s], B_cbf], writes=[psB[b]])
            P.op("act", I("activation", out=tmp1[:, 0:N], in_=ps[b][:, 0:N], func=AF.Ln, scale=1.0 / 64.0, bias=EPS),
                 reads=[psB[b]], writes=[tmp1B])
            P.op("act", I("activation", out=tmp1[:, 0:N], in_=tmp1[:, 0:N], func=AF.Exp, scale=-0.5), reads=[tmp1B], writes=[tmp1B])
            P.op("dve", I("scalar_tensor_tensor", out=qf[:, slot, 0:N], in0=qf[:, slot, 0:N], scalar=pvec[:, gcol:gcol + 1], in1=tmp1[:, 0:N], op0=ALU.mult, op1=ALU.mult),
                 reads=[qfB[slot], tmp1B, B_pvec], writes=[qfB[slot]])
            P.op("dve", I("tensor_copy", out=qnb[:, 0:N], in_=qf[:, slot, 0:N]), reads=[qfB[slot]], writes=[qnbB])
            b2 = nb()
            P.op("pe", I("matmul", ps[b2][:, 0:N], lhsT=perm_bf, rhs=qnb[:, 0:N], start=True, stop=True), reads=[qnbB, B_cbf], writes=[psB[b2]])
            P.op("dve", I("tensor_tensor", out=tmp1[:, 0:N], in0=ps[b2][:, 0:N], in1=cs[:, 1, 0:N], op=ALU.mult), reads=[psB[b2], csB, tmp1B], writes=[tmp1B])
            P.op("pool", I("tensor_tensor", out=qf[:, slot, 0:N], in0=qf[:, slot, 0:N], in1=cs[:, 0, 0:N], op=ALU.mult), reads=[qfB[slot], csB], writes=[qfB[slot]])
            if keep_f32:
                P.op("pool", I("tensor_tensor", out=qf[:, slot, 0:N], in0=qf[:, slot, 0:N], in1=tmp1[:, 0:N], op=ALU.add), reads=[qfB[slot], tmp1B], writes=[qfB[slot]])
                P.op("act", I("activation", out=dst, in_=qf[:, slot, 0:N], func=AF.Copy), reads=[qfB[slot]], writes=[dstB])
            else:
                P.op("pool", I("tensor_tensor", out=dst, in0=qf[:, slot, 0:N], in1=tmp1[:, 0:N], op=ALU.add), reads=[qfB[slot], tmp1B], writes=[dstB])

        def load_cs(col0, N):
            P.op("sp", I("dma_start", out=G["cs"][:, :, 0:N], in_=cstab_d[:, :, col0:col0 + N]), writes=[G["csB"]], dma_sem="cs", arena=True)

        def proj_one(col0, xnv, xnvB, N, slot):
            qf, qfB = G["qf"], G["qfB"]
            b = nb()
            for kc in range(8):
                P.op("pe", I("matmul", ps[b][:, 0:N], lhsT=wq[:, kc, col0:col0 + 128], rhs=xnv[:, kc, :], start=(kc == 0), stop=(kc == 7)),
                     reads=[wqB, xnvB[kc]], writes=[psB[b]])
            P.op("dve", I("tensor_copy", out=qf[:, slot, 0:N], in_=ps[b][:, 0:N]), reads=[psB[b]], writes=[qfB[slot]])

        def k_proj(xnv, xnvB, N, ktdst, keep=False):
            proj_one(512, xnv, xnvB, N, 0)
            qk_one(0, N, PV_KG, ktdst, KTB, keep_f32=keep)

        def q_proj(xnv, xnvB, N):
            for i in range(4):
                proj_one(i * 128, xnv, xnvB, N, i % 2)
                qk_one(i % 2, N, PV_QG, G["QT"][:, i, 0:N], G["QTB"][i])

        def v_proj_chunk(xnv, xnvB, tok0, ntok, b, bcol):
            for kc in range(8):
                P.op("pe", I("matmul", ps[b][0:ntok, bcol:bcol + 128], lhsT=xnv[:, kc, tok0:tok0 + ntok], rhs=wq[:, kc, 640:768], start=(kc == 0), stop=(kc == 7)),
                     reads=[wqB, xnvB[kc]], writes=[psB[b]])

        def va_store(b, nchunks, slot0):
            VA = G["VA"]
            src = ps[b][0:64, :].rearrange("p (j n) -> p j n", j=4)[:, 0:nchunks, :]
            P.op("dve", I("tensor_copy", out=VA[:, slot0:slot0 + nchunks, 0:64], in_=src[:, :, 0:64]), reads=[psB[b]], writes=[VAB])
            P.op("dve", I("tensor_copy", out=VA[:, slot0:slot0 + nchunks, 192:256], in_=src[:, :, 64:128]), reads=[psB[b]], writes=[VAB])

        def wout_residual(t, N):
            c0, _ = tile_cols(t)
            att, rec, attB, recB = G["att"], G["rec"], G["attB"], G["recB"]
            for oc in range(8):
                b = nb()
                for kc in range(8):
                    src = att[:, kc, 0:N] if kc < 4 else rec[:, kc - 4, 0:N]
                    srcB = attB[kc] if kc < 4 else recB[kc - 4]
                    P.op("pe", I("matmul", ps[b][:, 0:N], lhsT=wo[:, kc, oc * 128:(oc + 1) * 128], rhs=src, start=(kc == 0), stop=(kc == 7)),
                         reads=[woB, srcB], writes=[psB[b]])
                P.op("dve", I("tensor_tensor", out=xres[:, oc, c0:c0 + N], in0=ps[b][:, 0:N], in1=xres[:, oc, c0:c0 + N], op=ALU.add),
                     reads=[psB[b], xresB[t][oc]], writes=[xresB[t][oc]])

        P.barrier()
        alloc_persist()
        xin = [AR.get([128, 1024], F32)]
        xinB = [Buf()]
        xT = AR.get([128, 8, 128], F32)
        xTB = [Buf() for _ in range(8)]
        alloc_common(128, 1, nsets=0, nsq=2, want_g=False)
        P.op("pool", I("memset", G["VA"][:], 1.0), writes=[VAB])
        load_x_group(xhalo_d[0:128, :], 128, xT[:, :, 0:128], xTB, xin[0], xinB[0], 0)
        xnh = norm_tile(xT[:, :, 0:128], xTB, 128, PV_ENORM)
        load_cs(0, 128)
        k_proj(xnh, G["xnB"], 128, G["KT"][:, 0:128])
        b = nb()
        v_proj_chunk(xnh, G["xnB"], 0, 64, b, 0)
        v_proj_chunk(xnh, G["xnB"], 64, 64, b, 128)
        va_store(b, 2, 0)
        P.barrier()

        alloc_persist()
        alloc_common(512, 2, nsets=2, nsq=2, nxn=2)
        G["rec"] = AR.get([128, 4, 512], BF16)
        G["recB"] = [Buf() for _ in range(4)]
        G["att"] = AR.get([128, 4, 512], BF16)
        G["attB"] = [Buf() for _ in range(4)]
        G["QT"] = AR.get([128, 4, 512], BF16)
        G["QTB"] = [Buf() for _ in range(4)]
        PT_l = [AR.get([64, 1536], BF16) for _ in range(2)]
        PTB_l = [Buf(), Buf()]
        PTB_g = [[Buf(), Buf()], [Buf(), Buf()]]
        rc = AR.get([128, 256], F32)
        rcB = Buf()
        rc2 = AR.get([128, 256], F32)
        rc2B = Buf()
        eskB = B_small
        csflat = G["cs"].rearrange("p a n -> p (a n)")
        kf32 = csflat[:, 0:128]
        kf32B = G["csB"]
        vlast = csflat[0:64, 128:384].rearrange("p (j n) -> p j n", j=2)
        vlastB = G["csB"]
        outst = csflat[:, 512:1024]
        outstB = G["csB"]
        _ARENA_USE["main"] = AR.off
        def attn_norm_g(g, bank, col0, ncol, eskg, dst, dstB, v4):
            if g == 0:
                o_, d_, r_, rB_ = slice(0, 64), slice(64, 128), rc, rcB
            else:
                o_, d_, r_, rB_ = slice(64, 128), slice(0, 64), rc2, rc2B
            P.op("dve", I("tensor_tensor", out=v4(r_[o_, 0:ncol]), in0=v4(ps[bank][d_, col0:col0 + ncol]), in1=eskg, op=ALU.add), reads=[psB[bank], eskB, rB_], writes=[rB_])
            P.op("act", I("activation", out=r_[o_, 0:ncol], in_=r_[o_, 0:ncol], func=AF.Ln), reads=[rB_], writes=[rB_])
            P.op("act", I("activation", out=r_[o_, 0:ncol], in_=r_[o_, 0:ncol], func=AF.Exp, scale=-1.0), reads=[rB_], writes=[rB_])
            P.op("dve", I("tensor_tensor", out=dst, in0=v4(ps[bank][o_, col0:col0 + ncol]), in1=v4(r_[o_, 0:ncol]), op=ALU.mult), reads=[psB[bank], rB_], writes=dstB)

        def attn_block(t, blk):
            KT, VA, QT, QTB, att, attB = G["KT"], G["VA"], G["QT"], G["QTB"], G["att"], G["attB"]
            PT = PT_l[blk % 2]
            v4 = lambda a: a.rearrange("p (h t) -> p h t", h=4)
            for g in range(2):
                PTB = PTB_g[blk % 2][g]
                sA, sB = nb(), nb()
                pr = slice(0, 64) if g == 0 else slice(64, 128)
                for j in range(3):
                    kcol = (blk + j) * 64
                    for h4 in range(4):
                        bank = sA if j < 2 else sB
                        col = (j % 2) * 256 + h4 * 64
                        P.op("pe", I("matmul", ps[bank][0:64, col:col + 64], lhsT=KT[pr, kcol:kcol + 64], rhs=QT[pr, h4, blk * 64:(blk + 1) * 64], start=True, stop=True),
                             reads=[KTB, QTB[h4]], writes=[psB[bank]])
                if t == 0 and blk < 2:
                    for j in range(3):
                        bank = sA if j < 2 else sB
                        col = (j % 2) * 256
                        kw = dict(bias=cst[0:64, C_HB:C_HB + 1]) if blk + j < 2 else {}
                        P.op("act", I("activation", out=PT[:, j * 512 + g * 256:j * 512 + (g + 1) * 256], in_=ps[bank][0:64, col:col + 256], func=AF.Exp, scale=0.125, **kw),
                             reads=[psB[bank], B_cst, PTB], writes=[PTB])
                else:
                    P.op("act", I("activation", out=PT[:, 0:1024].rearrange("p (j x) -> p j x", j=2)[:, :, g * 256:(g + 1) * 256],
                                  in_=ps[sA][0:64, :].rearrange("p (j x) -> p j x", j=2), func=AF.Exp, scale=0.125),
                         reads=[psB[sA], PTB], writes=[PTB])
                    P.op("act", I("activation", out=PT[:, 1024 + g * 256:1024 + (g + 1) * 256], in_=ps[sB][0:64, 0:256], func=AF.Exp, scale=0.125),
                         reads=[psB[sB], PTB], writes=[PTB])
                for j in range(3):
                    P.op("pe", I("matmul", ps[sB][:, 256:512], lhsT=VA[:, blk + j, g * 128:(g + 1) * 128], rhs=PT[:, j * 512 + g * 256:j * 512 + (g + 1) * 256], start=(j == 0), stop=(j == 2)),
                         reads=[VAB, PTB], writes=[psB[sB]])
                if SUB < 3:
                    continue
                if g == 0:
                    attn_norm_g(0, sB, 256, 256, small[64:128, 8:12].unsqueeze(2).to_broadcast([64, 4, 64]), att[0:64, :, blk * 64:(blk + 1) * 64], attB, v4)
                else:
                    attn_norm_g(1, sB, 256, 256, small[0:64, 12:16].unsqueeze(2).to_broadcast([64, 4, 64]), att[64:128, :, blk * 64:(blk + 1) * 64], attB, v4)

        def l0_main_tile(t):
            c0, N = tile_cols(t)
            KT, VA = G["KT"], G["VA"]
            xv = xres[:, :, c0:c0 + N]
            xnv = norm_tile(xv, xresB[t], N, PV_ENORM, k=t)
            xnB = G["xnB"]
            load_cs(128 + c0, N)
            q_proj(xnv, xnB, N)
            k_proj(xnv, xnB, N, KT[:, 128:640], keep=(t == 3))
            if t == 3:
                b = nb()
                P.op("pe", I("transpose", ps[b][:, 0:128], G["qf"][:, 0, 384:512], ident), reads=[G["qfB"][0], B_cst], writes=[psB[b]])
                P.op("dve", I("tensor_copy", out=kf32[:, :], in_=ps[b][:, 0:128]), reads=[psB[b]], writes=[kf32B])
                P.op("sp", I("dma_start", out=kp_d, in_=kf32[:, :]), reads=[kf32B], dma_sem="o_kf", arena=True)
            for half in range(2):
                b = nb()
                for j in range(4):
                    v_proj_chunk(xnv, xnB, (half * 4 + j) * 64, 64, b, j * 128)
                va_store(b, 4, 2 + half * 4)
                if t == 3 and half == 1:
                    P.op("dve", I("tensor_copy", out=vlast[:, :, :], in_=ps[b][0:64, 256:512].rearrange("p (j n) -> p j n", j=2)), reads=[psB[b]], writes=[vlastB])
                    P.op("sp", I("dma_start", out=vp_d.rearrange("(j p) n -> p j n", p=64), in_=vlast[:, :, :]), reads=[vlastB], dma_sem="o_vl", arena=True)
            for blk in range(8 if SUB >= 2 else 0):
                attn_block(t, blk)
            if t < 3:
                P.op("pool", I("tensor_copy", out=KT[:, 0:128], in_=KT[:, 512:640]), reads=[KTB], writes=[KTB])
                P.op("pool", I("tensor_copy", out=VA[:, 0:2, :], in_=VA[:, 8:10, :]), reads=[VAB], writes=[VAB])
            if SUB >= 4:
                lru_tile(xnv, xnB, N, (1, N), None, True)
            if SUB >= 5:
                wout_residual(t, N)

        for t in range(4 if stage >= 2 else 0):
            l0_main_tile(t)

        b = nb()
        P.op("pe", I("transpose", ps[b][0:4, 0:128], G["hstate"][:, 0:4], ident), reads=hstB + [B_cst], writes=[psB[b]])
        P.op("dve", I("tensor_copy", out=outst[0:4, 0:128], in_=ps[b][0:4, 0:128]), reads=[psB[b]], writes=[outstB])
        P.op("sp", I("dma_start", out=hp_d, in_=outst[0:4, 0:128]), reads=[outstB], dma_sem="o_st", arena=True)
        b = nb()
        for c in range(4):
            P.op("pe", I("transpose", ps[b][0:3, c * 128:(c + 1) * 128], G["xrb"][:, c, 0:3], ident), reads=[xrbB[c], B_cst], writes=[psB[b]])
        P.op("dve", I("tensor_copy", out=outst[0:3, 0:512], in_=ps[b][0:3, 0:512]), reads=[psB[b], outstB], writes=[outstB])
        P.op("sp", I("dma_start", out=cp_d, in_=outst[0:3, 0:512]), reads=[outstB], dma_sem="o_st", arena=True)
        P.barrier()

        def l0_sample_tile():
            AR.reset()
            alloc_common(64, 2)
            G["KT"] = AR.get([128, 64], BF16)
            G["rec"] = AR.get([128, 4, 64], BF16)
            G["recB"] = [Buf() for _ in range(4)]
            G["att"] = AR.get([128, 4, 64], BF16)
            G["attB"] = [Buf() for _ in range(4)]
            G["QT"] = AR.get([128, 4, 64], BF16)
            G["QTB"] = [Buf() for _ in range(4)]
            G["xrb_s"] = AR.get([128, 4, 4, 19], F32)
            xrb_s = G["xrb_s"]
            rc = AR.get([128, 256], F32)
            rc2 = AR.get([128, 256], F32)
            rcB, rc2B = Buf(), Buf()
            outst = AR.get([128, 512], F32)
            outstB = Buf()
            h0s = AR.get([128, 4, 4], F32)
            h0sB = Buf()
            hns = AR.get([128, 4, 4], F32)
            hnsB = Buf()
            ckT = AR.get([128, 4, 128], BF16)
            ckTB = Buf()
            cvA = AR.get([128, 4, 256], BF16)
            cvAB = Buf()
            cld = AR.get([128, 4, 128], F32)
            cldB = Buf()
            vas = AR.get([16, 4, 256], BF16)
            vasB = Buf()
            vsf = AR.get([16, 4, 128], F32)
            vsfB = Buf()
            pts_c = AR.get([128, 512], BF16)
            pts_n = AR.get([16, 512], BF16)
            ptsB = Buf()
            esk_s = AR.get([128, 2, 4, 16], F32)
            eskB = Buf()
            st12 = AR.get([16, 512], F32)
            st12B = Buf()
            cvt = AR.get([128, 48], F32)
            cvtB = Buf()
            KT, QT, QTB, att, attB = G["KT"], G["QT"], G["QTB"], G["att"], G["attB"]
            c0, N = tile_cols(4)
            xv = xres[:, :, c0:c0 + N]
            for g in range(2):
                P.op("dve", I("tensor_copy", out=esk_s[:, g, :, :], in_=small[:, 8 + g * 4:12 + g * 4].unsqueeze(2).to_broadcast([128, 4, 16])),
                     reads=[B_small], writes=[eskB])
            P.op("sp", I("dma_start", out=st12[0:12, :], in_=stc_d), writes=[st12B], dma_sem="st12", arena=True)
            b = nb()
            for c in range(4):
                P.op("pe", I("transpose", ps[b][:, c * 12:(c + 1) * 12], st12[0:12, c * 128:(c + 1) * 128], ident[0:12, 0:12]), reads=[st12B, B_cst], writes=[psB[b]])
            P.op("dve", I("tensor_copy", out=xrb_s[:, :, :, 0:3], in_=ps[b][:, 0:48].rearrange("p (c s k) -> p c s k", c=4, s=4)), reads=[psB[b]], writes=xrbB)
            P.op("sp", I("dma_start", out=st12[0:4, :], in_=sth_d), reads=[st12B], writes=[st12B], dma_sem="st12", arena=True)
            b = nb()
            for c in range(4):
                P.op("pe", I("transpose", ps[b][:, c * 4:(c + 1) * 4], st12[0:4, c * 128:(c + 1) * 128], ident[0:4, 0:4]), reads=[st12B, B_cst], writes=[psB[b]])
            P.op("dve", I("tensor_copy", out=h0s[:, :, :], in_=ps[b][:, 0:16].rearrange("p (c s) -> p c s", c=4)), reads=[psB[b]], writes=[h0sB])
            P.op("sp", I("dma_start", out=cld[:, :, :], in_=ck_d.rearrange("s r n -> r s n")), writes=[cldB], dma_sem="cld", arena=True)
            b = nb()
            for s in range(4):
                P.op("pe", I("transpose", ps[b][:, s * 128:(s + 1) * 128], cld[:, s, :], ident), reads=[cldB, B_cst], writes=[psB[b]])
            P.op("dve", I("tensor_copy", out=ckT[:, :, :], in_=ps[b][:, :].rearrange("p (s n) -> p s n", s=4)), reads=[psB[b]], writes=[ckTB])
            P.op("sp", I("dma_start", out=cld[:, :, :], in_=cv_d.rearrange("s r n -> r s n")), reads=[cldB], writes=[cldB], dma_sem="cld", arena=True)
            P.op("pool", I("memset", cvA[:], 1.0), writes=[cvAB])
            P.op("pool", I("memset", vas[:], 1.0), writes=[vasB])
            P.op("dve", I("tensor_copy", out=cvA[:, :, 0:64], in_=cld[:, :, 0:64]), reads=[cldB, cvAB], writes=[cvAB])
            P.op("dve", I("tensor_copy", out=cvA[:, :, 192:256], in_=cld[:, :, 64:128]), reads=[cldB, cvAB], writes=[cvAB])
            P.op("sp", I("dma_start", out=ksm_d[:, 0:112, :], in_=ck_d[:, 16:128, :]), dma_sem="o_dram")
            P.op("sp", I("dma_start", out=vsm_d[:, 0:112, :], in_=cv_d[:, 16:128, :]), dma_sem="o_dram")

            xns = norm_tile(xv, xresB[4], N, PV_ENORM)
            xnB = G["xnB"]
            load_cs(128 + NT, N)
            q_proj(xns, xnB, N)
            k_proj(xns, xnB, N, KT[:, 0:N], keep=True)
            b = nb()
            P.op("pe", I("transpose", ps[b][0:64, 0:128], G["qf"][:, 0, 0:64], ident), reads=[G["qfB"][0], B_cst], writes=[psB[b]])
            P.op("dve", I("tensor_copy", out=outst[0:64, 0:128], in_=ps[b][0:64, 0:128]), reads=[psB[b], outstB], writes=[outstB])
            for s in range(4):
                P.op("sp", I("dma_start", out=ksm_d[s, 112:128, :], in_=outst[s * 16:(s + 1) * 16, 0:128]), reads=[outstB], dma_sem="o_st", arena=True)
            b = nb()
            for s in range(4):
                v_proj_chunk(xns, xnB, s * 16, 16, b, s * 128)
            srcv = ps[b][0:16, :].rearrange("p (s n) -> p s n", s=4)
            P.op("act", I("activation", out=vas[:, :, 0:64], in_=srcv[:, :, 0:64], func=AF.Copy), reads=[psB[b], vasB], writes=[vasB])
            P.op("dve", I("tensor_copy", out=vas[:, :, 192:256], in_=srcv[:, :, 64:128]), reads=[psB[b], vasB], writes=[vasB])
            P.op("dve", I("tensor_copy", out=vsf[:, :, :], in_=srcv), reads=[psB[b]], writes=[vsfB])
            P.op("sp", I("dma_start", out=vsm_d[:, 112:128, :].rearrange("s r n -> r s n"), in_=vsf[:, :, :]), reads=[vsfB], dma_sem="o_vsf", arena=True)
            scb, snb, ob = [nb(), nb()], [nb(), nb()], nb()
            for s in range(4):
                for h in range(8):
                    g, h4 = h // 4, h % 4
                    pr = slice(0, 64) if g == 0 else slice(64, 128)
                    col = s * 64 + h4 * 16
                    P.op("pe", I("matmul", ps[scb[g]][:, col:col + 16], lhsT=ckT[pr, s, :], rhs=QT[pr, h4, s * 16:(s + 1) * 16], start=True, stop=True),
                         reads=[ckTB, QTB[h4]], writes=[psB[scb[g]]])
                    P.op("pe", I("matmul", ps[snb[g]][0:16, col:col + 16], lhsT=KT[pr, s * 16:(s + 1) * 16], rhs=QT[pr, h4, s * 16:(s + 1) * 16], start=True, stop=True),
                         reads=[KTB, QTB[h4]], writes=[psB[snb[g]]])
            for g in range(2):
                P.op("act", I("activation", out=pts_c[:, g * 256:(g + 1) * 256], in_=ps[scb[g]][:, 0:256], func=AF.Exp, scale=0.125), reads=[psB[scb[g]], ptsB], writes=[ptsB])
                P.op("act", I("activation", out=pts_n[:, g * 256:(g + 1) * 256], in_=ps[snb[g]][0:16, 0:256], func=AF.Exp, scale=0.125), reads=[psB[snb[g]], ptsB], writes=[ptsB])
            for g in range(2):
                for s in range(4):
                    oc0 = g * 256 + s * 64
                    P.op("pe", I("matmul", ps[ob][:, oc0:oc0 + 64], lhsT=cvA[:, s, g * 128:(g + 1) * 128], rhs=pts_c[:, g * 256 + s * 64:g * 256 + (s + 1) * 64], start=True, stop=False),
                         reads=[cvAB, ptsB], writes=[psB[ob]])
                    P.op("pe", I("matmul", ps[ob][:, oc0:oc0 + 64], lhsT=vas[:, s, g * 128:(g + 1) * 128], rhs=pts_n[:, g * 256 + s * 64:g * 256 + (s + 1) * 64], start=False, stop=True),
                         reads=[vasB, ptsB], writes=[psB[ob]])
            v4 = lambda a: a.rearrange("p (s h t) -> p s h t", s=4, h=4)
            P.op("dve", I("tensor_tensor", out=v4(rc[0:64, 0:256]), in0=v4(ps[ob][64:128, 0:256]), in1=esk_s[64:128, 0, :, :].unsqueeze(1).to_broadcast([64, 4, 4, 16]), op=ALU.add),
                 reads=[psB[ob], eskB, rcB], writes=[rcB])
            P.op("act", I("activation", out=rc[0:64, 0:256], in_=rc[0:64, 0:256], func=AF.Ln), reads=[rcB], writes=[rcB])
            P.op("act", I("activation", out=rc[0:64, 0:256], in_=rc[0:64, 0:256], func=AF.Exp, scale=-1.0), reads=[rcB], writes=[rcB])
            P.op("dve", I("tensor_tensor", out=att[0:64, :, 0:64].rearrange("p h (s t) -> p s h t", s=4), in0=v4(ps[ob][0:64, 0:256]), in1=v4(rc[0:64, 0:256]), op=ALU.mult),
                 reads=[psB[ob], rcB], writes=attB)
            P.op("dve", I("tensor_tensor", out=v4(rc2[64:128, 0:256]), in0=v4(ps[ob][0:64, 256:512]), in1=esk_s[0:64, 1, :, :].unsqueeze(1).to_broadcast([64, 4, 4, 16]), op=ALU.add),
                 reads=[psB[ob], eskB, rc2B], writes=[rc2B])
            P.op("act", I("activation", out=rc2[64:128, 0:256], in_=rc2[64:128, 0:256], func=AF.Ln), reads=[rc2B], writes=[rc2B])
            P.op("act", I("activation", out=rc2[64:128, 0:256], in_=rc2[64:128, 0:256], func=AF.Exp, scale=-1.0), reads=[rc2B], writes=[rc2B])
            P.op("dve", I("tensor_tensor", out=att[64:128, :, 0:64].rearrange("p h (s t) -> p s h t", s=4), in0=v4(ps[ob][64:128, 256:512]), in1=v4(rc2[64:128, 0:256]), op=ALU.mult),
                 reads=[psB[ob], rc2B], writes=attB)

            def keep_h(c, T, TB):
                P.op("act", I("activation", out=hns[:, c, :], in_=T["hs"][:, 0:64].rearrange("p (s l) -> p s l", s=4)[:, :, 15], func=AF.Copy), reads=[TB["hs"], hnsB], writes=[hnsB])
            lru_tile(xns, xnB, N, (4, 16), None, True, hinit_override=lambda c, s: (h0s[:, c, s:s + 1], h0sB), hs_keep=keep_h)
            wout_residual(4, N)
            b = nb()
            for c in range(4):
                P.op("pe", I("transpose", ps[b][0:4, c * 128:(c + 1) * 128], hns[:, c, :], ident), reads=[hnsB, B_cst], writes=[psB[b]])
            P.op("dve", I("tensor_copy", out=outst[0:4, 0:512], in_=ps[b][0:4, 0:512]), reads=[psB[b], outstB], writes=[outstB])
            P.op("sp", I("dma_start", out=hsm_d, in_=outst[0:4, 0:512]), reads=[outstB], dma_sem="o_st", arena=True)
            P.op("pool", I("tensor_copy", out=cvt[:, 0:48].rearrange("p (c s k) -> p c s k", c=4, s=4), in_=xrb_s[:, :, :, 16:19]), reads=xrbB, writes=[cvtB])
            b = nb()
            for c in range(4):
                P.op("pe", I("transpose", ps[b][0:12, c * 128:(c + 1) * 128], cvt[:, c * 12:(c + 1) * 12], ident), reads=[cvtB, B_cst], writes=[psB[b]])
            P.op("dve", I("tensor_copy", out=outst[0:12, 0:512], in_=ps[b][0:12, 0:512]), reads=[psB[b], outstB], writes=[outstB])
            P.op("sp", I("dma_start", out=csm_d, in_=outst[0:12, 0:512]), reads=[outstB], dma_sem="o_st", arena=True)

        P.cur_tag = "sample"
        if stage >= 3:
            l0_sample_tile()
        P.cur_tag = None
        prefetch(5)
        if not MERGE_SAMPLE_FFN:
            P.barrier()

        ystB = [Buf(), Buf()]
        yctr = [0]

        def out_tile(t, yst):
            c0, N = tile_cols(t)
            ng = max(1, N // 128)
            npart = min(128, N)
            for g in range(ng):
                k = yctr[0] % 2
                ybuf, yB = yst[k], ystB[k]
                for half in range(2):
                    b = nb()
                    for cc in range(4):
                        c = half * 4 + cc
                        P.op("pe", I("transpose", ps[b][0:npart, cc * 128:(cc + 1) * 128], xres[:, c, c0 + g * 128:c0 + g * 128 + npart], ident),
                             reads=[xresB[t][c], B_cst], writes=[psB[b]])
                    if half == 0:
                        P.op("act", I("activation", out=ybuf[0:npart, 0:512], in_=ps[b][0:npart, :], func=AF.Copy), reads=[psB[b]], writes=[yB])
                    else:
                        P.op("dve", I("tensor_copy", out=ybuf[0:npart, 512:1024], in_=ps[b][0:npart, :]), reads=[psB[b]], writes=[yB])
                dst = y_d[c0 + g * 128:c0 + g * 128 + npart, :] if t < 4 else ys_d[0:NS, :]
                P.op("sp", I("dma_start", out=dst, in_=ybuf[0:npart, :]), reads=[yB], dma_sem="o_y%d" % k, arena=True)
                yctr[0] += 1

        def ffn_phase(layer, gcol, wbase, emit_out=False, reset=True):
            if reset:
                AR.reset()
            xna = AR.get([128, 8, NTOK], BF16)
            xnaB = [[Buf() for _ in range(8)] for _ in range(5)]
            sq_ = [AR.get([128, 512], BF16) for _ in range(4)]
            sqB_ = [Buf() for _ in range(4)]
            rstd_l = [AR.get([128, 512], F32) for _ in range(2)]
            rstdB_l = [Buf(), Buf()]
            hb_ = [AR.get([128, 4, 512], BF16) for _ in range(2)]
            hbB = [[Buf() for _ in range(4)] for _ in range(2)]
            rl = [AR.get([128, 512], F32) for _ in range(2)]
            rlB = [Buf(), Buf()]
            yst_f = [AR.get([128, 1024], F32) for _ in range(2)] if emit_out else None
            for t in range(5):
                c0, N = tile_cols(t)
                rmsnorm(xres[:, :, c0:c0 + N], xresB[t], N, gcol, xna[:, :, c0:c0 + N], xnaB[t], sq_, sqB_, rstd_l[t % 2], rstdB_l[t % 2])
            it = [0]
            for hb in range(8):
                wi = wbase + hb
                prefetch(wi + 2)
                slot = wslot(wi)
                W1 = ring[slot][:, 0:4096].rearrange("p (c n) -> p c n", c=8)
                W2 = ring[slot][:, 4096:8192].rearrange("p (c n) -> p c n", c=4)
                WB = ringB[slot]
                WB2 = ringB2[slot]

                def mm1(t, hsel):
                    c0, N = tile_cols(t)
                    for m in range(4):
                        b = nb()
                        for kc in range(8):
                            P.op("pe", I("matmul", ps[b][:, 0:N], lhsT=W1[:, kc, m * 128:(m + 1) * 128], rhs=xna[:, kc, c0:c0 + N], start=(kc == 0), stop=(kc == 7)),
                                 reads=[WB, xnaB[t][kc]], writes=[psB[b]])
                        r_ = m % 2
                        P.op("act", I("activation", out=rl[r_][:, 0:N], in_=ps[b][:, 0:N], func=AF.Relu), reads=[psB[b]], writes=[rlB[r_]])
                        P.op("act", I("activation", out=hb_[hsel][:, m, 0:N], in_=rl[r_][:, 0:N], func=AF.Square), reads=[rlB[r_]], writes=[hbB[hsel][m]])

                def mm2(t, hsel):
                    c0, N = tile_cols(t)
                    for oc in range(8):
                        b = nb()
                        for m in range(4):
                            P.op("pe", I("matmul", ps[b][:, 0:N], lhsT=W2[:, m, oc * 128:(oc + 1) * 128], rhs=hb_[hsel][:, m, 0:N], start=(m == 0), stop=(m == 3)),
                                 reads=[WB2, hbB[hsel][m]], writes=[psB[b]])
                        P.op("dve", I("tensor_tensor", out=xres[:, oc, c0:c0 + N], in0=ps[b][:, 0:N], in1=xres[:, oc, c0:c0 + N], op=ALU.add),
                             reads=[psB[b], xresB[t][oc]], writes=[xresB[t][oc]])

                mm1(0, it[0] % 2)
                for t in range(5):
                    if t + 1 < 5:
                        mm1(t + 1, (it[0] + 1) % 2)
                    mm2(t, it[0] % 2)
                    it[0] += 1
                    if emit_out and hb == 7:
                        out_tile(t, yst_f)

        if stage >= 4:
            ffn_phase(0, PV_F0NORM, 3, reset=not MERGE_SAMPLE_FFN)
        prefetch(13)
        P.barrier()

        AR.reset()
        xn1 = AR.get([128, 8, 512], BF16)
        xn1B = [Buf() for _ in range(8)]
        sq1 = [AR.get([128, 512], BF16) for _ in range(4)]
        sq1B = [Buf() for _ in range(4)]
        rstd1 = AR.get([128, 512], F32)
        rstd1B = Buf()
        ub = AR.get([128, 8, 512], BF16)
        ubB = [Buf() for _ in range(8)]
        vz = [AR.get([128, 1024], F32) for _ in range(3)]
        vzB = [Buf(), Buf(), Buf()]
        vn = AR.get([128, 4, 1024], BF16)
        vnB = [Buf() for _ in range(4)]
        gt = AR.get([128, 8, 512], BF16)
        gtB = [Buf() for _ in range(8)]
        vgb = AR.get([128, 1024], F32)
        bsb = AR.get([128, 8, 128], F32)
        bss = AR.get([128, 8, 4, 16], F32)
        wsf = AR.get([128, 8, 128], F32)
        wsb = AR.get([128, 8, 128], BF16)
        wbdf = AR.get([64, 8, 64], F32)
        wbd = AR.get([64, 8, 64], BF16)
        stt_l = [AR.get([128, 8], F32) for _ in range(2)]
        sttB_l = [Buf(), Buf()]
        sqd_l = [AR.get([128, 512], F32) for _ in range(2)]
        sqdB_l = [Buf(), Buf()]
        tmpg_l = [AR.get([128, 512], F32) for _ in range(3)]
        tmpgB_l = [Buf() for _ in range(3)]
        B_l1c = Buf()
        P.op("sp", I("dma_start", out=vgb[:, :], in_=ovg_d.partition_broadcast(128)), writes=[B_l1c], dma_sem="l1c", arena=True)
        P.op("sp", I("dma_start", out=bsb[:, :, :], in_=osb_d.partition_broadcast(128)), writes=[B_l1c], dma_sem="l1c", arena=True)
        P.op("sp", I("dma_start", out=wsf[:, :, :], in_=wst_d.rearrange("g s t -> s g t")), writes=[B_l1c], dma_sem="l1c", arena=True)
        P.op("pool", I("memset", wbdf[:], 0.0), writes=[B_l1c])
        for i in range(4):
            P.op("sp", I("dma_start", out=wbdf[16 * i:16 * i + 16, :, 16 * i:16 * i + 16], in_=wst_d[:, 0:16, 0:16].rearrange("g s t -> s g t")), reads=[B_l1c], writes=[B_l1c], dma_sem="l1c", arena=True)
        P.op("dve", I("tensor_tensor", out=wsb[:, :, :], in0=wsf[:, :, :], in1=cst[:, C_TRIU:C_TRIU + 128].unsqueeze(1).to_broadcast([128, 8, 128]), op=ALU.mult), reads=[B_l1c, B_cst], writes=[B_l1c])
        P.op("dve", I("tensor_tensor", out=wbd[:, :, :], in0=wbdf[:, :, :], in1=cst[0:64, C_BDM:C_BDM + 64].unsqueeze(1).to_broadcast([64, 8, 64]), op=ALU.mult), reads=[B_l1c, B_cst], writes=[B_l1c])
        P.op("dve", I("tensor_copy", out=bss[:, :, :, :], in_=bsb[:, :, 0:16].unsqueeze(2).to_broadcast([128, 8, 4, 16])), reads=[B_l1c], writes=[B_l1c])

        wu = ring[wslot(11)][:, 0:8192].rearrange("p (c n) -> p c n", c=8)
        wuB = ringB[wslot(11)]
        wv = ring[wslot(12)][:, 0:8192].rearrange("p (c n) -> p c n", c=8)
        wvB = ringB[wslot(12)]
        wo1 = ring[wslot(13)][:, 0:8192].rearrange("p (c n) -> p c n", c=8)
        wo1B = ringB[wslot(13)]
        for t in range(5 if stage >= 5 else 0):
            c0, N = tile_cols(t)
            xnv = xn1[:, :, 0:N]
            rmsnorm(xres[:, :, c0:c0 + N], xresB[t], N, PV_ONORM, xnv, xn1B, sq1, sq1B, rstd1, rstd1B)
            proj_fm(wu, wuB, 0, 8, xnv, xn1B, N,
                    lambda m, b: P.op("act", I("activation", out=ub[:, m, 0:N], in_=ps[b][:, 0:N], func=AF.Gelu_apprx_tanh), reads=[psB[b]], writes=[ubB[m]]))
            ng = max(1, N // 128)
            npart = min(128, N)
            for tg in range(ng):
                z = vz[tg % 3]
                zB = vzB[tg % 3]
                stt, sttB = stt_l[tg % 2], sttB_l[tg % 2]
                sqd, sqdB = sqd_l[tg % 2], sqdB_l[tg % 2]
                for half in range(2):
                    b = nb()
                    for kc in range(8):
                        P.op("pe", I("matmul", ps[b][0:npart, :], lhsT=xnv[:, kc, tg * 128:tg * 128 + npart], rhs=wv[:, kc, half * 512:(half + 1) * 512], start=(kc == 0), stop=(kc == 7)),
                             reads=[wvB, xn1B[kc]], writes=[psB[b]])
                    P.op("act", I("activation", out=z[0:npart, half * 512:(half + 1) * 512], in_=ps[b][0:npart, :], func=AF.Gelu_apprx_tanh, accum_out=stt[0:npart, half:half + 1]),
                         reads=[psB[b], sttB], writes=[zB, sttB])
                P.op("dve", I("tensor_tensor", out=stt[0:npart, 2:3], in0=stt[0:npart, 0:1], in1=stt[0:npart, 1:2], op=ALU.add), reads=[sttB], writes=[sttB])
                P.op("dve", I("tensor_scalar", out=stt[0:npart, 2:3], in0=stt[0:npart, 2:3], scalar1=-1.0 / 1024.0, scalar2=None, op0=ALU.mult), reads=[sttB], writes=[sttB])
                P.op("dve", I("tensor_scalar", out=z[0:npart, :], in0=z[0:npart, :], scalar1=stt[0:npart, 2:3], scalar2=None, op0=ALU.add), reads=[zB, sttB], writes=[zB])
                P.op("act", I("activation", out=sqd[0:npart, 0:512], in_=z[0:npart, 0:512], func=AF.Square, accum_out=stt[0:npart, 3:4]), reads=[zB, sttB, sqdB], writes=[sqdB, sttB])
                P.op("act", I("activation", out=sqd[0:npart, 0:512], in_=z[0:npart, 512:1024], func=AF.Square, accum_out=stt[0:npart, 4:5]), reads=[zB, sttB, sqdB], writes=[sqdB, sttB])
                P.op("dve", I("tensor_tensor", out=stt[0:npart, 5:6], in0=stt[0:npart, 3:4], in1=stt[0:npart, 4:5], op=ALU.add), reads=[sttB], writes=[sttB])
                P.op("act", I("activation", out=stt[0:npart, 5:6], in_=stt[0:npart, 5:6], func=AF.Ln, scale=1.0 / 1024.0, bias=EPS), reads=[sttB], writes=[sttB])
                P.op("act", I("activation", out=stt[0:npart, 5:6], in_=stt[0:npart, 5:6], func=AF.Exp, scale=-0.5), reads=[sttB], writes=[sttB])
                P.op("dve", I("scalar_tensor_tensor", out=z[0:npart, :], in0=z[0:npart, :], scalar=stt[0:npart, 5:6], in1=vgb[0:npart, :], op0=ALU.mult, op1=ALU.mult),
                     reads=[zB, sttB, B_l1c], writes=[zB])
                P.op("act", I("activation", out=vn[0:npart, tg, :], in_=z[0:npart, :], func=AF.Copy), reads=[zB], writes=[vnB[tg]])
                if t == 4:
                    P.op("sp", I("dma_start", out=gv_d, in_=z[0:NS, :]), reads=[zB], dma_sem="o_gv", arena=True)
            for g in range(8):
                b = nb()
                tmpg, tmpgB = tmpg_l[g % 3], tmpgB_l[g % 3]
                for tg in range(ng):
                    if t < 4:
                        P.op("pe", I("matmul", ps[b][:, tg * 128:(tg + 1) * 128], lhsT=vn[:, tg, g * 128:(g + 1) * 128], rhs=wsb[:, g, :], start=True, stop=True),
                             reads=[vnB[tg], B_l1c], writes=[psB[b]])
                    else:
                        P.op("pe", I("matmul", ps[b][:, 0:64], lhsT=vn[0:64, 0, g * 128:(g + 1) * 128], rhs=wbd[:, g, :], start=True, stop=True),
                             reads=[vnB[0], B_l1c], writes=[psB[b]])
                if t < 4:
                    P.op("dve", I("tensor_tensor", out=tmpg[:, 0:512].rearrange("p (a n) -> p a n", a=4), in0=ps[b][:, :].rearrange("p (a n) -> p a n", a=4),
                                                                   in1=bsb[:, g, :].unsqueeze(1).to_broadcast([128, 4, 128]), op=ALU.add), reads=[psB[b], B_l1c, tmpgB], writes=[tmpgB])
                else:
                    P.op("dve", I("tensor_tensor", out=tmpg[:, 0:64], in0=ps[b][:, 0:64], in1=bss[:, g, :, :].rearrange("p a n -> p (a n)"), op=ALU.add),
                         reads=[psB[b], B_l1c, tmpgB], writes=[tmpgB])
                P.op("pool", I("tensor_tensor", out=gt[:, g, 0:N], in0=tmpg[:, 0:N], in1=ub[:, g, 0:N], op=ALU.mult), reads=[tmpgB, ubB[g]], writes=[gtB[g]])
            for oc in range(8):
                b = nb()
                for kc in range(8):
                    P.op("pe", I("matmul", ps[b][:, 0:N], lhsT=wo1[:, kc, oc * 128:(oc + 1) * 128], rhs=gt[:, kc, 0:N], start=(kc == 0), stop=(kc == 7)),
                         reads=[wo1B, gtB[kc]], writes=[psB[b]])
                P.op("dve", I("tensor_tensor", out=xres[:, oc, c0:c0 + N], in0=ps[b][:, 0:N], in1=xres[:, oc, c0:c0 + N], op=ALU.add),
                     reads=[psB[b], xresB[t][oc]], writes=[xresB[t][oc]])
        prefetch(16)
        P.barrier()

        if stage >= 6:
            ffn_phase(1, PV_F1NORM, 14, emit_out=True)
        P.barrier()

        if stage < 6:
            AR.reset()
            yst_l = [AR.get([128, 1024], F32) for _ in range(2)]
            for t in range(5):
                out_tile(t, yst_l)

        _EST[0] = None
        P.emit(final_waits=[k for k in P.dma_counts if str(k).startswith("o_")])
        _EST[0] = (getattr(P, "est_us", None), getattr(P, "phase_end", None))
    return nc


_EST = [None]
_ARENA_USE = {}
_PROG_CACHE = {}
STAGE = 6
SUB = 9


def _rope_tables(pos):
    half = 32
    inv = (np.float32(10000.0) ** (-np.arange(half, dtype=np.float32) / np.float32(half))).astype(np.float32)
    ang = pos.astype(np.float32)[None, :] * inv[:, None]
    cos = np.cos(ang).astype(np.float32)
    sin = np.sin(ang).astype(np.float32)
    cosr = np.concatenate([cos, cos, cos, cos], 0)
    sinr = np.concatenate([-sin, sin, -sin, sin], 0)
    return np.stack([cosr, sinr], 1)


def kernel(x_prompt, x_sample, cache_swa_k, cache_swa_v, state_lru_h, state_lru_conv,
           e_norm_g, e_w_in, e_q_norm_g, e_k_norm_g, e_sinks, e_conv_w, e_conv_b,
           e_gate_a_w, e_gate_a_b, e_gate_x_w, e_gate_x_b, e_lru_lambda, e_w_out,
           o_norm_g, o_w_in, o_v_norm_g, o_spatial_w, o_spatial_b, o_w_out,
           ffn_norm_g, ffn_w1, ffn_w2):
    f = lambda a: np.ascontiguousarray(np.asarray(a, dtype=np.float32))
    x_prompt, x_sample = f(x_prompt), f(x_sample)
    cache_swa_k, cache_swa_v = f(cache_swa_k), f(cache_swa_v)
    state_lru_h, state_lru_conv = f(state_lru_h), f(state_lru_conv)

    def chunks(v):
        v = f(v).reshape(-1, 128)
        return v.T
    pvec = np.zeros((128, NPV), np.float32)
    pvec[:, PV_ENORM:PV_ENORM + 8] = chunks(e_norm_g[0])
    pvec[:, PV_F0NORM:PV_F0NORM + 8] = chunks(ffn_norm_g[0])
    pvec[:, PV_ONORM:PV_ONORM + 8] = chunks(o_norm_g[0])
    pvec[:, PV_F1NORM:PV_F1NORM + 8] = chunks(ffn_norm_g[1])
    pvec[:, PV_QG] = np.tile(f(e_q_norm_g[0]), 2)
    pvec[:, PV_KG] = np.tile(f(e_k_norm_g[0]), 2)
    cw = f(e_conv_w[0])
    for c in range(4):
        for i in range(4):
            pvec[:, PV_CW + c * 4 + i] = cw[i, c * 128:(c + 1) * 128]
    pvec[:, PV_CB:PV_CB + 4] = chunks(e_conv_b[0])
    pvec[:, PV_BA:PV_BA + 4] = chunks(e_gate_a_b[0])
    pvec[:, PV_BX:PV_BX + 4] = chunks(e_gate_x_b[0])
    pvec[:, PV_LAM:PV_LAM + 4] = chunks(e_lru_lambda[0])
    pvec[:, PV_SINK:PV_SINK + 8] = np.broadcast_to(f(e_sinks[0])[None, :], (128, 8))

    qperm = np.concatenate([np.arange(h * 64, (h + 1) * 64) for h in (0, 4, 1, 5, 2, 6, 3, 7)])
    ewin = f(e_w_in[0])
    ewin = np.ascontiguousarray(np.concatenate([ewin[:, qperm], ewin[:, 512:]], 1))
    ewout = f(e_w_out[0])
    ewout = np.ascontiguousarray(np.concatenate([ewout[qperm, :], ewout[512:, :]], 0))
    wst = np.ascontiguousarray(np.transpose(f(o_spatial_w[0]), (0, 2, 1)))

    base = np.zeros((128, NCST), np.float32)
    base[:, C_ID:C_ID + 128] = np.eye(128, dtype=np.float32)
    base[:, C_ONES:C_ONES + 128] = 1.0
    blk = np.arange(128) // 64
    base[:, C_BONES:C_BONES + 128] = (blk[:, None] == blk[None, :]).astype(np.float32)
    pm = np.zeros((128, 128), np.float32)
    for m in range(128):
        k = m + 32 if (m % 64) < 32 else m - 32
        pm[k, m] = 1.0
    base[:, C_PERM:C_PERM + 128] = pm
    base[:, C_TRIU:C_TRIU + 128] = np.triu(np.ones((128, 128), np.float32))
    s_ = np.arange(64)
    base[0:64, C_BDM:C_BDM + 64] = ((s_[:, None] // 16 == s_[None, :] // 16) & (s_[:, None] <= s_[None, :])).astype(np.float32)

    common = {
        "pvec": pvec, "ewin": ewin, "ewout": ewout, "ga": f(e_gate_a_w[0]), "gx": f(e_gate_x_w[0]),
        "owin": f(o_w_in[0]), "owout": f(o_w_out[0]), "wst": wst, "osb": f(o_spatial_b[0]), "ovg": f(o_v_norm_g[0]),
        "w1": f(ffn_w1), "w2": f(ffn_w2),
    }
    in_maps = []
    for c in range(NCORES):
        seq, j = c // 4, c % 4
        start = j * NT
        xmain = x_prompt[seq, start:start + NT]
        xpre = np.zeros((NPRE, 1024), np.float32)
        if start > 0:
            xpre[NPRE - start:] = x_prompt[seq, 0:start]
        xhalo = np.zeros((128, 1024), np.float32)
        if start > 0:
            xhalo[:] = x_prompt[seq, start - 128:start]
        cst = base.copy()
        cst[:, C_HB] = 0.0 if start > 0 else -30000.0
        for ptile in range(NPRE // TS):
            cst[:, C_PF + ptile] = 1.0 if (ptile * TS >= NPRE - start) else 0.0
        pos = np.concatenate([np.arange(start - 128, start + NT), 4096 + (np.arange(NS) % 16)]).astype(np.float32)
        m = dict(common)
        m.update({
            "xmain": np.ascontiguousarray(xmain), "xpre": xpre, "xhalo": xhalo,
            "xsm": np.ascontiguousarray(x_sample[4 * c:4 * c + 4].reshape(NS, 1024)),
            "ck": np.ascontiguousarray(cache_swa_k[0, 4 * c:4 * c + 4].reshape(4, 128, 128)),
            "cv": np.ascontiguousarray(cache_swa_v[0, 4 * c:4 * c + 4].reshape(4, 128, 128)),
            "sth": np.ascontiguousarray(state_lru_h[0, 4 * c:4 * c + 4]),
            "stc": np.ascontiguousarray(state_lru_conv[0, 4 * c:4 * c + 4].reshape(12, 512)),
            "cst": cst, "cstab": np.ascontiguousarray(_rope_tables(pos)),
        })
        in_maps.append(m)

    if "nc" not in _PROG_CACHE:
        _PROG_CACHE["nc"] = build_program(STAGE)
    nc = _PROG_CACHE["nc"]
    res = run_bass_kernel_spmd(nc, in_maps, core_ids=list(range(NCORES)))
    r = res.results

    y_prompt = np.stack([np.concatenate([r[s * 4 + j]["y"] for j in range(4)], 0) for s in range(2)], 0)
    y_sample = np.concatenate([r[c]["ys"].reshape(4, 16, 1024) for c in range(NCORES)], 0)
    kp = np.stack([r[s * 4 + 3]["kp"].reshape(128, 2, 64) for s in range(2)], 0)[None]
    vp = np.stack([r[s * 4 + 3]["vp"].reshape(128, 2, 64) for s in range(2)], 0)[None]
    hp = np.stack([r[s * 4 + 3]["hp"].reshape(512) for s in range(2)], 0)[None]
    cp = np.stack([r[s * 4 + 3]["cp"] for s in range(2)], 0)[None]
    ksm = np.concatenate([r[c]["ksm"].reshape(4, 128, 2, 64) for c in range(NCORES)], 0)[None]
    vsm = np.concatenate([r[c]["vsm"].reshape(4, 128, 2, 64) for c in range(NCORES)], 0)[None]
    hsm = np.concatenate([r[c]["hsm"] for c in range(NCORES)], 0)[None]
    csm = np.concatenate([r[c]["csm"].reshape(4, 3, 512) for c in range(NCORES)], 0)[None]
    gv = np.concatenate([r[c]["gv"].reshape(4, 16, 1024) for c in range(NCORES)], 0)[None]
    outs = (y_prompt, y_sample, kp, vp, hp, cp, ksm, vsm, hsm, csm, gv)
    return tuple(np.ascontiguousarray(o, dtype=np.float32) for o in outs)
```
